# Optimizing a Trainium2 kernel written in Bass

```python
import math
import jax, jax.numpy as jnp
from jax import lax
import numpy as np

D_MODEL = 1024
BATCH = 16
SEQ = 2048
DEPTH = 4

D_FF = 2816
N_BRANCH = 3
BRANCH_WIDTH = D_MODEL // 2
POOL_WINDOWS = (2, 4, 8, 16)
POOL_GROUPS = len(POOL_WINDOWS)
POOL_GROUP_DIM = BRANCH_WIDTH // POOL_GROUPS
DN_HEAD_DIM = 128
DN_HEADS = BRANCH_WIDTH // DN_HEAD_DIM
DN_CONV = 4
DN_CHUNK = 64
SB_HEAD_DIM = 128
SB_HEADS = BRANCH_WIDTH // SB_HEAD_DIM
SB_BLOCK = 128
EPS = 1e-6

IN_SPLITS = (
    BRANCH_WIDTH,
    3 * BRANCH_WIDTH,
    BRANCH_WIDTH,
    DN_HEADS,
    DN_HEADS,
    3 * BRANCH_WIDTH,
    N_BRANCH * D_MODEL,
)
P_IN = sum(IN_SPLITS)

kernel_name = "hybrid_pool_deltanet_stickbreak_macaron"


def rms_norm(x, g):
    xf = x.astype(jnp.float32)
    y = xf * lax.rsqrt(jnp.mean(xf * xf, axis=-1, keepdims=True) + EPS)
    return (y * g.astype(jnp.float32)).astype(x.dtype)


def swiglu(h, w_gate, w_up, w_down):
    return (jax.nn.silu(h @ w_gate) * (h @ w_up)) @ w_down


def pool_mixer(u, w_group, scale):
    b, s, _ = u.shape
    uf = u.astype(jnp.float32)
    csum = jnp.cumsum(uf, axis=1)
    count = jnp.arange(1, s + 1, dtype=jnp.float32)[None, :, None]
    outs = []
    for gi, win in enumerate(POOL_WINDOWS):
        sl = slice(gi * POOL_GROUP_DIM, (gi + 1) * POOL_GROUP_DIM)
        c = csum[..., sl]
        c_lag = jnp.pad(c, ((0, 0), (win, 0), (0, 0)))[:, :s]
        mean = (c - c_lag) / jnp.minimum(count, float(win))
        outs.append(mean - uf[..., sl])
    pooled = jnp.stack(outs, axis=2).astype(u.dtype)
    mixed = jnp.einsum('bsgc,gcd->bsgd', pooled, w_group).reshape(b, s, BRANCH_WIDTH)
    return mixed * scale


def causal_depthwise_conv(x, w):
    k, c = w.shape
    return lax.conv_general_dilated(
        x, w[:, None, :].astype(x.dtype), window_strides=(1,), padding=((k - 1, 0),),
        dimension_numbers=('NWC', 'WIO', 'NWC'), feature_group_count=c)


def gated_deltanet(qkv_in, z, a, b_logit, conv_w, A_log, dt_bias, out_gain):
    f32 = jnp.float32
    bsz, s, _ = qkv_in.shape
    h, d, c = DN_HEADS, DN_HEAD_DIM, DN_CHUNK
    n = s // c
    qkv = jax.nn.silu(causal_depthwise_conv(qkv_in, conv_w)).astype(f32)
    q, k, v = jnp.split(qkv, 3, axis=-1)

    def to_chunks(t):
        return t.reshape(bsz, n, c, h, d).transpose(0, 3, 1, 2, 4)

    q, k, v = to_chunks(q), to_chunks(k), to_chunks(v)
    q = q * lax.rsqrt(jnp.sum(q * q, -1, keepdims=True) + EPS) * (d ** -0.5)
    k = k * lax.rsqrt(jnp.sum(k * k, -1, keepdims=True) + EPS)
    beta = jax.nn.sigmoid(b_logit.astype(f32)).reshape(bsz, n, c, h).transpose(0, 3, 1, 2)
    g = -jnp.exp(A_log.astype(f32)) * jax.nn.softplus(a.astype(f32) + dt_bias.astype(f32))
    g = g.reshape(bsz, n, c, h).transpose(0, 3, 1, 2)
    gc = jnp.cumsum(g, axis=-1)

    idx = jnp.arange(c)
    lower_incl = idx[:, None] >= idx[None, :]
    strict = idx[:, None] > idx[None, :]
    diff = gc[..., :, None] - gc[..., None, :]
    decay = jnp.where(lower_incl, jnp.exp(jnp.where(lower_incl, diff, 0.0)), 0.0)

    kb = k * beta[..., None]
    lmat = jnp.einsum('bhnid,bhnjd->bhnij', kb, k) * jnp.where(strict, decay, 0.0)
    eye = jnp.eye(c, dtype=f32)
    rhs = jnp.concatenate([v * beta[..., None], kb * jnp.exp(gc)[..., None]], axis=-1)
    sol = lax.linalg.triangular_solve(lmat + eye, rhs, left_side=True, lower=True, unit_diagonal=True)
    u, w = sol[..., :d], sol[..., d:]

    attn_qk = jnp.einsum('bhnid,bhnjd->bhnij', q, k) * decay
    q_dec = q * jnp.exp(gc)[..., None]
    k_dec = k * jnp.exp(gc[..., -1:] - gc)[..., None]
    chunk_decay = jnp.exp(gc[..., -1])

    def step(state, xs):
        u_n, w_n, qd_n, kd_n, a_n, cd_n = xs
        v_new = u_n - jnp.einsum('bhcd,bhde->bhce', w_n, state)
        o_n = (jnp.einsum('bhcd,bhde->bhce', qd_n, state)
               + jnp.einsum('bhij,bhje->bhie', a_n, v_new))
        state = state * cd_n[..., None, None] + jnp.einsum('bhcd,bhce->bhde', kd_n, v_new)
        return state, o_n

    xs = tuple(jnp.moveaxis(t, 2, 0) for t in (u, w, q_dec, k_dec, attn_qk, chunk_decay))
    state0 = jnp.zeros((bsz, h, d, d), f32)
    _, o = lax.scan(step, state0, xs)
    o = o.transpose(1, 0, 3, 2, 4).reshape(bsz, s, h, d)
    o = o * lax.rsqrt(jnp.mean(o * o, -1, keepdims=True) + EPS) * out_gain.astype(f32)
    o = o * jax.nn.silu(z.astype(f32)).reshape(bsz, s, h, d)
    return o.reshape(bsz, s, BRANCH_WIDTH).astype(qkv_in.dtype)


def stick_breaking_attention(qkv):
    f32 = jnp.float32
    bsz, s, _ = qkv.shape
    h, d, blk = SB_HEADS, SB_HEAD_DIM, SB_BLOCK
    q, k, v = [t.reshape(bsz, s, h, d).transpose(0, 2, 1, 3) for t in jnp.split(qkv, 3, axis=-1)]
    scale = d ** -0.5
    outs = []
    for i in range(s // blk):
        q0, kl = i * blk, (i + 1) * blk
        qb = q[:, :, q0:kl]
        kb, vb = k[:, :, :kl], v[:, :, :kl]
        logits = jnp.einsum('bhqd,bhkd->bhqk', qb, kb).astype(f32) * scale
        causal = jnp.arange(kl)[None, :] < jnp.arange(q0, kl)[:, None]
        log_not = jnp.where(causal, jax.nn.log_sigmoid(-logits), 0.0)
        tail = lax.cumsum(log_not, axis=3, reverse=True) - log_not
        weights = jnp.where(causal, jnp.exp(jax.nn.log_sigmoid(logits) + tail), 0.0)
        outs.append(jnp.einsum('bhqk,bhkd->bhqd', weights.astype(vb.dtype), vb))
    o = jnp.concatenate(outs, axis=2)
    return o.transpose(0, 2, 1, 3).reshape(bsz, s, BRANCH_WIDTH)


def setup_inputs(seed: int = 0) -> dict:
    key = jax.random.key(seed)
    ks = jax.random.split(key, 17)
    L, D, F = DEPTH, D_MODEL, D_FF
    f32 = jnp.float32

    def dense(k, shape, fan_in):
        return jax.random.normal(k, shape, f32) * (fan_in ** -0.5)

    def gain(k, shape):
        return 1.0 + 0.02 * jax.random.normal(k, shape, f32)

    dt = jnp.exp(jax.random.uniform(ks[12], (L, DN_HEADS), f32,
                                    minval=math.log(1e-3), maxval=math.log(1e-1)))
    return {
        "x": jax.random.normal(ks[0], (BATCH, SEQ, D), f32),
        "ffn_norm": gain(ks[1], (L, 2, D)),
        "ffn_w_gate": dense(ks[2], (L, 2, D, F), D),
        "ffn_w_up": dense(ks[3], (L, 2, D, F), D),
        "ffn_w_down": dense(ks[4], (L, 2, F, D), F),
        "mix_norm": gain(ks[5], (L, D)),
        "w_in": dense(ks[6], (L, D, P_IN), D),
        "b_gate": 0.01 * jax.random.normal(ks[7], (L, N_BRANCH * D), f32),
        "pool_w": dense(ks[8], (L, POOL_GROUPS, POOL_GROUP_DIM, POOL_GROUP_DIM), POOL_GROUP_DIM),
        "pool_scale": gain(ks[9], (L, BRANCH_WIDTH)),
        "dn_conv": dense(ks[10], (L, DN_CONV, 3 * BRANCH_WIDTH), DN_CONV),
        "dn_A_log": jnp.log(jax.random.uniform(ks[11], (L, DN_HEADS), f32, minval=1.0, maxval=16.0)),
        "dn_dt_bias": dt + jnp.log(-jnp.expm1(-dt)),
        "dn_out_norm": gain(ks[13], (L, DN_HEAD_DIM)),
        "w_branch": dense(ks[14], (L, N_BRANCH, BRANCH_WIDTH, D), BRANCH_WIDTH),
        "w_out": dense(ks[15], (L, D, D), D),
        "final_norm": gain(ks[16], (D,)),
    }


def reference(x, ffn_norm, ffn_w_gate, ffn_w_up, ffn_w_down, mix_norm, w_in, b_gate,
              pool_w, pool_scale, dn_conv, dn_A_log, dn_dt_bias, dn_out_norm,
              w_branch, w_out, final_norm):
    bsz, s, d_model = x.shape
    split_points = [int(p) for p in np.cumsum(IN_SPLITS)[:-1]]
    for l in range(DEPTH):
        hf = rms_norm(x, ffn_norm[l, 0])
        x = x + 0.5 * swiglu(hf, ffn_w_gate[l, 0], ffn_w_up[l, 0], ffn_w_down[l, 0])

        h = rms_norm(x, mix_norm[l])
        proj = h @ w_in[l]
        u_pool, dn_qkv, dn_z, dn_a, dn_b, sb_qkv, gate_logits = jnp.split(proj, split_points, axis=-1)
        y_pool = pool_mixer(u_pool, pool_w[l], pool_scale[l])
        y_dn = gated_deltanet(dn_qkv, dn_z, dn_a, dn_b, dn_conv[l], dn_A_log[l], dn_dt_bias[l], dn_out_norm[l])
        y_sb = stick_breaking_attention(sb_qkv)

        branches = jnp.stack([y_pool, y_dn, y_sb], axis=2)
        branch_d = jnp.einsum('bsnw,nwd->bsnd', branches, w_branch[l])
        gates = jax.nn.sigmoid((gate_logits + b_gate[l]).astype(jnp.float32)).astype(x.dtype)
        gates = gates.reshape(bsz, s, N_BRANCH, d_model)
        merged = jnp.sum(gates * branch_d, axis=2)
        x = x + merged @ w_out[l]

        hf = rms_norm(x, ffn_norm[l, 1])
        x = x + 0.5 * swiglu(hf, ffn_w_gate[l, 1], ffn_w_up[l, 1], ffn_w_down[l, 1])
    return rms_norm(x, final_norm)
```

```python
import numpy as np
from contextlib import ExitStack

import concourse.bass as bass
import concourse.mybir as mybir
from concourse.bass_utils import run_bass_kernel_spmd

F32 = mybir.dt.float32
F32R = mybir.dt.float32r
BF16 = mybir.dt.bfloat16
AF = mybir.ActivationFunctionType
ALU = mybir.AluOpType

D = 1024
KC = 8
S = 2048
TT = 512
NT = S // TT
FF = 2816
FC = 22
FH = 11
L = 4
NH = 4
EPS = 1e-6
NCORES = 8
SEQ_PER_CORE = 2
POOL_WINDOWS = (2, 4, 8, 16)
NEG = -30000.0
DNC = 128

CV_FFN = 0
CV_MIX = CV_FFN + L * 2 * KC
CV_FIN = CV_MIX + L * KC
CV_BG = CV_FIN + KC
CV_PS = CV_BG + L * 24
CV_CONV = CV_PS + L * 4
CV_ON = CV_CONV + L * 48
CV_AL = CV_ON + L
CV_DT = CV_AL + L * 4
CV_INVC = CV_DT + L * 4
CV_EPS = CV_INVC + 64
CV_ONE = CV_EPS + 1
NCV = CV_ONE + 1

CM_ID = 0
CM_ONES = 128
CM_NEGONES = 256
CM_NEGUI = 384
CM_UPI = 512
CM_LOS = 640
CM_INDA = 768
CM_INDB = 896
CM_MB = 1024
NCM = CM_MB + 4 * 512


def _host_consts():
    cm = np.zeros((128, NCM), np.float32)
    idx = np.arange(128)
    cm[:, CM_ID:CM_ID + 128] = np.eye(128, dtype=np.float32)
    cm[:, CM_ONES:CM_ONES + 128] = 1.0
    cm[:, CM_NEGONES:CM_NEGONES + 128] = -1.0
    cm[:, CM_NEGUI:CM_NEGUI + 128] = -(idx[:, None] >= idx[None, :]).astype(np.float32)
    same = (idx[:, None] // DNC) == (idx[None, :] // DNC)
    cm[:, CM_UPI:CM_UPI + 128] = (same & (idx[:, None] <= idx[None, :])).astype(np.float32)
    cm[:, CM_LOS:CM_LOS + 128] = (same & (idx[:, None] > idx[None, :])).astype(np.float32)
    cm[:, CM_INDA:CM_INDA + 128] = (idx[:, None] < DNC).astype(np.float32)
    cm[:, CM_INDB:CM_INDB + 128] = (idx[:, None] >= 64).astype(np.float32)
    t = np.arange(512)
    for m in range(4):
        ok = (m * 128 + idx[:, None]) < t[None, :]
        cm[:, CM_MB + m * 512:CM_MB + (m + 1) * 512] = np.where(ok, 0.0, NEG).astype(np.float32)
    return cm


def _fm(v):
    n = v.shape[-1] // 128
    return np.ascontiguousarray(v.reshape(n, 128).T)


def _host_cvec(inp):
    cv = np.zeros((128, NCV), np.float32)
    for l in range(L):
        for j in range(2):
            c = CV_FFN + (l * 2 + j) * KC
            cv[:, c:c + KC] = _fm(inp["ffn_norm"][l, j])
        cv[:, CV_MIX + l * KC:CV_MIX + (l + 1) * KC] = _fm(inp["mix_norm"][l])
        cv[:, CV_BG + l * 24:CV_BG + (l + 1) * 24] = _fm(inp["b_gate"][l])
        cv[:, CV_PS + l * 4:CV_PS + (l + 1) * 4] = _fm(inp["pool_scale"][l])
        for k in range(4):
            c = CV_CONV + l * 48 + k * 12
            cv[:, c:c + 12] = _fm(inp["dn_conv"][l, k])
        cv[:, CV_ON + l] = inp["dn_out_norm"][l]
        cv[:, CV_AL + l * 4:CV_AL + (l + 1) * 4] = np.broadcast_to(inp["dn_A_log"][l], (128, 4))
        cv[:, CV_DT + l * 4:CV_DT + (l + 1) * 4] = np.broadcast_to(inp["dn_dt_bias"][l], (128, 4))
    cv[:, CV_FIN:CV_FIN + KC] = _fm(inp["final_norm"])
    cv[:, CV_EPS] = EPS
    cv[:, CV_ONE] = 1.0
    for g, w in enumerate(POOL_WINDOWS):
        cnt = np.minimum(np.arange(1, 17), w).astype(np.float32)
        cv[:, CV_INVC + g * 16:CV_INVC + (g + 1) * 16] = (np.float32(1.0) / cnt)[None, :]
    return cv


def _unit_cols(w, c0, ncol=128):
    nk = w.shape[0] // 128
    blk = w[:, c0:c0 + ncol].reshape(nk, 128, ncol).transpose(1, 0, 2)
    return blk.reshape(128, nk * ncol)


WIN_POOL, WIN_DQ, WIN_DK, WIN_DV, WIN_DZ, WIN_SQ, WIN_SK, WIN_SV, WIN_G = 0, 4, 8, 12, 16, 20, 24, 28, 32
_WIN_COLS = ([i * 128 for i in range(20)] + [2568 + i * 128 for i in range(12)]
             + [4104 + i * 128 for i in range(24)])


def _host_weights(inp, layers):
    nl = len(layers)
    out = {}
    wg = np.empty((nl, 2, FC, 128, KC * 128), np.float32)
    wu = np.empty((nl, 2, FC, 128, KC * 128), np.float32)
    wd = np.empty((nl, 2, 2, KC, 128, FH * 128), np.float32)
    win = np.empty((nl, 56, 128, KC * 128), np.float32)
    wab = np.empty((nl, 128, KC * 8), np.float32)
    wb = np.empty((nl, 3, KC, 128, 4 * 128), np.float32)
    wo = np.empty((nl, KC, 128, KC * 128), np.float32)
    wpool = np.empty((nl, 128, 4 * 128), np.float32)
    for li, l in enumerate(layers):
        for j in range(2):
            G = inp["ffn_w_gate"][l, j]
            U = inp["ffn_w_up"][l, j]
            Dn = inp["ffn_w_down"][l, j]
            for fc in range(FC):
                wg[li, j, fc] = _unit_cols(G, fc * 128)
                wu[li, j, fc] = _unit_cols(U, fc * 128)
            for hf in range(2):
                sub = Dn[hf * FH * 128:(hf + 1) * FH * 128]
                for dc in range(KC):
                    wd[li, j, hf, dc] = _unit_cols(sub, dc * 128)
        W = inp["w_in"][l]
        for ci, c0 in enumerate(_WIN_COLS):
            win[li, ci] = _unit_cols(W, c0)
        wab[li] = _unit_cols(W, 2560, 8)
        for n in range(3):
            for dc in range(KC):
                wb[li, n, dc] = _unit_cols(inp["w_branch"][l, n], dc * 128)
        for dc in range(KC):
            wo[li, dc] = _unit_cols(inp["w_out"][l], dc * 128)
        wpool[li] = inp["pool_w"][l].transpose(1, 0, 2).reshape(128, 4 * 128)
    out.update(wg=wg, wu=wu, wd=wd, win=win, wab=wab, wb=wb, wo=wo, wpool=wpool)
    return out


class Sem:
    def __init__(self, h, name):
        self.h = h
        self.name = name
        self.count = 0


class Track:
    __slots__ = ("w", "rs", "excl")

    def __init__(self, excl=False):
        self.w = None
        self.rs = {}
        self.excl = excl


class Eng:
    def __init__(self, name, sem):
        self.name = name
        self.sem = sem
        self.known = {}
        self.q = []


class Ctx:
    def __init__(self, nc, es):
        self.nc = nc
        self.es = es
        self.eng = {}
        self.tracks = {}
        self.nops = 0

    def new_sem(self, name):
        return Sem(self.es.enter_context(self.nc.semaphore(name)), name)

    def add_engine(self, name):
        self.eng[name] = Eng(name, self.new_sem("s_" + name))

    def T(self, *key):
        t = self.tracks.get(key)
        if t is None:
            t = Track(excl=(key[0] == "PS"))
            self.tracks[key] = t
        return t

    @staticmethod
    def _split(R, W):
        ex = [t for t in R if t.excl]
        if ex:
            return [t for t in R if not t.excl], list(W) + ex
        return R, W

    def _waits(self, E, R, W):
        need = {}
        for t in R:
            if t.w is not None:
                s, v = t.w
                if need.get(s, 0) < v:
                    need[s] = v
        for t in W:
            if t.w is not None:
                s, v = t.w
                if need.get(s, 0) < v:
                    need[s] = v
            for s, v in t.rs.items():
                if need.get(s, 0) < v:
                    need[s] = v
        for s, v in need.items():
            if s is E.sem and E.name == "pe":
                continue
            if E.known.get(s, 0) >= v:
                continue
            E.q.append(("w", s, v))
            E.known[s] = v

    def _mark(self, tok, R, W):
        for t in R:
            t.rs[tok[0]] = tok[1]
        for t in W:
            t.w = tok
            t.rs = {}

    def op(self, en, meth, kw, R=(), W=()):
        E = self.eng[en]
        R, W = self._split(R, W)
        self._waits(E, R, W)
        if callable(meth):
            fn = meth
        else:
            fn = (lambda e, m=meth, k=dict(kw): getattr(e, m)(**k))
        E.q.append(("i", fn, E.sem, 1))
        E.sem.count += 1
        self._mark((E.sem, E.sem.count), R, W)
        self.nops += 1

    def dma(self, en, pairs, dsem, R=(), W=()):
        E = self.eng[en]
        self._waits(E, R, W)
        for o, i, kw in pairs:
            E.q.append(("i", (lambda e, o=o, i=i, kw=kw: e.dma_start(out=o, in_=i, **kw)), dsem, 16))
            dsem.count += 16
        self._mark((dsem, dsem.count), R, W)

    def barrier(self, extra_sems=()):
        sems = [e.sem for e in self.eng.values()] + list(extra_sems)
        for E in self.eng.values():
            for s in sems:
                if s is E.sem or s.count == 0:
                    continue
                if E.known.get(s, 0) >= s.count:
                    continue
                E.q.append(("w", s, s.count))
                E.known[s] = s.count

    @staticmethod
    def replay(q, e):
        for a in q:
            if a[0] == "w":
                e.wait_ge(a[1].h, a[2])
            else:
                a[1](e).then_inc(a[2].h, a[3])


class Prog:
    def __init__(self, nseq, layers, phases=("ffn0", "mix", "ffn1"), final_norm=True, debug=None,
                 mix_order=(2, 0, 1)):
        self.mix_order = mix_order
        self.dn_stop = 3
        self.marks = []
        self.npe = 0
        self.dn_bsteps = 1000
        self.nseq = nseq
        self.layers = list(layers)
        self.nl = len(self.layers)
        self.phases = phases
        self.final_norm = final_norm
        self.debug = debug

    def build(self):
        nc = bass.Bass("TRN2", target_bir_lowering=False)
        self.nc = nc
        nl = self.nl
        dr = lambda name, shape, kind="ExternalInput": nc.dram_tensor(name, list(shape), F32, kind=kind)
        self.d_x = dr("xT", (self.nseq, 128, KC, S))
        self.d_y = dr("yT", (self.nseq, 128, KC, S), "ExternalOutput")
        self.d_cv = dr("cvec", (128, NCV))
        self.d_cm = dr("cmat", (128, NCM))
        self.d_wg = dr("wg", (nl, 2, FC, 128, KC * 128))
        self.d_wu = dr("wu", (nl, 2, FC, 128, KC * 128))
        self.d_wd = dr("wd", (nl, 2, 2, KC, 128, FH * 128))
        self.d_win = dr("win", (nl, 56, 128, KC * 128))
        self.d_wab = dr("wab", (nl, 128, KC * 8))
        self.d_wb = dr("wb", (nl, 3, KC, 128, 4 * 128))
        self.d_wo = dr("wo", (nl, KC, 128, KC * 128))
        self.d_wpool = dr("wpool", (nl, 128, 4 * 128))
        self.d_xsp = dr("xspill", (128, KC, S), "Internal")
        if self.debug:
            self.d_dbg = dr("dbg", self.debug["shape"], "ExternalOutput")

        with ExitStack() as es:
            ctx = Ctx(nc, es)
            self.ctx = ctx
            sb = lambda name, shape, dt: es.enter_context(nc.sbuf_tensor(name, list(shape), dt))
            self.CV = sb("CV", (128, NCV), F32)
            self.CMF = sb("CMF", (128, CM_MB), F32)
            self.CMB = sb("CMB", (128, NCM), BF16)
            self.CMR = self.CMF
            self.NRING = 4
            self.RSZ = 1408
            self.RING = sb("RING", (128, self.NRING, self.RSZ), BF16)
            self.H = sb("H", (128, KC, S), BF16)
            self.SQ = sb("SQ", (128, 2, TT), BF16)
            self.RSTD = sb("RSTD", (128, 2, TT), F32)
            self.TMPF = sb("TMPF", (128, 2, TT), F32)
            self.ACOLS = 35840
            self.ARENA = sb("ARENA", (128, self.ACOLS), F32)
            self.PS = [es.enter_context(nc.psum_tensor("PS%d" % i, [128, 512], F32)) for i in range(8)]
            self.ring_sems = [ctx.new_sem("ring%d" % i) for i in range(self.NRING)]
            self.sem_c = ctx.new_sem("consts")
            self.sem_c2 = ctx.new_sem("consts2")
            self.sem_x = ctx.new_sem("xio")
            self.sem_o = ctx.new_sem("out")
            self.ring_i = 0
            for en in ("pe", "act", "dve", "pool", "sp"):
                ctx.add_engine(en)
            self.emit()
            ctx.barrier(extra_sems=[self.sem_o, self.sem_x, self.sem_c, self.sem_c2] + self.ring_sems)
            block = es.enter_context(nc.Block())

            @block.tensor
            def _(e):
                Ctx.replay(ctx.eng["pe"].q, e)

            @block.scalar
            def _(e):
                Ctx.replay(ctx.eng["act"].q, e)

            @block.vector
            def _(e):
                Ctx.replay(ctx.eng["dve"].q, e)

            @block.gpsimd
            def _(e):
                Ctx.replay(ctx.eng["pool"].q, e)

            @block.sync
            def _(e):
                Ctx.replay(ctx.eng["sp"].q, e)
        return nc

    def av(self, off, n, dt=F32, pat=None, **kw):
        ap = self.ARENA[:, off:off + n]
        if dt is not F32:
            ap = ap.bitcast(dt)
        if pat:
            ap = ap.rearrange(pat, **kw)
        return ap

    def cvc(self, col, n=1):
        return self.CV[:, col:col + n]

    def wload(self, dram_ap, E):
        ctx = self.ctx
        i = self.ring_i % self.NRING
        self.ring_i += 1
        tr = ctx.T("ring", i)
        dst = self.RING[:, i, 0:E]
        pairs = []
        step = 2048
        for c0 in range(0, E, step):
            c1 = min(E, c0 + step)
            pairs.append((self.RING[:, i, c0:c1], dram_ap[:, c0:c1], {}))
        ctx.dma("pool", pairs, self.ring_sems[i], R=(), W=(tr,))
        return dst, tr

    def mark(self, name):
        self.marks.append((name, self.npe))

    def mm(self, out, pairs, R, W, start=True, stop=True):
        n = len(pairs)
        self.npe += n

        def f(e):
            last = None
            for i, (l, r) in enumerate(pairs):
                last = e.matmul(out, l, r, start=(start and i == 0), stop=(stop and i == n - 1))
            return last
        self.ctx.op("pe", f, None, R, W)

    def emit(self):
        ctx = self.ctx
        T = ctx.T
        ctx.dma("sp", [(self.CV[:, :], self.d_cv[:, :], {}), (self.CMF[:, :], self.d_cm[:, 0:CM_MB], {})],
                self.sem_c, W=(T("cv"), T("cmf")))
        pairs = [(self.CMB[:, c:c + 1024], self.d_cm[:, c:c + 1024], {}) for c in range(0, NCM, 1024)]
        ctx.dma("pool", pairs, self.sem_c2, W=(T("cmb"),))
        ctx.barrier(extra_sems=[self.sem_c, self.sem_c2])
        self.X = self.av(0, KC * S, F32, "p (k n) -> p k n", k=KC)
        for sq in range(self.nseq):
            ctx.dma("sp", [(self.X[:, kc, :], self.d_x[sq, :, kc, :], {}) for kc in range(KC)],
                    self.sem_x, W=[T("X", kc, t) for kc in range(KC) for t in range(NT)])
            for li in range(self.nl):
                if "ffn0" in self.phases:
                    self.ffn(li, 0)
                if "mix" in self.phases:
                    self.mixer(li)
                if "ffn1" in self.phases:
                    self.ffn(li, 1)
            if self.final_norm:
                self.final(sq)
            else:
                ctx.dma("sp", [(self.d_y[sq, :, kc, :], self.X[:, kc, :], {}) for kc in range(KC)],
                        self.sem_o, R=[T("X", kc, t) for kc in range(KC) for t in range(NT)])
            ctx.barrier(extra_sems=[self.sem_o, self.sem_x])

    def rmsnorm(self, gcol, dst, dst_tr):
        ctx = self.ctx
        T = ctx.T
        ones_b = self.CMB[:, CM_ONES:CM_ONES + 128]
        PN = self.PS[6]
        for t in range(NT):
            ts = slice(t * TT, (t + 1) * TT)
            for kc in range(KC):
                sl = (t * KC + kc) % 2
                ctx.op("act", "activation", dict(out=self.SQ[:, sl, :], in_=self.X[:, kc, ts], func=AF.Square),
                       R=(T("X", kc, t),), W=(T("SQ", sl),))
                self.mm(PN[:, :], [(ones_b, self.SQ[:, sl, :])], R=(T("SQ", sl), T("cmb")), W=(T("PS", 6),),
                        start=(kc == 0), stop=(kc == KC - 1))
            rs = t % 2
            ctx.op("act", "activation", dict(out=self.RSTD[:, rs, :], in_=PN[:, :], func=AF.Sqrt,
                                             bias=self.cvc(CV_EPS), scale=1.0 / D),
                   R=(T("PS", 6), T("cv")), W=(T("RSTD", rs),))
            ctx.op("dve", "reciprocal", dict(out=self.RSTD[:, rs, :], in_=self.RSTD[:, rs, :]),
                   R=(T("RSTD", rs),), W=(T("RSTD", rs),))
            for kc in range(KC):
                ctx.op("dve", "scalar_tensor_tensor", dict(
                    out=dst[:, kc, ts], in0=self.X[:, kc, ts], scalar=self.cvc(gcol + kc), in1=self.RSTD[:, rs, :],
                    op0=ALU.mult, op1=ALU.mult),
                    R=(T("X", kc, t), T("RSTD", rs), T("cv")), W=(dst_tr(kc, t),))

    def ffn(self, li, j):
        ctx = self.ctx
        T = ctx.T
        ctx.barrier()
        self.mark('ffn%d' % j)
        A = self.av(KC * S, FH * S // 2, BF16, "p (f n) -> p f n", f=FH)
        self.rmsnorm(CV_FFN + (li * 2 + j) * KC, self.H, lambda kc, t: T("H", t))
        k = 0
        for hf in range(2):
            for fl in range(FH):
                fc = hf * FH + fl
                wg, wgt = self.wload(self.d_wg[li, j, fc], KC * 128)
                wu, wut = self.wload(self.d_wu[li, j, fc], KC * 128)
                for t in range(NT):
                    ts = slice(t * TT, (t + 1) * TT)
                    b = k % 2
                    k += 1
                    PG, PU = self.PS[b], self.PS[2 + b]
                    self.mm(PG[:, :], [(wg[:, kc * 128:(kc + 1) * 128], self.H[:, kc, ts]) for kc in range(KC)],
                            R=(wgt, T("H", t)), W=(T("PS", b),))
                    self.mm(PU[:, :], [(wu[:, kc * 128:(kc + 1) * 128], self.H[:, kc, ts]) for kc in range(KC)],
                            R=(wut, T("H", t)), W=(T("PS", 2 + b),))
                    ctx.op("act", "activation", dict(out=self.TMPF[:, b, :], in_=PG[:, :], func=AF.Silu),
                           R=(T("PS", b),), W=(T("TMPF", b),))
                    ctx.op("dve", "tensor_tensor", dict(out=A[:, fl, ts], in0=self.TMPF[:, b, :], in1=PU[:, :],
                                                        op=ALU.mult),
                           R=(T("TMPF", b), T("PS", 2 + b)), W=(T("A", fl, t),))
            for dc in range(KC):
                wd, wdt = self.wload(self.d_wd[li, j, hf, dc], FH * 128)
                for t in range(NT):
                    ts = slice(t * TT, (t + 1) * TT)
                    b = k % 2
                    k += 1
                    PY = self.PS[4 + b]
                    self.mm(PY[:, :], [(wd[:, fl * 128:(fl + 1) * 128], A[:, fl, ts]) for fl in range(FH)],
                            R=[wdt] + [T("A", fl, t) for fl in range(FH)], W=(T("PS", 4 + b),))
                    ctx.op("dve", "scalar_tensor_tensor", dict(
                        out=self.X[:, dc, ts], in0=PY[:, :], scalar=0.5, in1=self.X[:, dc, ts],
                        op0=ALU.mult, op1=ALU.add),
                        R=(T("PS", 4 + b), T("X", dc, t)), W=(T("X", dc, t),))

    def mixer(self, li):
        ctx = self.ctx
        T = ctx.T
        ctx.barrier()
        self.mark('mixnorm')
        self.rmsnorm(CV_MIX + li * KC, self.H, lambda kc, t: T("H", t))
        xtr = [T("X", kc, t) for kc in range(KC) for t in range(NT)]
        ctx.dma("sp", [(self.d_xsp[:, kc, :], self.X[:, kc, :], {}) for kc in range(KC)], self.sem_x, R=xtr)
        ctx.barrier(extra_sems=[self.sem_x])
        self.MACC = self.av(self.ACOLS - 8192, 8192, BF16, "p (k n) -> p k n", k=KC)
        self.Y = self.av(self.ACOLS - 12288, 4096, BF16, "p (k n) -> p k n", k=4)
        order = self.mix_order
        first = True
        for n in order:
            self.mark('branch%d' % n)
            if n == 2:
                self.sb_branch(li)
            elif n == 0:
                self.pool_branch(li)
            else:
                self.dn_branch(li)
            self.mark('merge%d' % n)
            self.merge(li, n, first)
            first = False
        ctx.barrier()
        self.mark('wout')
        ctx.dma("sp", [(self.X[:, kc, :], self.d_xsp[:, kc, :], {}) for kc in range(KC)], self.sem_x, W=xtr)
        k = 0
        for dc in range(KC):
            wo, wot = self.wload(self.d_wo[li, dc], KC * 128)
            for t in range(NT):
                ts = slice(t * TT, (t + 1) * TT)
                PY = self.PS[4 + k % 2]
                ptr = T("PS", 4 + k % 2)
                k += 1
                self.mm(PY[:, :], [(wo[:, kc * 128:(kc + 1) * 128], self.MACC[:, kc, ts]) for kc in range(KC)],
                        R=[wot] + [T("MACC", kc, t) for kc in range(KC)], W=(ptr,))
                ctx.op("dve", "tensor_tensor", dict(out=self.X[:, dc, ts], in0=PY[:, :], in1=self.X[:, dc, ts],
                                                    op=ALU.add),
                       R=(ptr, T("X", dc, t)), W=(T("X", dc, t),))

    def merge(self, li, n, first):
        ctx = self.ctx
        T = ctx.T
        k = 0
        for dc in range(KC):
            wgt, wgtt = self.wload(self.d_win[li, WIN_G + n * 8 + dc], KC * 128)
            wbr, wbrt = self.wload(self.d_wb[li, n, dc], 4 * 128)
            for t in range(NT):
                ts = slice(t * TT, (t + 1) * TT)
                b = k % 2
                k += 1
                PG, PB_ = self.PS[b], self.PS[2 + b]
                self.mm(PG[:, :], [(wgt[:, kc * 128:(kc + 1) * 128], self.H[:, kc, ts]) for kc in range(KC)],
                        R=(wgtt, T("H", t)), W=(T("PS", b),))
                ctx.op("act", "activation", dict(out=self.TMPF[:, b, :], in_=PG[:, :], func=AF.Sigmoid,
                                                 bias=self.cvc(CV_BG + li * 24 + n * 8 + dc), scale=1.0),
                       R=(T("PS", b), T("cv")), W=(T("TMPF", b),))
                self.mm(PB_[:, :], [(wbr[:, k4 * 128:(k4 + 1) * 128], self.Y[:, k4, ts]) for k4 in range(4)],
                        R=[wbrt] + [T("Y", k4, t) for k4 in range(4)], W=(T("PS", 2 + b),))
                if first:
                    ctx.op("dve", "tensor_tensor", dict(out=self.MACC[:, dc, ts], in0=self.TMPF[:, b, :],
                                                        in1=PB_[:, :], op=ALU.mult),
                           R=(T("TMPF", b), T("PS", 2 + b)), W=(T("MACC", dc, t),))
                else:
                    ctx.op("dve", "tensor_tensor", dict(out=self.RSTD[:, b, :], in0=self.TMPF[:, b, :],
                                                        in1=PB_[:, :], op=ALU.mult),
                           R=(T("TMPF", b), T("PS", 2 + b)), W=(T("RSTD", b),))
                    ctx.op("dve", "tensor_tensor", dict(out=self.MACC[:, dc, ts], in0=self.RSTD[:, b, :],
                                                        in1=self.MACC[:, dc, ts], op=ALU.add),
                           R=(T("RSTD", b), T("MACC", dc, t)), W=(T("MACC", dc, t),))

    def sb_branch(self, li):
        ctx = self.ctx
        T = ctx.T
        QT = self.av(0, 1024, BF16)
        KT = self.av(1024, 1024, BF16)
        VT = self.av(2048, 1024, BF16, "p (c d) -> p c d", c=16)
        SP = self.av(3072, 4096, BF16, "p (c t) -> p c t", c=16)
        ET = self.av(7168, 1024, F32, "p (b t) -> p b t", b=2)
        WT = self.av(8192, 512, BF16, "p (b t) -> p b t", b=2)
        SS = self.av(8704, 4096, BF16, "p (c t) -> p c t", c=16)
        ident_b = self.CMB[:, CM_ID:CM_ID + 128]
        negui = self.CMB[:, CM_NEGUI:CM_NEGUI + 128]
        negones = self.CMB[:, CM_NEGONES:CM_NEGONES + 128]
        cmb = T("cmb")
        ka = kb = ko = kp = 0
        for hd in range(NH):
            wq, wqt = self.wload(self.d_win[li, WIN_SQ + hd], KC * 128)
            wk, wkt = self.wload(self.d_win[li, WIN_SK + hd], KC * 128)
            wv, wvt = self.wload(self.d_win[li, WIN_SV + hd], KC * 128)
            for t in range(NT):
                ts = slice(t * TT, (t + 1) * TT)
                b = kp % 2
                kp += 1
                self.mm(self.PS[b][:, :], [(wq[:, kc * 128:(kc + 1) * 128], self.H[:, kc, ts]) for kc in range(KC)],
                        R=(wqt, T("H", t)), W=(T("PS", b),))
                ctx.op("dve", "tensor_scalar_mul", dict(out=QT[:, ts], in0=self.PS[b][:, :], scalar1=128.0 ** -0.5),
                       R=(T("PS", b),), W=(T("QT", t),))
                b = kp % 2
                kp += 1
                self.mm(self.PS[b][:, :], [(wk[:, kc * 128:(kc + 1) * 128], self.H[:, kc, ts]) for kc in range(KC)],
                        R=(wkt, T("H", t)), W=(T("PS", b),))
                ctx.op("dve", "tensor_copy", dict(out=KT[:, ts], in_=self.PS[b][:, :]),
                       R=(T("PS", b),), W=(T("KT", t),))
            for g4 in range(4):
                b = kp % 2
                kp += 1
                for cc in range(4):
                    c = g4 * 4 + cc
                    self.mm(self.PS[b][:, cc * 128:(cc + 1) * 128],
                            [(self.H[:, kc, c * 128:(c + 1) * 128], wv[:, kc * 128:(kc + 1) * 128]) for kc in range(KC)],
                            R=(wvt, T("H", g4)), W=(T("PS", b),))
                ctx.op("dve", "tensor_copy", dict(out=VT[:, g4 * 4:(g4 + 1) * 4, :],
                                                  in_=self.PS[b][:, :].rearrange("p (c d) -> p c d", c=4)),
                       R=(T("PS", b),), W=(T("VT", g4),))
            for I in range(4):
                n = 4 * (I + 1)
                qs = QT[:, I * TT:(I + 1) * TT]
                qtr = T("QT", I)
                for c in range(n - 1, -1, -1):
                    b = ka % 2
                    ka += 1
                    pairs = [(KT[:, c * 128:(c + 1) * 128], qs)]
                    if c >= 4 * I:
                        m = c - 4 * I
                        pairs.append((ident_b, self.CMB[:, CM_MB + m * 512:CM_MB + (m + 1) * 512]))
                    self.mm(self.PS[b][:, :], pairs, R=(T("KT", c // 4), qtr, cmb), W=(T("PS", b),))
                    ctx.op("act", "activation", dict(out=ET[:, b, :], in_=self.PS[b][:, :], func=AF.Exp),
                           R=(T("PS", b),), W=(T("ET", b),))
                    ctx.op("act", "activation", dict(out=SP[:, c, :], in_=ET[:, b, :], func=AF.Ln,
                                                     bias=self.cvc(CV_ONE), scale=1.0),
                           R=(T("ET", b), T("cv")), W=(T("SP", c),))
                    if c == n - 1:
                        pass
                    elif c >= 1:
                        prev = SP[:, c + 1, :] if c + 1 == n - 1 else SS[:, c + 1, :]
                        prevt = T("SP", c + 1) if c + 1 == n - 1 else T("SS", c + 1)
                        ctx.op("dve", "tensor_tensor", dict(out=SS[:, c, :], in0=prev, in1=SP[:, c, :], op=ALU.add),
                               R=(prevt, T("SP", c)), W=(T("SS", c),))
                po = 4 + ko % 2
                ko += 1

                def grp(c):
                    nonlocal kb
                    b = 2 + kb % 2
                    kb += 1
                    pairs = [(KT[:, c * 128:(c + 1) * 128], qs)]
                    if c >= 4 * I:
                        m = c - 4 * I
                        pairs.append((ident_b, self.CMB[:, CM_MB + m * 512:CM_MB + (m + 1) * 512]))
                    pairs.append((negui, SP[:, c, :]))
                    rd = [T("KT", c // 4), qtr, cmb, T("SP", c)]
                    if c + 1 <= n - 1:
                        if c + 1 == n - 1:
                            pairs.append((negones, SP[:, c + 1, :]))
                            rd.append(T("SP", c + 1))
                        else:
                            pairs.append((negones, SS[:, c + 1, :]))
                            rd.append(T("SS", c + 1))
                    self.mm(self.PS[b][:, :], pairs, R=rd, W=(T("PS", b),))
                    return b
                bnext = grp(0)
                for c in range(n):
                    b = bnext
                    wb_ = c % 2
                    ctx.op("act", "activation", dict(out=WT[:, wb_, :], in_=self.PS[b][:, :], func=AF.Exp),
                           R=(T("PS", b),), W=(T("WT", wb_),))
                    if c + 1 < n:
                        bnext = grp(c + 1)
                    self.mm(self.PS[po][:, :], [(VT[:, c, :], WT[:, wb_, :])],
                            R=(T("VT", c // 4), T("WT", wb_)), W=(T("PS", po),), start=(c == 0), stop=(c == n - 1))
                ctx.op("dve", "tensor_copy", dict(out=self.Y[:, hd, I * TT:(I + 1) * TT], in_=self.PS[po][:, :]),
                       R=(T("PS", po),), W=(T("Y", hd, I),))

    def pool_branch(self, li):
        ctx = self.ctx
        T = ctx.T
        PADW = 16
        UP = self.av(0, 2064, F32)
        PA = self.av(2064, 2064, F32)
        PBf = self.av(4128, 2064, F32)
        POOLED = self.av(6192, 1024, BF16)
        FIX = self.av(7216, 16, F32)
        for nm, buf in (("UP", UP), ("PA", PA), ("PBf", PBf)):
            ctx.op("dve", "memset", dict(ap=buf[:, 0:PADW], constant=0.0), W=(T(nm),))
        kp = 0
        for g, win in enumerate(POOL_WINDOWS):
            wu, wut = self.wload(self.d_win[li, WIN_POOL + g], KC * 128)
            wp, wpt = self.wload(self.d_wpool[li], 4 * 128)
            for t in range(NT):
                ts = slice(t * TT, (t + 1) * TT)
                b = kp % 2
                kp += 1
                self.mm(self.PS[b][:, :], [(wu[:, kc * 128:(kc + 1) * 128], self.H[:, kc, ts]) for kc in range(KC)],
                        R=(wut, T("H", t)), W=(T("PS", b),))
                ctx.op("dve", "tensor_copy", dict(out=UP[:, PADW + t * TT:PADW + (t + 1) * TT], in_=self.PS[b][:, :]),
                       R=(T("PS", b),), W=(T("UP"),))
            cur, curn = UP, "UP"
            bufs = [(PA, "PA"), (PBf, "PBf")]
            for lev in range(g + 1):
                sh = 2 ** lev
                nxt, nxtn = bufs[lev % 2]
                ctx.op("dve", "tensor_tensor", dict(out=nxt[:, PADW:PADW + S], in0=cur[:, PADW:PADW + S],
                                                    in1=cur[:, PADW - sh:PADW - sh + S], op=ALU.add),
                       R=(T(curn),), W=(T(nxtn),))
                cur, curn = nxt, nxtn
            ctx.op("dve", "scalar_tensor_tensor", dict(out=POOLED[:, :], in0=cur[:, PADW:PADW + S], scalar=1.0 / win,
                                                       in1=UP[:, PADW:PADW + S], op0=ALU.mult, op1=ALU.subtract),
                   R=(T(curn), T("UP")), W=(T("POOLED"),))
            ctx.op("dve", "tensor_tensor", dict(out=FIX[:, :], in0=cur[:, PADW:PADW + 16],
                                                in1=self.cvc(CV_INVC + g * 16, 16), op=ALU.mult),
                   R=(T(curn), T("cv")), W=(T("FIX"),))
            ctx.op("dve", "tensor_tensor", dict(out=POOLED[:, 0:16], in0=FIX[:, :], in1=UP[:, PADW:PADW + 16],
                                                op=ALU.subtract),
                   R=(T("FIX"), T("UP")), W=(T("POOLED"),))
            for t in range(NT):
                ts = slice(t * TT, (t + 1) * TT)
                b = kp % 2
                kp += 1
                self.mm(self.PS[b][:, :], [(wp[:, g * 128:(g + 1) * 128], POOLED[:, ts])],
                        R=(wpt, T("POOLED")), W=(T("PS", b),))
                ctx.op("dve", "tensor_scalar_mul", dict(out=self.Y[:, g, ts], in0=self.PS[b][:, :],
                                                        scalar1=self.cvc(CV_PS + li * 4 + g)),
                       R=(T("PS", b), T("cv")), W=(T("Y", g, t),))

    def dn_branch(self, li):
        ctx = self.ctx
        T = ctx.T
        o = 0

        def take(n):
            nonlocal o
            r = o
            o += n
            return r
        QINS = [self.av(take(2056), 2056, F32) for _ in range(2)]
        OT = QINS[0][:, 8:8 + S]
        QN = self.av(take(2048), 2048, F32)
        KN = self.av(take(2048), 2048, F32)
        VN = self.av(take(2048), 2048, F32)
        KDEC = KN.rearrange("p (c d) -> p c d", c=16)
        ATT = VN.rearrange("p (c d) -> p c d", c=16)
        NCH = 8
        KBG = self.av(take(2048), 2048, F32, "p (c d) -> p c d", c=16)
        WTf = KBG.rearrange("p c d -> p (c d)")
        BV = self.av(take(2048), 2048, F32, "p (c d) -> p c d", c=16)
        SQ2 = self.av(take(1024), 1024, BF16)
        IB = [[self.av(take(128), 128, F32) for _ in range(5)] for _ in range(NCH)]
        ABT = self.av(take(128), 128, F32, "p (c e) -> p c e", c=16)
        G = self.av(take(64), 64, F32, "p (h c) -> p h c", h=4)
        BETA = self.av(take(64), 64, F32, "p (h c) -> p h c", h=4)
        EXPS = self.av(take(256), 256, F32, "p (k h c) -> p k h c", k=4, h=4)
        EGC, EREM, CDA, CDB = (EXPS[:, i] for i in range(4))
        BEG = self.av(take(64), 64, F32, "p (h c) -> p h c", h=4)
        EA = self.av(take(4), 4, F32)
        E1 = self.av(take(32), 32, F32)
        ST = self.av(take(128), 128, F32)
        VNEW = self.av(take(128), 128, F32)
        RS2 = self.TMPF[:, 0, :]
        assert o <= self.ACOLS - 12288, o
        Rr = lambda ap: ap
        IDF = self.CMF[:, CM_ID:CM_ID + 128]
        UPI = self.CMF[:, CM_UPI:CM_UPI + 128]
        LOS = self.CMF[:, CM_LOS:CM_LOS + 128]
        IDR = self.CMR[:, CM_ID:CM_ID + 128]
        ONESR = self.CMR[:, CM_ONES:CM_ONES + 128]
        UPIR = self.CMR[:, CM_UPI:CM_UPI + 128]
        INDS = [self.CMR[:, c:c + 128] for c in (CM_UPI, CM_LOS, CM_INDA, CM_INDB)]
        ones_b = self.CMB[:, CM_ONES:CM_ONES + 128]
        cmf, cv = T("cmf"), T("cv")
        P7 = self.PS[7]

        wab, wabt = self.wload(self.d_wab[li], KC * 8)
        for c in range(16):
            self.mm(P7[:, c * 8:(c + 1) * 8],
                    [(self.H[:, kc, c * 128:(c + 1) * 128], wab[:, kc * 8:(kc + 1) * 8]) for kc in range(KC)],
                    R=(wabt, T("H", c // 4)), W=(T("PS", 7),))
        ctx.op("dve", "tensor_copy", dict(out=ABT[:, :, :], in_=P7[:, 0:128].rearrange("p (c e) -> p c e", c=16)),
               R=(T("PS", 7),), W=(T("ABT"),))
        ctx.op("act", "activation", dict(out=EA[:, :], in_=self.cvc(CV_AL + li * 4, 4), func=AF.Exp),
               R=(cv,), W=(T("EA"),))
        for hd in range(NH):
            ctx.op("act", "activation", dict(out=E1[:, 0:16], in_=ABT[:, :, hd], func=AF.Exp,
                                             bias=self.cvc(CV_DT + li * 4 + hd), scale=1.0),
                   R=(T("ABT"), cv), W=(T("E1"),))
            ctx.op("act", "activation", dict(out=E1[:, 0:16], in_=E1[:, 0:16], func=AF.Ln,
                                             bias=self.cvc(CV_ONE), scale=1.0),
                   R=(T("E1"), cv), W=(T("E1"),))
            ctx.op("dve", "tensor_scalar", dict(out=Rr(G[:, hd, :]), in0=E1[:, 0:16], scalar1=EA[:, hd:hd + 1], scalar2=-1.0,
                                                op0=ALU.mult, op1=ALU.mult),
                   R=(T("E1"), T("EA")), W=(T("G"),))
            ctx.op("act", "activation", dict(out=E1[:, 16:32], in_=ABT[:, :, 4 + hd], func=AF.Exp, scale=-1.0),
                   R=(T("ABT"),), W=(T("E1b"),))
            ctx.op("dve", "tensor_scalar_add", dict(out=E1[:, 16:32], in0=E1[:, 16:32], scalar1=1.0),
                   R=(T("E1b"),), W=(T("E1b"),))
            ctx.op("dve", "reciprocal", dict(out=BETA[:, hd, :], in_=E1[:, 16:32]),
                   R=(T("E1b"),), W=(T("BETA"),))
        Gf = G.rearrange("p h c -> p (h c)")
        for i, M in enumerate(INDS):
            self.mm(P7[:, 128 + i * 64:128 + (i + 1) * 64], [(M, Rr(Gf))], R=(T("G"), cmf), W=(T("PS", 7),))
        ctx.op("act", "activation", dict(out=EXPS.rearrange("p k h c -> p (k h c)"), in_=P7[:, 128:384], func=AF.Exp),
               R=(T("PS", 7),), W=(T("EXPS"),))
        ctx.op("dve", "tensor_tensor", dict(out=BEG[:, :, :], in0=BETA[:, :, :], in1=EGC, op=ALU.mult),
               R=(T("BETA"), T("EXPS")), W=(T("BEG"),))

        kp = 0
        if self.dn_stop == 0:
            return
        for hd in range(NH):
            self.mark('dnA%d' % hd)
            ZS = self.Y[:, hd, :]
            def chunk_gen(jq, nm, dst, widx, cidx):
                nonlocal kp
                qsel = (3 * hd + jq) % 2
                QIN = QINS[qsel]
                w, wt = self.wload(self.d_win[li, widx], KC * 128)
                ctx.op("dve", "memset", dict(ap=QIN[:, 0:8], constant=0.0), W=(T("QIN", qsel),))
                for t in range(NT):
                    ts = slice(t * TT, (t + 1) * TT)
                    b = kp % 2
                    kp += 1
                    self.mm(self.PS[b][:, :], [(w[:, kc * 128:(kc + 1) * 128], self.H[:, kc, ts]) for kc in range(KC)],
                            R=(wt, T("H", t)), W=(T("PS", b),))
                    ctx.op("act", "activation", dict(out=QIN[:, 8 + t * TT:8 + (t + 1) * TT], in_=self.PS[b][:, :], func=AF.Copy),
                           R=(T("PS", b),), W=(T("QIN", qsel),))
                yield
                dtr = [T(nm, p) for p in range(16)]
                cbase = CV_CONV + li * 48 + cidx
                ctx.op("act", "activation", dict(out=Rr(dst[:, :]), in_=QIN[:, 5:5 + S], func=AF.Copy, scale=self.cvc(cbase)),
                       R=(T("QIN", qsel), cv), W=dtr)
                for k in range(1, 4):
                    ctx.op("dve", "scalar_tensor_tensor", dict(out=Rr(dst[:, :]), in0=QIN[:, 5 + k:5 + k + S],
                                                               scalar=self.cvc(cbase + 12 * k), in1=dst[:, :],
                                                               op0=ALU.mult, op1=ALU.add),
                           R=[T("QIN", qsel), cv] + dtr, W=dtr)
                yield
                ctx.op("act", "activation", dict(out=Rr(dst[:, :]), in_=dst[:, :], func=AF.Silu), R=dtr, W=dtr)
                if nm != "VN":
                    ctx.op("act", "activation", dict(out=SQ2[:, :], in_=dst[:, :], func=AF.Square), R=dtr, W=(T("SQ2"),))
                    for t in range(NT):
                        ts = slice(t * TT, (t + 1) * TT)
                        b = kp % 2
                        kp += 1
                        ptr = [T(nm, p) for p in range(4 * t, 4 * t + 4)]
                        self.mm(self.PS[b][:, :], [(ones_b, SQ2[:, ts])], R=(T("SQ2"), T("cmb")), W=(T("PS", b),))
                        ctx.op("act", "activation", dict(out=RS2[:, :], in_=self.PS[b][:, :], func=AF.Sqrt,
                                                         bias=self.cvc(CV_EPS), scale=1.0),
                               R=(T("PS", b), cv), W=(T("TMPF", 0),))
                        ctx.op("dve", "reciprocal", dict(out=RS2[:, :], in_=RS2[:, :]), R=(T("TMPF", 0),), W=(T("TMPF", 0),))
                        if nm == "QN":
                            ctx.op("dve", "scalar_tensor_tensor", dict(out=Rr(dst[:, ts]), in0=dst[:, ts], scalar=128.0 ** -0.5,
                                                                       in1=RS2[:, :], op0=ALU.mult, op1=ALU.mult),
                                   R=[T("TMPF", 0)] + ptr, W=ptr)
                        else:
                            ctx.op("dve", "tensor_tensor", dict(out=Rr(dst[:, ts]), in0=dst[:, ts], in1=RS2[:, :], op=ALU.mult),
                                   R=[T("TMPF", 0)] + ptr, W=ptr)

            def _fin(g_):
                for _ in g_:
                    pass
            cg = [chunk_gen(jq, *args) for jq, args in enumerate((("QN", QN, WIN_DQ + hd, hd), ("KN", KN, WIN_DK + hd, 4 + hd),
                                                                  ("VN", VN, WIN_DV + hd, 8 + hd)))]
            next(cg[0]); next(cg[0]); next(cg[1]); _fin(cg[0]); next(cg[1]); next(cg[2]); _fin(cg[1]); next(cg[2]); _fin(cg[2])
            w, wt = self.wload(self.d_win[li, WIN_DZ + hd], KC * 128)
            for t in range(NT):
                ts = slice(t * TT, (t + 1) * TT)
                b = kp % 2
                kp += 1
                self.mm(self.PS[b][:, :], [(w[:, kc * 128:(kc + 1) * 128], self.H[:, kc, ts]) for kc in range(KC)],
                        R=(wt, T("H", t)), W=(T("PS", b),))
                ctx.op("act", "activation", dict(out=ZS[:, ts], in_=self.PS[b][:, :], func=AF.Silu),
                       R=(T("PS", b),), W=(T("Y", hd, t),))

            if self.dn_stop == 1:
                continue
            self.mark('dnB%d' % hd)
            def chain(ci, p):
                cs = slice(p * 128, (p + 1) * 128)
                B0, B1, B2, B3, B4 = IB[ci]
                bt = [T("IB", ci, i) for i in range(5)]
                Pc = self.PS[ci]
                pt = T("PS", ci)

                def q(i):
                    return Pc[:, i * 128:(i + 1) * 128]
                kn, qn, vn = T("KN", p), T("QN", p), T("VN", p)
                self.mm(q(0), [(Rr(KN[:, cs]), IDR)], R=(kn, cmf), W=(pt,))
                self.mm(q(1), [(Rr(VN[:, cs]), IDR)], R=(vn, cmf), W=(pt,))
                self.mm(q(2), [(Rr(KN[:, cs]), Rr(KN[:, cs]))], R=(kn,), W=(pt,))
                self.mm(q(3), [(Rr(KN[:, cs]), Rr(QN[:, cs]))], R=(kn, qn), W=(pt,))
                ctx.op("dve", "tensor_scalar_mul", dict(out=Rr(B0), in0=LOS, scalar1=G[:, hd, p:p + 1]),
                       R=(T("G"), cmf), W=(bt[0],))
                yield
                ctx.op("dve", "tensor_scalar_mul", dict(out=Rr(KBG[:, p, :]), in0=q(0), scalar1=BEG[:, hd, p:p + 1]),
                       R=(pt, T("BEG")), W=(T("KBG", p),))
                ctx.op("dve", "tensor_scalar_mul", dict(out=Rr(KDEC[:, p, :]), in0=q(0), scalar1=EREM[:, hd, p:p + 1]),
                       R=(pt, T("EXPS")), W=(kn,))
                ctx.op("dve", "tensor_scalar_mul", dict(out=Rr(BV[:, p, :]), in0=q(1), scalar1=BETA[:, hd, p:p + 1]),
                       R=(pt, T("BETA")), W=(T("BV", p),))
                self.mm(q(0), [(UPIR, Rr(B0))], R=(bt[0], cmf), W=(pt,))
                self.mm(q(1), [(Rr(B0), UPIR)], R=(bt[0], cmf), W=(pt,))
                yield
                ctx.op("act", "activation", dict(out=B1, in_=q(0), func=AF.Exp), R=(pt,), W=(bt[1],))
                ctx.op("act", "activation", dict(out=B2, in_=q(1), func=AF.Exp), R=(pt,), W=(bt[2],))
                yield
                ctx.op("dve", "scalar_tensor_tensor", dict(out=Rr(B3), in0=B1, scalar=BETA[:, hd, p:p + 1], in1=LOS,
                                                           op0=ALU.mult, op1=ALU.mult),
                       R=(bt[1], T("BETA"), cmf), W=(bt[3],))
                ctx.op("dve", "tensor_tensor", dict(out=Rr(B3), in0=B3, in1=q(2), op=ALU.mult),
                       R=(bt[3], pt), W=(bt[3],))
                ctx.op("dve", "tensor_tensor", dict(out=B2, in0=B2, in1=UPI, op=ALU.mult),
                       R=(bt[2], cmf), W=(bt[2],))
                ctx.op("dve", "tensor_tensor", dict(out=Rr(ATT[:, p, :]), in0=B2, in1=q(3), op=ALU.mult),
                       R=(bt[2], pt), W=(vn,))
                yield
                self.mm(q(0), [(Rr(B3), IDR)], R=(bt[3], cmf), W=(pt,))
                yield
                ctx.op("dve", "tensor_copy", dict(out=Rr(B4), in_=q(0)), R=(pt,), W=(bt[4],))
                ctx.op("dve", "tensor_tensor", dict(out=Rr(B2), in0=IDF, in1=q(0), op=ALU.subtract),
                       R=(pt, cmf, vn), W=(bt[2],))
                yield
                Lc, Uc, Ln, Un = 3, 4, 1, 0
                NLEV = 6 if DNC == 128 else 5
                for lev in range(NLEV):
                    last = lev == NLEV - 1
                    if not last:
                        self.mm(q(0), [(Rr(IB[ci][Lc]), Rr(IB[ci][Uc]))], R=(bt[Lc], bt[Uc]), W=(pt,))
                    self.mm(q(1), [(Rr(IB[ci][Uc]), Rr(IB[ci][Lc]))], R=(bt[Lc], bt[Uc]), W=(pt,))
                    yield
                    if not last:
                        ctx.op("act", "activation", dict(out=Rr(IB[ci][Un]), in_=q(0), func=AF.Copy),
                               R=(pt,), W=(bt[Un],))
                    ctx.op("dve", "tensor_copy", dict(out=Rr(IB[ci][Ln]), in_=q(1)), R=(pt,), W=(bt[Ln],))
                    yield
                    self.mm(q(2), [(Rr(IB[ci][Ln]), Rr(B2))], R=(bt[Ln], bt[2]), W=(pt,))
                    yield
                    ctx.op("dve", "tensor_tensor", dict(out=Rr(B2), in0=B2, in1=q(2), op=ALU.add),
                           R=(bt[2], pt), W=(bt[2],))
                    yield
                    Lc, Ln = Ln, Lc
                    Uc, Un = Un, Uc
                self.mm(q(0), [(Rr(B2), Rr(BV[:, p, :]))], R=(bt[2], T("BV", p)), W=(pt,))
                self.mm(q(1), [(Rr(KBG[:, p, :]), Rr(B2))], R=(bt[2], T("KBG", p)), W=(pt,))
                ctx.op("dve", "tensor_scalar_mul", dict(out=Rr(B0), in0=IDF, scalar1=EGC[:, hd, p:p + 1]),
                       R=(T("EXPS"), cmf, bt[0]), W=(bt[0],))
                self.mm(q(2), [(ONESR, Rr(B0))], R=(bt[0], cmf), W=(pt,))
                yield
                ctx.op("dve", "tensor_copy", dict(out=Rr(BV[:, p, :]), in_=q(0)), R=(pt,), W=(T("BV", p),))
                ctx.op("dve", "tensor_copy", dict(out=Rr(WTf[:, cs]), in_=q(1)),
                       R=(pt,), W=(T("KBG", p),))
                ctx.op("dve", "tensor_tensor", dict(out=Rr(QN[:, cs]), in0=QN[:, cs], in1=q(2), op=ALU.mult),
                       R=(qn, pt), W=(qn,))
                yield

            for g4 in range(16 // NCH):
                gens = [chain(ci, g4 * NCH + ci) for ci in range(NCH)]
                alive = True
                rounds = 0
                while alive and rounds < self.dn_bsteps:
                    rounds += 1
                    alive = False
                    for gen in gens:
                        try:
                            next(gen)
                            alive = True
                        except StopIteration:
                            pass

            if self.dn_stop == 2:
                continue
            self.mark('dnC%d' % hd)
            ctx.op("dve", "memset", dict(ap=Rr(ST[:, :]), constant=0.0), W=(T("ST"),))
            NSTEP = S // DNC
            PERB = 512 // DNC
            for n in range(NSTEP):
                if DNC == 128:
                    p, hf = n, 0
                    r = slice(0, 128)
                else:
                    p, hf = n // 2, n % 2
                    r = slice(hf * 64, hf * 64 + 64)
                a = n % 2
                PVb, PSb = self.PS[a], self.PS[2 + a]
                po = 4 + (n // PERB) % 2
                self.mm(PVb[:, 0:128], [(Rr(WTf[:, p * 128:(p + 1) * 128]), Rr(ST[:, :]))], R=(T("KBG", p), T("ST")),
                        W=(T("PS", a),))
                oc = slice((n % PERB) * DNC, (n % PERB) * DNC + DNC)
                self.mm(self.PS[po][:, oc], [(Rr(ST[:, :]), Rr(QN[:, n * DNC:(n + 1) * DNC]))],
                        R=(T("ST"), T("QN", p)), W=(T("PS", po),), start=True, stop=False)
                ctx.op("dve", "tensor_tensor", dict(out=Rr(VNEW[r, :]), in0=BV[r, p, :], in1=PVb[r, 0:128], op=ALU.subtract),
                       R=(T("BV", p), T("PS", a)), W=(T("VNEW"),))
                self.mm(PSb[:, 0:128], [(Rr(KDEC[r, p, :]), Rr(VNEW[r, :]))], R=(T("KN", p), T("VNEW")), W=(T("PS", 2 + a),))
                self.mm(self.PS[po][:, oc], [(Rr(VNEW[r, :]), Rr(ATT[r, p, r]))],
                        R=(T("VNEW"), T("VN", p)), W=(T("PS", po),), start=False, stop=True)
                cd = (CDA if hf == 0 else CDB)[:, hd, p:p + 1]
                ctx.op("dve", "scalar_tensor_tensor", dict(out=Rr(ST[:, :]), in0=ST[:, :], scalar=cd, in1=PSb[:, 0:128],
                                                           op0=ALU.mult, op1=ALU.add),
                       R=(T("ST"), T("EXPS"), T("PS", 2 + a)), W=(T("ST"),))
                if n % PERB == PERB - 1:
                    t = n // PERB
                    ctx.op("dve", "tensor_copy", dict(out=OT[:, t * TT:(t + 1) * TT], in_=self.PS[po][:, :]),
                           R=(T("PS", po),), W=(T("QIN", 0),))
            self.mark('dnN%d' % hd)
            ctx.op("act", "activation", dict(out=SQ2[:, :], in_=OT, func=AF.Square), R=(T("QIN", 0),), W=(T("SQ2"),))
            for t in range(NT):
                ts = slice(t * TT, (t + 1) * TT)
                b = kp % 2
                kp += 1
                self.mm(self.PS[b][:, :], [(ones_b, SQ2[:, ts])], R=(T("SQ2"), T("cmb")), W=(T("PS", b),))
                ctx.op("act", "activation", dict(out=self.RSTD[:, 0, :], in_=self.PS[b][:, :], func=AF.Sqrt,
                                                 bias=self.cvc(CV_EPS), scale=1.0 / 128),
                       R=(T("PS", b), cv), W=(T("RSTD", 0),))
                ctx.op("dve", "reciprocal", dict(out=self.RSTD[:, 0, :], in_=self.RSTD[:, 0, :]), R=(T("RSTD", 0),), W=(T("RSTD", 0),))
                ctx.op("dve", "scalar_tensor_tensor", dict(out=self.TMPF[:, b, :], in0=OT[:, ts], scalar=self.cvc(CV_ON + li),
                                                           in1=self.RSTD[:, 0, :], op0=ALU.mult, op1=ALU.mult),
                       R=(T("QIN", 0), T("RSTD", 0), cv), W=(T("TMPF", b),))
                ctx.op("dve", "tensor_tensor", dict(out=self.Y[:, hd, ts], in0=self.TMPF[:, b, :], in1=ZS[:, ts], op=ALU.mult),
                       R=(T("TMPF", b), T("Y", hd, t)), W=(T("Y", hd, t),))


    def final(self, sq):
        ctx = self.ctx
        T = ctx.T
        ctx.barrier()
        self.mark('final')
        OUT = self.av(KC * S, KC * S, F32, "p (k n) -> p k n", k=KC)
        self.rmsnorm(CV_FIN, OUT, lambda kc, t: T("OUT", kc, t))
        ctx.dma("sp", [(self.d_y[sq, :, kc, :], OUT[:, kc, :], {}) for kc in range(KC)],
                self.sem_o, R=[T("OUT", kc, t) for kc in range(KC) for t in range(NT)])


def _prep_core_inputs(inp, seqs, layers):
    x = inp["x"]
    xT = np.stack([np.ascontiguousarray(x[b].T.reshape(KC, 128, S).transpose(1, 0, 2)) for b in seqs])
    m = {"xT": xT, "cvec": _host_cvec(inp), "cmat": _host_consts()}
    return m


def _unlayout(yT):
    return np.ascontiguousarray(yT.transpose(1, 0, 2).reshape(D, S).T)


_CACHE = {}


def kernel(**inputs):
    inp = {k: np.asarray(v, dtype=np.float32) for k, v in inputs.items()}
    B = inp["x"].shape[0]
    prog = Prog(SEQ_PER_CORE, range(L))
    nc = prog.build()
    wts = _host_weights(inp, range(L))
    cv, cm = _host_cvec(inp), _host_consts()
    in_maps = []
    for c in range(NCORES):
        seqs = [c * SEQ_PER_CORE + i for i in range(SEQ_PER_CORE)]
        m = _prep_core_inputs(inp, seqs, range(L))
        m.update(wts)
        in_maps.append(m)
    res = run_bass_kernel_spmd(nc, in_maps, core_ids=list(range(NCORES)))
    out = np.empty((B, S, D), np.float32)
    for c in range(NCORES):
        yT = res.results[c]["yT"]
        for i in range(SEQ_PER_CORE):
            out[c * SEQ_PER_CORE + i] = _unlayout(yT[i])
    return out
```

```python
import numpy as np
from contextlib import ExitStack

import concourse.bass as bass
import concourse.mybir as mybir
from concourse.bass_utils import run_bass_kernel_spmd

F32 = mybir.dt.float32
F32R = mybir.dt.float32r
BF16 = mybir.dt.bfloat16
AF = mybir.ActivationFunctionType
ALU = mybir.AluOpType

D = 1024
KC = 8
S = 2048
TT = 512
NT = S // TT
FF = 2816
FC = 22
FH = 11
L = 4
NH = 4
EPS = 1e-6
NCORES = 8
SEQ_PER_CORE = 2
POOL_WINDOWS = (2, 4, 8, 16)
NEG = -30000.0
DNC = 128

CV_FFN = 0
CV_MIX = CV_FFN + L * 2 * KC
CV_FIN = CV_MIX + L * KC
CV_BG = CV_FIN + KC
CV_PS = CV_BG + L * 24
CV_CONV = CV_PS + L * 4
CV_ON = CV_CONV + L * 48
CV_AL = CV_ON + L
CV_DT = CV_AL + L * 4
CV_INVC = CV_DT + L * 4
CV_EPS = CV_INVC + 64
CV_ONE = CV_EPS + 1
NCV = CV_ONE + 1

CM_ID = 0
CM_ONES = 128
CM_NEGONES = 256
CM_NEGUI = 384
CM_UPI = 512
CM_LOS = 640
CM_INDA = 768
CM_INDB = 896
CM_MB = 1024
NCM = CM_MB + 4 * 512


def _host_consts():
    cm = np.zeros((128, NCM), np.float32)
    idx = np.arange(128)
    cm[:, CM_ID:CM_ID + 128] = np.eye(128, dtype=np.float32)
    cm[:, CM_ONES:CM_ONES + 128] = 1.0
    cm[:, CM_NEGONES:CM_NEGONES + 128] = -1.0
    cm[:, CM_NEGUI:CM_NEGUI + 128] = -(idx[:, None] >= idx[None, :]).astype(np.float32)
    same = (idx[:, None] // DNC) == (idx[None, :] // DNC)
    cm[:, CM_UPI:CM_UPI + 128] = (same & (idx[:, None] <= idx[None, :])).astype(np.float32)
    cm[:, CM_LOS:CM_LOS + 128] = (same & (idx[:, None] > idx[None, :])).astype(np.float32)
    cm[:, CM_INDA:CM_INDA + 128] = (idx[:, None] < DNC).astype(np.float32)
    cm[:, CM_INDB:CM_INDB + 128] = (idx[:, None] >= 64).astype(np.float32)
    t = np.arange(512)
    for m in range(4):
        ok = (m * 128 + idx[:, None]) < t[None, :]
        cm[:, CM_MB + m * 512:CM_MB + (m + 1) * 512] = np.where(ok, 0.0, NEG).astype(np.float32)
    return cm


def _fm(v):
    n = v.shape[-1] // 128
    return np.ascontiguousarray(v.reshape(n, 128).T)


def _host_cvec(inp):
    cv = np.zeros((128, NCV), np.float32)
    for l in range(L):
        for j in range(2):
            c = CV_FFN + (l * 2 + j) * KC
            cv[:, c:c + KC] = _fm(inp["ffn_norm"][l, j])
        cv[:, CV_MIX + l * KC:CV_MIX + (l + 1) * KC] = _fm(inp["mix_norm"][l])
        cv[:, CV_BG + l * 24:CV_BG + (l + 1) * 24] = _fm(inp["b_gate"][l])
        cv[:, CV_PS + l * 4:CV_PS + (l + 1) * 4] = _fm(inp["pool_scale"][l])
        for k in range(4):
            c = CV_CONV + l * 48 + k * 12
            cv[:, c:c + 12] = _fm(inp["dn_conv"][l, k])
        cv[:, CV_ON + l] = inp["dn_out_norm"][l]
        cv[:, CV_AL + l * 4:CV_AL + (l + 1) * 4] = np.broadcast_to(inp["dn_A_log"][l], (128, 4))
        cv[:, CV_DT + l * 4:CV_DT + (l + 1) * 4] = np.broadcast_to(inp["dn_dt_bias"][l], (128, 4))
    cv[:, CV_FIN:CV_FIN + KC] = _fm(inp["final_norm"])
    cv[:, CV_EPS] = EPS
    cv[:, CV_ONE] = 1.0
    for g, w in enumerate(POOL_WINDOWS):
        cnt = np.minimum(np.arange(1, 17), w).astype(np.float32)
        cv[:, CV_INVC + g * 16:CV_INVC + (g + 1) * 16] = (np.float32(1.0) / cnt)[None, :]
    return cv


def _unit_cols(w, c0, ncol=128):
    nk = w.shape[0] // 128
    blk = w[:, c0:c0 + ncol].reshape(nk, 128, ncol).transpose(1, 0, 2)
    return blk.reshape(128, nk * ncol)


WIN_POOL, WIN_DQ, WIN_DK, WIN_DV, WIN_DZ, WIN_SQ, WIN_SK, WIN_SV, WIN_G = 0, 4, 8, 12, 16, 20, 24, 28, 32
_WIN_COLS = ([i * 128 for i in range(20)] + [2568 + i * 128 for i in range(12)]
             + [4104 + i * 128 for i in range(24)])


def _host_weights(inp, layers):
    nl = len(layers)
    out = {}
    wg = np.empty((nl, 2, FC, 128, KC * 128), np.float32)
    wu = np.empty((nl, 2, FC, 128, KC * 128), np.float32)
    wd = np.empty((nl, 2, 2, KC, 128, FH * 128), np.float32)
    win = np.empty((nl, 56, 128, KC * 128), np.float32)
    wab = np.empty((nl, 128, KC * 8), np.float32)
    wb = np.empty((nl, 3, KC, 128, 4 * 128), np.float32)
    wo = np.empty((nl, KC, 128, KC * 128), np.float32)
    wpool = np.empty((nl, 128, 4 * 128), np.float32)
    for li, l in enumerate(layers):
        for j in range(2):
            G = inp["ffn_w_gate"][l, j]
            U = inp["ffn_w_up"][l, j]
            Dn = inp["ffn_w_down"][l, j]
            for fc in range(FC):
                wg[li, j, fc] = _unit_cols(G, fc * 128)
                wu[li, j, fc] = _unit_cols(U, fc * 128)
            for hf in range(2):
                sub = Dn[hf * FH * 128:(hf + 1) * FH * 128]
                for dc in range(KC):
                    wd[li, j, hf, dc] = _unit_cols(sub, dc * 128)
        W = inp["w_in"][l]
        for ci, c0 in enumerate(_WIN_COLS):
            win[li, ci] = _unit_cols(W, c0)
        wab[li] = _unit_cols(W, 2560, 8)
        for n in range(3):
            for dc in range(KC):
                wb[li, n, dc] = _unit_cols(inp["w_branch"][l, n], dc * 128)
        for dc in range(KC):
            wo[li, dc] = _unit_cols(inp["w_out"][l], dc * 128)
        wpool[li] = inp["pool_w"][l].transpose(1, 0, 2).reshape(128, 4 * 128)
    out.update(wg=wg, wu=wu, wd=wd, win=win, wab=wab, wb=wb, wo=wo, wpool=wpool)
    return out


class Sem:
    def __init__(self, h, name):
        self.h = h
        self.name = name
        self.count = 0


class Track:
    __slots__ = ("w", "rs", "excl")

    def __init__(self, excl=False):
        self.w = None
        self.rs = {}
        self.excl = excl


class Eng:
    def __init__(self, name, sem):
        self.name = name
        self.sem = sem
        self.known = {}
        self.q = []


class Ctx:
    def __init__(self, nc, es):
        self.nc = nc
        self.es = es
        self.eng = {}
        self.tracks = {}
        self.nops = 0

    def new_sem(self, name):
        return Sem(self.es.enter_context(self.nc.semaphore(name)), name)

    def add_engine(self, name):
        self.eng[name] = Eng(name, self.new_sem("s_" + name))

    def T(self, *key):
        t = self.tracks.get(key)
        if t is None:
            t = Track(excl=(key[0] == "PS"))
            self.tracks[key] = t
        return t

    @staticmethod
    def _split(R, W):
        ex = [t for t in R if t.excl]
        if ex:
            return [t for t in R if not t.excl], list(W) + ex
        return R, W

    def _waits(self, E, R, W):
        need = {}
        for t in R:
            if t.w is not None:
                s, v = t.w
                if need.get(s, 0) < v:
                    need[s] = v
        for t in W:
            if t.w is not None:
                s, v = t.w
                if need.get(s, 0) < v:
                    need[s] = v
            for s, v in t.rs.items():
                if need.get(s, 0) < v:
                    need[s] = v
        for s, v in need.items():
            if s is E.sem and E.name == "pe":
                continue
            if E.known.get(s, 0) >= v:
                continue
            E.q.append(("w", s, v))
            E.known[s] = v

    def _mark(self, tok, R, W):
        for t in R:
            t.rs[tok[0]] = tok[1]
        for t in W:
            t.w = tok
            t.rs = {}

    def op(self, en, meth, kw, R=(), W=()):
        E = self.eng[en]
        R, W = self._split(R, W)
        self._waits(E, R, W)
        if callable(meth):
            fn = meth
        else:
            fn = (lambda e, m=meth, k=dict(kw): getattr(e, m)(**k))
        E.q.append(("i", fn, E.sem, 1))
        E.sem.count += 1
        self._mark((E.sem, E.sem.count), R, W)
        self.nops += 1

    def dma(self, en, pairs, dsem, R=(), W=()):
        E = self.eng[en]
        self._waits(E, R, W)
        for o, i, kw in pairs:
            E.q.append(("i", (lambda e, o=o, i=i, kw=kw: e.dma_start(out=o, in_=i, **kw)), dsem, 16))
            dsem.count += 16
        self._mark((dsem, dsem.count), R, W)

    def barrier(self, extra_sems=(), include_pool=False):
        sems = [e.sem for e in self.eng.values()] + list(extra_sems)
        for E in self.eng.values():
            if E.name == "pool" and not include_pool:
                continue
            for s in sems:
                if s is E.sem or s.count == 0:
                    continue
                if E.known.get(s, 0) >= s.count:
                    continue
                E.q.append(("w", s, s.count))
                E.known[s] = s.count

    @staticmethod
    def replay(q, e):
        for a in q:
            if a[0] == "w":
                e.wait_ge(a[1].h, a[2])
            else:
                a[1](e).then_inc(a[2].h, a[3])


class Prog:
    def __init__(self, nseq, layers, phases=("ffn0", "mix", "ffn1"), final_norm=True, debug=None,
                 mix_order=(2, 0, 1)):
        self.mix_order = mix_order
        self.dn_stop = 3
        self.marks = []
        self.npe = 0
        self.dn_bsteps = 1000
        self.nseq = nseq
        self.layers = list(layers)
        self.nl = len(self.layers)
        self.phases = phases
        self.final_norm = final_norm
        self.debug = debug

    def build(self):
        nc = bass.Bass("TRN2", target_bir_lowering=False)
        self.nc = nc
        nl = self.nl
        dr = lambda name, shape, kind="ExternalInput": nc.dram_tensor(name, list(shape), F32, kind=kind)
        self.d_x = dr("xT", (self.nseq, 128, KC, S))
        self.d_y = dr("yT", (self.nseq, 128, KC, S), "ExternalOutput")
        self.d_cv = dr("cvec", (128, NCV))
        self.d_cm = dr("cmat", (128, NCM))
        self.d_wg = dr("wg", (nl, 2, FC, 128, KC * 128))
        self.d_wu = dr("wu", (nl, 2, FC, 128, KC * 128))
        self.d_wd = dr("wd", (nl, 2, 2, KC, 128, FH * 128))
        self.d_win = dr("win", (nl, 56, 128, KC * 128))
        self.d_wab = dr("wab", (nl, 128, KC * 8))
        self.d_wb = dr("wb", (nl, 3, KC, 128, 4 * 128))
        self.d_wo = dr("wo", (nl, KC, 128, KC * 128))
        self.d_wpool = dr("wpool", (nl, 128, 4 * 128))
        self.d_xsp = dr("xspill", (128, KC, S), "Internal")
        if self.debug:
            self.d_dbg = dr("dbg", self.debug["shape"], "ExternalOutput")

        with ExitStack() as es:
            ctx = Ctx(nc, es)
            self.ctx = ctx
            sb = lambda name, shape, dt: es.enter_context(nc.sbuf_tensor(name, list(shape), dt))
            self.CV = sb("CV", (128, NCV), F32)
            self.CMF = sb("CMF", (128, CM_MB), F32)
            self.CMB = sb("CMB", (128, NCM), BF16)
            self.CMR = self.CMF
            self.NRING = 5
            self.RSZ = 1408
            self.RING = sb("RING", (128, self.NRING, self.RSZ), BF16)
            self.H = sb("H", (128, KC, S), BF16)
            self.SQ = sb("SQ", (128, 2, TT), BF16)
            self.RSTD = sb("RSTD", (128, 2, TT), F32)
            self.TMPF = sb("TMPF", (128, 2, TT), F32)
            self.ACOLS = 35840
            self.ARENA = sb("ARENA", (128, self.ACOLS), F32)
            self.PS = [es.enter_context(nc.psum_tensor("PS%d" % i, [128, 512], F32)) for i in range(8)]
            self.ring_sems = [ctx.new_sem("ring%d" % i) for i in range(self.NRING)]
            self.sem_c = ctx.new_sem("consts")
            self.sem_c2 = ctx.new_sem("consts2")
            self.sem_x = ctx.new_sem("xio")
            self.sem_o = ctx.new_sem("out")
            self.ring_i = 0
            for en in ("pe", "act", "dve", "pool", "sp"):
                ctx.add_engine(en)
            self.emit()
            ctx.barrier(extra_sems=[self.sem_o, self.sem_x, self.sem_c, self.sem_c2] + self.ring_sems)
            block = es.enter_context(nc.Block())

            @block.tensor
            def _(e):
                Ctx.replay(ctx.eng["pe"].q, e)

            @block.scalar
            def _(e):
                Ctx.replay(ctx.eng["act"].q, e)

            @block.vector
            def _(e):
                Ctx.replay(ctx.eng["dve"].q, e)

            @block.gpsimd
            def _(e):
                Ctx.replay(ctx.eng["pool"].q, e)

            @block.sync
            def _(e):
                Ctx.replay(ctx.eng["sp"].q, e)
        return nc

    def av(self, off, n, dt=F32, pat=None, **kw):
        ap = self.ARENA[:, off:off + n]
        if dt is not F32:
            ap = ap.bitcast(dt)
        if pat:
            ap = ap.rearrange(pat, **kw)
        return ap

    def cvc(self, col, n=1):
        return self.CV[:, col:col + n]

    def wload(self, dram_ap, E):
        ctx = self.ctx
        i = self.ring_i % self.NRING
        self.ring_i += 1
        tr = ctx.T("ring", i)
        dst = self.RING[:, i, 0:E]
        pairs = []
        step = 2048
        for c0 in range(0, E, step):
            c1 = min(E, c0 + step)
            pairs.append((self.RING[:, i, c0:c1], dram_ap[:, c0:c1], {}))
        ctx.dma("pool", pairs, self.ring_sems[i], R=(), W=(tr,))
        return dst, tr

    def mark(self, name):
        self.marks.append((name, self.npe))

    def mm(self, out, pairs, R, W, start=True, stop=True):
        n = len(pairs)
        self.npe += n

        def f(e):
            last = None
            for i, (l, r) in enumerate(pairs):
                last = e.matmul(out, l, r, start=(start and i == 0), stop=(stop and i == n - 1))
            return last
        self.ctx.op("pe", f, None, R, W)

    def emit(self):
        ctx = self.ctx
        T = ctx.T
        ctx.dma("sp", [(self.CV[:, :], self.d_cv[:, :], {}), (self.CMF[:, :], self.d_cm[:, 0:CM_MB], {})],
                self.sem_c, W=(T("cv"), T("cmf")))
        pairs = [(self.CMB[:, c:c + 1024], self.d_cm[:, c:c + 1024], {}) for c in range(0, NCM, 1024)]
        ctx.dma("pool", pairs, self.sem_c2, W=(T("cmb"),))
        ctx.barrier(extra_sems=[self.sem_c, self.sem_c2])
        self.X = self.av(0, KC * S, F32, "p (k n) -> p k n", k=KC)
        for sq in range(self.nseq):
            ctx.dma("sp", [(self.X[:, kc, :], self.d_x[sq, :, kc, :], {}) for kc in range(KC)],
                    self.sem_x, W=[T("X", kc, t) for kc in range(KC) for t in range(NT)])
            for li in range(self.nl):
                if "ffn0" in self.phases:
                    self.ffn(li, 0)
                if "mix" in self.phases:
                    self.mixer(li)
                if "ffn1" in self.phases:
                    self.ffn(li, 1)
            if self.final_norm:
                self.final(sq)
            else:
                ctx.dma("sp", [(self.d_y[sq, :, kc, :], self.X[:, kc, :], {}) for kc in range(KC)],
                        self.sem_o, R=[T("X", kc, t) for kc in range(KC) for t in range(NT)])
            ctx.barrier(extra_sems=[self.sem_o, self.sem_x])

    def rmsnorm(self, gcol, dst, dst_tr):
        ctx = self.ctx
        T = ctx.T
        ones_b = self.CMB[:, CM_ONES:CM_ONES + 128]
        PN = self.PS[6]
        for t in range(NT):
            ts = slice(t * TT, (t + 1) * TT)
            for kc in range(KC):
                sl = (t * KC + kc) % 2
                ctx.op("act", "activation", dict(out=self.SQ[:, sl, :], in_=self.X[:, kc, ts], func=AF.Square),
                       R=(T("X", kc, t),), W=(T("SQ", sl),))
                self.mm(PN[:, :], [(ones_b, self.SQ[:, sl, :])], R=(T("SQ", sl), T("cmb")), W=(T("PS", 6),),
                        start=(kc == 0), stop=(kc == KC - 1))
            rs = t % 2
            ctx.op("act", "activation", dict(out=self.RSTD[:, rs, :], in_=PN[:, :], func=AF.Sqrt,
                                             bias=self.cvc(CV_EPS), scale=1.0 / D),
                   R=(T("PS", 6), T("cv")), W=(T("RSTD", rs),))
            ctx.op("dve", "reciprocal", dict(out=self.RSTD[:, rs, :], in_=self.RSTD[:, rs, :]),
                   R=(T("RSTD", rs),), W=(T("RSTD", rs),))
            for kc in range(KC):
                ctx.op("dve", "scalar_tensor_tensor", dict(
                    out=dst[:, kc, ts], in0=self.X[:, kc, ts], scalar=self.cvc(gcol + kc), in1=self.RSTD[:, rs, :],
                    op0=ALU.mult, op1=ALU.mult),
                    R=(T("X", kc, t), T("RSTD", rs), T("cv")), W=(dst_tr(kc, t),))

    def ffn(self, li, j):
        ctx = self.ctx
        T = ctx.T
        ctx.barrier()
        self.mark('ffn%d' % j)
        A = self.av(KC * S, FH * S // 2, BF16, "p (f n) -> p f n", f=FH)
        self.rmsnorm(CV_FFN + (li * 2 + j) * KC, self.H, lambda kc, t: T("H", t))
        k = 0
        for hf in range(2):
            for fl in range(FH):
                fc = hf * FH + fl
                wg, wgt = self.wload(self.d_wg[li, j, fc], KC * 128)
                wu, wut = self.wload(self.d_wu[li, j, fc], KC * 128)
                for t in range(NT):
                    ts = slice(t * TT, (t + 1) * TT)
                    b = k % 2
                    k += 1
                    PG, PU = self.PS[b], self.PS[2 + b]
                    self.mm(PG[:, :], [(wg[:, kc * 128:(kc + 1) * 128], self.H[:, kc, ts]) for kc in range(KC)],
                            R=(wgt, T("H", t)), W=(T("PS", b),))
                    self.mm(PU[:, :], [(wu[:, kc * 128:(kc + 1) * 128], self.H[:, kc, ts]) for kc in range(KC)],
                            R=(wut, T("H", t)), W=(T("PS", 2 + b),))
                    ctx.op("act", "activation", dict(out=self.TMPF[:, b, :], in_=PG[:, :], func=AF.Silu),
                           R=(T("PS", b),), W=(T("TMPF", b),))
                    ctx.op("dve", "tensor_tensor", dict(out=A[:, fl, ts], in0=self.TMPF[:, b, :], in1=PU[:, :],
                                                        op=ALU.mult),
                           R=(T("TMPF", b), T("PS", 2 + b)), W=(T("A", fl, t),))
            for dc in range(KC):
                wd, wdt = self.wload(self.d_wd[li, j, hf, dc], FH * 128)
                for t in range(NT):
                    ts = slice(t * TT, (t + 1) * TT)
                    b = k % 2
                    k += 1
                    PY = self.PS[4 + b]
                    self.mm(PY[:, :], [(wd[:, fl * 128:(fl + 1) * 128], A[:, fl, ts]) for fl in range(FH)],
                            R=[wdt] + [T("A", fl, t) for fl in range(FH)], W=(T("PS", 4 + b),))
                    ctx.op("dve", "scalar_tensor_tensor", dict(
                        out=self.X[:, dc, ts], in0=PY[:, :], scalar=0.5, in1=self.X[:, dc, ts],
                        op0=ALU.mult, op1=ALU.add),
                        R=(T("PS", 4 + b), T("X", dc, t)), W=(T("X", dc, t),))

    def mixer(self, li):
        ctx = self.ctx
        T = ctx.T
        ctx.barrier()
        self.mark('mixnorm')
        xtr = [T("X", kc, t) for kc in range(KC) for t in range(NT)]
        ctx.dma("sp", [(self.d_xsp[:, kc, :], self.X[:, kc, :], {}) for kc in range(KC)], self.sem_x, R=xtr)
        self.rmsnorm(CV_MIX + li * KC, self.H, lambda kc, t: T("H", t))
        ctx.barrier(extra_sems=[self.sem_x])
        self.MACC = self.av(self.ACOLS - 8192, 8192, BF16, "p (k n) -> p k n", k=KC)
        self.Y = self.av(self.ACOLS - 12288, 4096, BF16, "p (k n) -> p k n", k=4)
        order = self.mix_order
        first = True
        for n in order:
            self.mark('branch%d' % n)
            if n == 2:
                self.sb_branch(li)
            elif n == 0:
                self.pool_branch(li)
            else:
                self.dn_branch(li)
            if n == order[-1]:
                ctx.barrier()
                ctx.dma("sp", [(self.X[:, kc, :], self.d_xsp[:, kc, :], {}) for kc in range(KC)], self.sem_x, W=xtr)
            self.mark('merge%d' % n)
            self.merge(li, n, first)
            first = False
        self.mark('wout')
        k = 0
        for dc in range(KC):
            wo, wot = self.wload(self.d_wo[li, dc], KC * 128)
            for t in range(NT):
                ts = slice(t * TT, (t + 1) * TT)
                PY = self.PS[4 + k % 2]
                ptr = T("PS", 4 + k % 2)
                k += 1
                self.mm(PY[:, :], [(wo[:, kc * 128:(kc + 1) * 128], self.MACC[:, kc, ts]) for kc in range(KC)],
                        R=[wot] + [T("MACC", kc, t) for kc in range(KC)], W=(ptr,))
                ctx.op("dve", "tensor_tensor", dict(out=self.X[:, dc, ts], in0=PY[:, :], in1=self.X[:, dc, ts],
                                                    op=ALU.add),
                       R=(ptr, T("X", dc, t)), W=(T("X", dc, t),))

    def merge(self, li, n, first):
        ctx = self.ctx
        T = ctx.T
        k = 0
        for dc in range(KC):
            wgt, wgtt = self.wload(self.d_win[li, WIN_G + n * 8 + dc], KC * 128)
            wbr, wbrt = self.wload(self.d_wb[li, n, dc], 4 * 128)
            for t in range(NT):
                ts = slice(t * TT, (t + 1) * TT)
                b = k % 2
                k += 1
                PG, PB_ = self.PS[b], self.PS[2 + b]
                self.mm(PG[:, :], [(wgt[:, kc * 128:(kc + 1) * 128], self.H[:, kc, ts]) for kc in range(KC)],
                        R=(wgtt, T("H", t)), W=(T("PS", b),))
                ctx.op("act", "activation", dict(out=self.TMPF[:, b, :], in_=PG[:, :], func=AF.Sigmoid,
                                                 bias=self.cvc(CV_BG + li * 24 + n * 8 + dc), scale=1.0),
                       R=(T("PS", b), T("cv")), W=(T("TMPF", b),))
                self.mm(PB_[:, :], [(wbr[:, k4 * 128:(k4 + 1) * 128], self.Y[:, k4, ts]) for k4 in range(4)],
                        R=[wbrt] + [T("Y", k4, t) for k4 in range(4)], W=(T("PS", 2 + b),))
                if first:
                    ctx.op("dve", "tensor_tensor", dict(out=self.MACC[:, dc, ts], in0=self.TMPF[:, b, :],
                                                        in1=PB_[:, :], op=ALU.mult),
                           R=(T("TMPF", b), T("PS", 2 + b)), W=(T("MACC", dc, t),))
                else:
                    ctx.op("dve", "tensor_tensor", dict(out=self.RSTD[:, b, :], in0=self.TMPF[:, b, :],
                                                        in1=PB_[:, :], op=ALU.mult),
                           R=(T("TMPF", b), T("PS", 2 + b)), W=(T("RSTD", b),))
                    ctx.op("dve", "tensor_tensor", dict(out=self.MACC[:, dc, ts], in0=self.RSTD[:, b, :],
                                                        in1=self.MACC[:, dc, ts], op=ALU.add),
                           R=(T("RSTD", b), T("MACC", dc, t)), W=(T("MACC", dc, t),))

    def sb_branch(self, li):
        ctx = self.ctx
        T = ctx.T
        QT = self.av(0, 1024, BF16)
        KT = self.av(1024, 1024, BF16)
        VT = self.av(2048, 1024, BF16, "p (c d) -> p c d", c=16)
        SP = self.av(3072, 4096, BF16, "p (c t) -> p c t", c=16)
        ET = self.av(7168, 1024, F32, "p (b t) -> p b t", b=2)
        WT = self.av(8192, 512, BF16, "p (b t) -> p b t", b=2)
        SS = self.av(8704, 4096, BF16, "p (c t) -> p c t", c=16)
        ident_b = self.CMB[:, CM_ID:CM_ID + 128]
        negui = self.CMB[:, CM_NEGUI:CM_NEGUI + 128]
        negones = self.CMB[:, CM_NEGONES:CM_NEGONES + 128]
        cmb = T("cmb")
        ka = kb = ko = kp = 0
        for hd in range(NH):
            wq, wqt = self.wload(self.d_win[li, WIN_SQ + hd], KC * 128)
            wk, wkt = self.wload(self.d_win[li, WIN_SK + hd], KC * 128)
            wv, wvt = self.wload(self.d_win[li, WIN_SV + hd], KC * 128)
            for t in range(NT):
                ts = slice(t * TT, (t + 1) * TT)
                b = kp % 2
                kp += 1
                self.mm(self.PS[b][:, :], [(wq[:, kc * 128:(kc + 1) * 128], self.H[:, kc, ts]) for kc in range(KC)],
                        R=(wqt, T("H", t)), W=(T("PS", b),))
                ctx.op("dve", "tensor_scalar_mul", dict(out=QT[:, ts], in0=self.PS[b][:, :], scalar1=128.0 ** -0.5),
                       R=(T("PS", b),), W=(T("QT", t),))
                b = kp % 2
                kp += 1
                self.mm(self.PS[b][:, :], [(wk[:, kc * 128:(kc + 1) * 128], self.H[:, kc, ts]) for kc in range(KC)],
                        R=(wkt, T("H", t)), W=(T("PS", b),))
                ctx.op("dve", "tensor_copy", dict(out=KT[:, ts], in_=self.PS[b][:, :]),
                       R=(T("PS", b),), W=(T("KT", t),))
            for g4 in range(4):
                b = kp % 2
                kp += 1
                for cc in range(4):
                    c = g4 * 4 + cc
                    self.mm(self.PS[b][:, cc * 128:(cc + 1) * 128],
                            [(self.H[:, kc, c * 128:(c + 1) * 128], wv[:, kc * 128:(kc + 1) * 128]) for kc in range(KC)],
                            R=(wvt, T("H", g4)), W=(T("PS", b),))
                ctx.op("dve", "tensor_copy", dict(out=VT[:, g4 * 4:(g4 + 1) * 4, :],
                                                  in_=self.PS[b][:, :].rearrange("p (c d) -> p c d", c=4)),
                       R=(T("PS", b),), W=(T("VT", g4),))
            for I in range(4):
                n = 4 * (I + 1)
                qs = QT[:, I * TT:(I + 1) * TT]
                qtr = T("QT", I)
                for c in range(n - 1, -1, -1):
                    b = ka % 2
                    ka += 1
                    pairs = [(KT[:, c * 128:(c + 1) * 128], qs)]
                    if c >= 4 * I:
                        m = c - 4 * I
                        pairs.append((ident_b, self.CMB[:, CM_MB + m * 512:CM_MB + (m + 1) * 512]))
                    self.mm(self.PS[b][:, :], pairs, R=(T("KT", c // 4), qtr, cmb), W=(T("PS", b),))
                    ctx.op("act", "activation", dict(out=ET[:, b, :], in_=self.PS[b][:, :], func=AF.Exp),
                           R=(T("PS", b),), W=(T("ET", b),))
                    ctx.op("act", "activation", dict(out=SP[:, c, :], in_=ET[:, b, :], func=AF.Ln,
                                                     bias=self.cvc(CV_ONE), scale=1.0),
                           R=(T("ET", b), T("cv")), W=(T("SP", c),))
                    if c == n - 1:
                        pass
                    elif c >= 1:
                        prev = SP[:, c + 1, :] if c + 1 == n - 1 else SS[:, c + 1, :]
                        prevt = T("SP", c + 1) if c + 1 == n - 1 else T("SS", c + 1)
                        ctx.op("dve", "tensor_tensor", dict(out=SS[:, c, :], in0=prev, in1=SP[:, c, :], op=ALU.add),
                               R=(prevt, T("SP", c)), W=(T("SS", c),))
                po = 4 + ko % 2
                ko += 1

                def grp(c):
                    nonlocal kb
                    b = 2 + kb % 2
                    kb += 1
                    pairs = [(KT[:, c * 128:(c + 1) * 128], qs)]
                    if c >= 4 * I:
                        m = c - 4 * I
                        pairs.append((ident_b, self.CMB[:, CM_MB + m * 512:CM_MB + (m + 1) * 512]))
                    pairs.append((negui, SP[:, c, :]))
                    rd = [T("KT", c // 4), qtr, cmb, T("SP", c)]
                    if c + 1 <= n - 1:
                        if c + 1 == n - 1:
                            pairs.append((negones, SP[:, c + 1, :]))
                            rd.append(T("SP", c + 1))
                        else:
                            pairs.append((negones, SS[:, c + 1, :]))
                            rd.append(T("SS", c + 1))
                    self.mm(self.PS[b][:, :], pairs, R=rd, W=(T("PS", b),))
                    return b
                bnext = grp(0)
                for c in range(n):
                    b = bnext
                    wb_ = c % 2
                    ctx.op("act", "activation", dict(out=WT[:, wb_, :], in_=self.PS[b][:, :], func=AF.Exp),
                           R=(T("PS", b),), W=(T("WT", wb_),))
                    if c + 1 < n:
                        bnext = grp(c + 1)
                    self.mm(self.PS[po][:, :], [(VT[:, c, :], WT[:, wb_, :])],
                            R=(T("VT", c // 4), T("WT", wb_)), W=(T("PS", po),), start=(c == 0), stop=(c == n - 1))
                ctx.op("dve", "tensor_copy", dict(out=self.Y[:, hd, I * TT:(I + 1) * TT], in_=self.PS[po][:, :]),
                       R=(T("PS", po),), W=(T("Y", hd, I),))

    def pool_branch(self, li):
        ctx = self.ctx
        T = ctx.T
        PADW = 16
        UP = self.av(0, 2064, F32)
        PA = self.av(2064, 2064, F32)
        PBf = self.av(4128, 2064, F32)
        POOLED = self.av(6192, 1024, BF16)
        FIX = self.av(7216, 16, F32)
        for nm, buf in (("UP", UP), ("PA", PA), ("PBf", PBf)):
            ctx.op("dve", "memset", dict(ap=buf[:, 0:PADW], constant=0.0), W=(T(nm),))
        kp = 0
        for g, win in enumerate(POOL_WINDOWS):
            wu, wut = self.wload(self.d_win[li, WIN_POOL + g], KC * 128)
            wp, wpt = self.wload(self.d_wpool[li], 4 * 128)
            for t in range(NT):
                ts = slice(t * TT, (t + 1) * TT)
                b = kp % 2
                kp += 1
                self.mm(self.PS[b][:, :], [(wu[:, kc * 128:(kc + 1) * 128], self.H[:, kc, ts]) for kc in range(KC)],
                        R=(wut, T("H", t)), W=(T("PS", b),))
                ctx.op("dve", "tensor_copy", dict(out=UP[:, PADW + t * TT:PADW + (t + 1) * TT], in_=self.PS[b][:, :]),
                       R=(T("PS", b),), W=(T("UP"),))
            cur, curn = UP, "UP"
            bufs = [(PA, "PA"), (PBf, "PBf")]
            for lev in range(g + 1):
                sh = 2 ** lev
                nxt, nxtn = bufs[lev % 2]
                ctx.op("dve", "tensor_tensor", dict(out=nxt[:, PADW:PADW + S], in0=cur[:, PADW:PADW + S],
                                                    in1=cur[:, PADW - sh:PADW - sh + S], op=ALU.add),
                       R=(T(curn),), W=(T(nxtn),))
                cur, curn = nxt, nxtn
            ctx.op("dve", "scalar_tensor_tensor", dict(out=POOLED[:, :], in0=cur[:, PADW:PADW + S], scalar=1.0 / win,
                                                       in1=UP[:, PADW:PADW + S], op0=ALU.mult, op1=ALU.subtract),
                   R=(T(curn), T("UP")), W=(T("POOLED"),))
            ctx.op("dve", "tensor_tensor", dict(out=FIX[:, :], in0=cur[:, PADW:PADW + 16],
                                                in1=self.cvc(CV_INVC + g * 16, 16), op=ALU.mult),
                   R=(T(curn), T("cv")), W=(T("FIX"),))
            ctx.op("dve", "tensor_tensor", dict(out=POOLED[:, 0:16], in0=FIX[:, :], in1=UP[:, PADW:PADW + 16],
                                                op=ALU.subtract),
                   R=(T("FIX"), T("UP")), W=(T("POOLED"),))
            for t in range(NT):
                ts = slice(t * TT, (t + 1) * TT)
                b = kp % 2
                kp += 1
                self.mm(self.PS[b][:, :], [(wp[:, g * 128:(g + 1) * 128], POOLED[:, ts])],
                        R=(wpt, T("POOLED")), W=(T("PS", b),))
                ctx.op("dve", "tensor_scalar_mul", dict(out=self.Y[:, g, ts], in0=self.PS[b][:, :],
                                                        scalar1=self.cvc(CV_PS + li * 4 + g)),
                       R=(T("PS", b), T("cv")), W=(T("Y", g, t),))

    def dn_branch(self, li):
        ctx = self.ctx
        T = ctx.T
        o = 0

        def take(n):
            nonlocal o
            r = o
            o += n
            return r
        QIN = self.av(take(2056), 2056, F32)
        OT = QIN[:, 8:8 + S]
        QN = self.av(take(2048), 2048, F32)
        KN = self.av(take(2048), 2048, F32)
        VN = self.av(take(2048), 2048, F32)
        KDEC = KN.rearrange("p (c d) -> p c d", c=16)
        ATT = VN.rearrange("p (c d) -> p c d", c=16)
        NCH = 8
        ZS = self.av(take(1024), 1024, BF16)
        KBG = self.av(take(2048), 2048, F32, "p (c d) -> p c d", c=16)
        WTf = KBG.rearrange("p c d -> p (c d)")
        BV = self.av(take(2048), 2048, F32, "p (c d) -> p c d", c=16)
        SQ2 = self.av(take(1024), 1024, BF16)
        IB = [[self.av(take(128), 128, F32) for _ in range(5)] for _ in range(NCH)]
        ABT = self.av(take(128), 128, F32, "p (c e) -> p c e", c=16)
        G = self.av(take(64), 64, F32, "p (h c) -> p h c", h=4)
        BETA = self.av(take(64), 64, F32, "p (h c) -> p h c", h=4)
        EXPS = self.av(take(256), 256, F32, "p (k h c) -> p k h c", k=4, h=4)
        EGC, EREM, CDA, CDB = (EXPS[:, i] for i in range(4))
        BEG = self.av(take(64), 64, F32, "p (h c) -> p h c", h=4)
        EA = self.av(take(4), 4, F32)
        E1 = self.av(take(32), 32, F32)
        ST = self.av(take(128), 128, F32)
        VNEW = self.av(take(128), 128, F32)
        RS2 = self.av(take(512), 512, F32)
        assert o <= self.ACOLS - 12288, o
        Rr = lambda ap: ap
        IDF = self.CMF[:, CM_ID:CM_ID + 128]
        UPI = self.CMF[:, CM_UPI:CM_UPI + 128]
        LOS = self.CMF[:, CM_LOS:CM_LOS + 128]
        IDR = self.CMR[:, CM_ID:CM_ID + 128]
        ONESR = self.CMR[:, CM_ONES:CM_ONES + 128]
        UPIR = self.CMR[:, CM_UPI:CM_UPI + 128]
        INDS = [self.CMR[:, c:c + 128] for c in (CM_UPI, CM_LOS, CM_INDA, CM_INDB)]
        ones_b = self.CMB[:, CM_ONES:CM_ONES + 128]
        cmf, cv = T("cmf"), T("cv")
        P7 = self.PS[7]

        wab, wabt = self.wload(self.d_wab[li], KC * 8)
        for c in range(16):
            self.mm(P7[:, c * 8:(c + 1) * 8],
                    [(self.H[:, kc, c * 128:(c + 1) * 128], wab[:, kc * 8:(kc + 1) * 8]) for kc in range(KC)],
                    R=(wabt, T("H", c // 4)), W=(T("PS", 7),))
        ctx.op("dve", "tensor_copy", dict(out=ABT[:, :, :], in_=P7[:, 0:128].rearrange("p (c e) -> p c e", c=16)),
               R=(T("PS", 7),), W=(T("ABT"),))
        ctx.op("act", "activation", dict(out=EA[:, :], in_=self.cvc(CV_AL + li * 4, 4), func=AF.Exp),
               R=(cv,), W=(T("EA"),))
        for hd in range(NH):
            ctx.op("act", "activation", dict(out=E1[:, 0:16], in_=ABT[:, :, hd], func=AF.Exp,
                                             bias=self.cvc(CV_DT + li * 4 + hd), scale=1.0),
                   R=(T("ABT"), cv), W=(T("E1"),))
            ctx.op("act", "activation", dict(out=E1[:, 0:16], in_=E1[:, 0:16], func=AF.Ln,
                                             bias=self.cvc(CV_ONE), scale=1.0),
                   R=(T("E1"), cv), W=(T("E1"),))
            ctx.op("dve", "tensor_scalar", dict(out=Rr(G[:, hd, :]), in0=E1[:, 0:16], scalar1=EA[:, hd:hd + 1], scalar2=-1.0,
                                                op0=ALU.mult, op1=ALU.mult),
                   R=(T("E1"), T("EA")), W=(T("G"),))
            ctx.op("act", "activation", dict(out=E1[:, 16:32], in_=ABT[:, :, 4 + hd], func=AF.Exp, scale=-1.0),
                   R=(T("ABT"),), W=(T("E1b"),))
            ctx.op("dve", "tensor_scalar_add", dict(out=E1[:, 16:32], in0=E1[:, 16:32], scalar1=1.0),
                   R=(T("E1b"),), W=(T("E1b"),))
            ctx.op("dve", "reciprocal", dict(out=BETA[:, hd, :], in_=E1[:, 16:32]),
                   R=(T("E1b"),), W=(T("BETA"),))
        Gf = G.rearrange("p h c -> p (h c)")
        for i, M in enumerate(INDS):
            self.mm(P7[:, 128 + i * 64:128 + (i + 1) * 64], [(M, Rr(Gf))], R=(T("G"), cmf), W=(T("PS", 7),))
        ctx.op("act", "activation", dict(out=EXPS.rearrange("p k h c -> p (k h c)"), in_=P7[:, 128:384], func=AF.Exp),
               R=(T("PS", 7),), W=(T("EXPS"),))
        ctx.op("dve", "tensor_tensor", dict(out=BEG[:, :, :], in0=BETA[:, :, :], in1=EGC, op=ALU.mult),
               R=(T("BETA"), T("EXPS")), W=(T("BEG"),))

        kp = 0
        if self.dn_stop == 0:
            return
        for hd in range(NH):
            self.mark('dnA%d' % hd)
            for nm, dst, widx, cidx in (("QN", QN, WIN_DQ + hd, hd), ("KN", KN, WIN_DK + hd, 4 + hd),
                                        ("VN", VN, WIN_DV + hd, 8 + hd)):
                w, wt = self.wload(self.d_win[li, widx], KC * 128)
                ctx.op("dve", "memset", dict(ap=QIN[:, 0:8], constant=0.0), W=(T("QIN"),))
                for t in range(NT):
                    ts = slice(t * TT, (t + 1) * TT)
                    b = kp % 2
                    kp += 1
                    self.mm(self.PS[b][:, :], [(w[:, kc * 128:(kc + 1) * 128], self.H[:, kc, ts]) for kc in range(KC)],
                            R=(wt, T("H", t)), W=(T("PS", b),))
                    ctx.op("act", "activation", dict(out=QIN[:, 8 + t * TT:8 + (t + 1) * TT], in_=self.PS[b][:, :], func=AF.Copy),
                           R=(T("PS", b),), W=(T("QIN"),))
                dtr = [T(nm, p) for p in range(16)]
                cbase = CV_CONV + li * 48 + cidx
                ctx.op("act", "activation", dict(out=Rr(dst[:, :]), in_=QIN[:, 5:5 + S], func=AF.Copy, scale=self.cvc(cbase)),
                       R=(T("QIN"), cv), W=dtr)
                for k in range(1, 4):
                    ctx.op("dve", "scalar_tensor_tensor", dict(out=Rr(dst[:, :]), in0=QIN[:, 5 + k:5 + k + S],
                                                               scalar=self.cvc(cbase + 12 * k), in1=dst[:, :],
                                                               op0=ALU.mult, op1=ALU.add),
                           R=[T("QIN"), cv] + dtr, W=dtr)
                ctx.op("act", "activation", dict(out=Rr(dst[:, :]), in_=dst[:, :], func=AF.Silu), R=dtr, W=dtr)
                if nm != "VN":
                    ctx.op("act", "activation", dict(out=SQ2[:, :], in_=dst[:, :], func=AF.Square), R=dtr, W=(T("SQ2"),))
                    for t in range(NT):
                        ts = slice(t * TT, (t + 1) * TT)
                        b = kp % 2
                        kp += 1
                        ptr = [T(nm, p) for p in range(4 * t, 4 * t + 4)]
                        self.mm(self.PS[b][:, :], [(ones_b, SQ2[:, ts])], R=(T("SQ2"), T("cmb")), W=(T("PS", b),))
                        ctx.op("act", "activation", dict(out=RS2[:, :], in_=self.PS[b][:, :], func=AF.Sqrt,
                                                         bias=self.cvc(CV_EPS), scale=1.0),
                               R=(T("PS", b), cv), W=(T("RS2"),))
                        ctx.op("dve", "reciprocal", dict(out=RS2[:, :], in_=RS2[:, :]), R=(T("RS2"),), W=(T("RS2"),))
                        if nm == "QN":
                            ctx.op("dve", "scalar_tensor_tensor", dict(out=Rr(dst[:, ts]), in0=dst[:, ts], scalar=128.0 ** -0.5,
                                                                       in1=RS2[:, :], op0=ALU.mult, op1=ALU.mult),
                                   R=[T("RS2")] + ptr, W=ptr)
                        else:
                            ctx.op("dve", "tensor_tensor", dict(out=Rr(dst[:, ts]), in0=dst[:, ts], in1=RS2[:, :], op=ALU.mult),
                                   R=[T("RS2")] + ptr, W=ptr)
            w, wt = self.wload(self.d_win[li, WIN_DZ + hd], KC * 128)
            for t in range(NT):
                ts = slice(t * TT, (t + 1) * TT)
                b = kp % 2
                kp += 1
                self.mm(self.PS[b][:, :], [(w[:, kc * 128:(kc + 1) * 128], self.H[:, kc, ts]) for kc in range(KC)],
                        R=(wt, T("H", t)), W=(T("PS", b),))
                ctx.op("act", "activation", dict(out=ZS[:, ts], in_=self.PS[b][:, :], func=AF.Silu),
                       R=(T("PS", b),), W=(T("ZS", t),))

            if self.dn_stop == 1:
                continue
            self.mark('dnB%d' % hd)
            def chain(ci, p):
                cs = slice(p * 128, (p + 1) * 128)
                B0, B1, B2, B3, B4 = IB[ci]
                bt = [T("IB", ci, i) for i in range(5)]
                Pc = self.PS[ci]
                pt = T("PS", ci)

                def q(i):
                    return Pc[:, i * 128:(i + 1) * 128]
                kn, qn, vn = T("KN", p), T("QN", p), T("VN", p)
                self.mm(q(0), [(Rr(KN[:, cs]), IDR)], R=(kn, cmf), W=(pt,))
                self.mm(q(1), [(Rr(VN[:, cs]), IDR)], R=(vn, cmf), W=(pt,))
                self.mm(q(2), [(Rr(KN[:, cs]), Rr(KN[:, cs]))], R=(kn,), W=(pt,))
                self.mm(q(3), [(Rr(KN[:, cs]), Rr(QN[:, cs]))], R=(kn, qn), W=(pt,))
                ctx.op("dve", "tensor_scalar_mul", dict(out=Rr(B0), in0=LOS, scalar1=G[:, hd, p:p + 1]),
                       R=(T("G"), cmf), W=(bt[0],))
                yield
                ctx.op("dve", "tensor_scalar_mul", dict(out=Rr(KBG[:, p, :]), in0=q(0), scalar1=BEG[:, hd, p:p + 1]),
                       R=(pt, T("BEG")), W=(T("KBG", p),))
                ctx.op("dve", "tensor_scalar_mul", dict(out=Rr(KDEC[:, p, :]), in0=q(0), scalar1=EREM[:, hd, p:p + 1]),
                       R=(pt, T("EXPS")), W=(kn,))
                ctx.op("dve", "tensor_scalar_mul", dict(out=Rr(BV[:, p, :]), in0=q(1), scalar1=BETA[:, hd, p:p + 1]),
                       R=(pt, T("BETA")), W=(T("BV", p),))
                self.mm(q(0), [(UPIR, Rr(B0))], R=(bt[0], cmf), W=(pt,))
                self.mm(q(1), [(Rr(B0), UPIR)], R=(bt[0], cmf), W=(pt,))
                yield
                ctx.op("act", "activation", dict(out=B1, in_=q(0), func=AF.Exp), R=(pt,), W=(bt[1],))
                ctx.op("act", "activation", dict(out=B2, in_=q(1), func=AF.Exp), R=(pt,), W=(bt[2],))
                yield
                ctx.op("dve", "scalar_tensor_tensor", dict(out=Rr(B3), in0=B1, scalar=BETA[:, hd, p:p + 1], in1=LOS,
                                                           op0=ALU.mult, op1=ALU.mult),
                       R=(bt[1], T("BETA"), cmf), W=(bt[3],))
                ctx.op("dve", "tensor_tensor", dict(out=Rr(B3), in0=B3, in1=q(2), op=ALU.mult),
                       R=(bt[3], pt), W=(bt[3],))
                ctx.op("dve", "tensor_tensor", dict(out=B2, in0=B2, in1=UPI, op=ALU.mult),
                       R=(bt[2], cmf), W=(bt[2],))
                ctx.op("dve", "tensor_tensor", dict(out=Rr(ATT[:, p, :]), in0=B2, in1=q(3), op=ALU.mult),
                       R=(bt[2], pt), W=(vn,))
                yield
                self.mm(q(0), [(Rr(B3), IDR)], R=(bt[3], cmf), W=(pt,))
                yield
                ctx.op("act", "activation", dict(out=Rr(B4), in_=q(0), func=AF.Copy), R=(pt,), W=(bt[4],))
                ctx.op("dve", "tensor_tensor", dict(out=Rr(B2), in0=IDF, in1=q(0), op=ALU.subtract),
                       R=(pt, cmf, vn), W=(bt[2],))
                yield
                Lc, Uc, Ln, Un = 3, 4, 1, 0
                NLEV = 6 if DNC == 128 else 5
                for lev in range(NLEV):
                    last = lev == NLEV - 1
                    if not last:
                        self.mm(q(0), [(Rr(IB[ci][Lc]), Rr(IB[ci][Uc]))], R=(bt[Lc], bt[Uc]), W=(pt,))
                    self.mm(q(1), [(Rr(IB[ci][Uc]), Rr(IB[ci][Lc]))], R=(bt[Lc], bt[Uc]), W=(pt,))
                    yield
                    if not last:
                        ctx.op("act", "activation", dict(out=Rr(IB[ci][Un]), in_=q(0), func=AF.Copy),
                               R=(pt,), W=(bt[Un],))
                    ctx.op("dve", "tensor_copy", dict(out=Rr(IB[ci][Ln]), in_=q(1)), R=(pt,), W=(bt[Ln],))
                    yield
                    self.mm(q(2), [(Rr(IB[ci][Ln]), Rr(B2))], R=(bt[Ln], bt[2]), W=(pt,))
                    yield
                    ctx.op("dve", "tensor_tensor", dict(out=Rr(B2), in0=B2, in1=q(2), op=ALU.add),
                           R=(bt[2], pt), W=(bt[2],))
                    yield
                    Lc, Ln = Ln, Lc
                    Uc, Un = Un, Uc
                self.mm(q(0), [(Rr(B2), Rr(BV[:, p, :]))], R=(bt[2], T("BV", p)), W=(pt,))
                self.mm(q(1), [(Rr(KBG[:, p, :]), Rr(B2))], R=(bt[2], T("KBG", p)), W=(pt,))
                ctx.op("dve", "tensor_scalar_mul", dict(out=Rr(B0), in0=IDF, scalar1=EGC[:, hd, p:p + 1]),
                       R=(T("EXPS"), cmf, bt[0]), W=(bt[0],))
                self.mm(q(2), [(ONESR, Rr(B0))], R=(bt[0], cmf), W=(pt,))
                yield
                ctx.op("dve", "tensor_copy", dict(out=Rr(BV[:, p, :]), in_=q(0)), R=(pt,), W=(T("BV", p),))
                ctx.op("act", "activation", dict(out=Rr(WTf[:, cs]), in_=q(1), func=AF.Copy),
                       R=(pt,), W=(T("KBG", p),))
                ctx.op("dve", "tensor_tensor", dict(out=Rr(QN[:, cs]), in0=QN[:, cs], in1=q(2), op=ALU.mult),
                       R=(qn, pt), W=(qn,))
                yield

            for g4 in range(16 // NCH):
                gens = [chain(ci, g4 * NCH + ci) for ci in range(NCH)]
                alive = True
                rounds = 0
                while alive and rounds < self.dn_bsteps:
                    rounds += 1
                    alive = False
                    for gen in gens:
                        try:
                            next(gen)
                            alive = True
                        except StopIteration:
                            pass

            if self.dn_stop == 2:
                continue
            self.mark('dnC%d' % hd)
            ctx.op("dve", "memset", dict(ap=Rr(ST[:, :]), constant=0.0), W=(T("ST"),))
            NSTEP = S // DNC
            PERB = 512 // DNC
            for n in range(NSTEP):
                if DNC == 128:
                    p, hf = n, 0
                    r = slice(0, 128)
                else:
                    p, hf = n // 2, n % 2
                    r = slice(hf * 64, hf * 64 + 64)
                a = n % 2
                PVb, PSb = self.PS[a], self.PS[2 + a]
                po = 4 + (n // PERB) % 2
                self.mm(PVb[:, 0:128], [(Rr(WTf[:, p * 128:(p + 1) * 128]), Rr(ST[:, :]))], R=(T("KBG", p), T("ST")),
                        W=(T("PS", a),))
                oc = slice((n % PERB) * DNC, (n % PERB) * DNC + DNC)
                self.mm(self.PS[po][:, oc], [(Rr(ST[:, :]), Rr(QN[:, n * DNC:(n + 1) * DNC]))],
                        R=(T("ST"), T("QN", p)), W=(T("PS", po),), start=True, stop=False)
                ctx.op("dve", "tensor_tensor", dict(out=Rr(VNEW[r, :]), in0=BV[r, p, :], in1=PVb[r, 0:128], op=ALU.subtract),
                       R=(T("BV", p), T("PS", a)), W=(T("VNEW"),))
                self.mm(PSb[:, 0:128], [(Rr(KDEC[r, p, :]), Rr(VNEW[r, :]))], R=(T("KN", p), T("VNEW")), W=(T("PS", 2 + a),))
                self.mm(self.PS[po][:, oc], [(Rr(VNEW[r, :]), Rr(ATT[r, p, r]))],
                        R=(T("VNEW"), T("VN", p)), W=(T("PS", po),), start=False, stop=True)
                cd = (CDA if hf == 0 else CDB)[:, hd, p:p + 1]
                ctx.op("dve", "scalar_tensor_tensor", dict(out=Rr(ST[:, :]), in0=ST[:, :], scalar=cd, in1=PSb[:, 0:128],
                                                           op0=ALU.mult, op1=ALU.add),
                       R=(T("ST"), T("EXPS"), T("PS", 2 + a)), W=(T("ST"),))
                if n % PERB == PERB - 1:
                    t = n // PERB
                    ctx.op("dve", "tensor_copy", dict(out=OT[:, t * TT:(t + 1) * TT], in_=self.PS[po][:, :]),
                           R=(T("PS", po),), W=(T("QIN"),))
            self.mark('dnN%d' % hd)
            ctx.op("act", "activation", dict(out=SQ2[:, :], in_=OT, func=AF.Square), R=(T("QIN"),), W=(T("SQ2"),))
            for t in range(NT):
                ts = slice(t * TT, (t + 1) * TT)
                b = kp % 2
                kp += 1
                self.mm(self.PS[b][:, :], [(ones_b, SQ2[:, ts])], R=(T("SQ2"), T("cmb")), W=(T("PS", b),))
                ctx.op("act", "activation", dict(out=RS2[:, :], in_=self.PS[b][:, :], func=AF.Sqrt,
                                                 bias=self.cvc(CV_EPS), scale=1.0 / 128),
                       R=(T("PS", b), cv), W=(T("RS2"),))
                ctx.op("dve", "reciprocal", dict(out=RS2[:, :], in_=RS2[:, :]), R=(T("RS2"),), W=(T("RS2"),))
                ctx.op("dve", "scalar_tensor_tensor", dict(out=self.TMPF[:, b, :], in0=OT[:, ts], scalar=self.cvc(CV_ON + li),
                                                           in1=RS2[:, :], op0=ALU.mult, op1=ALU.mult),
                       R=(T("QIN"), T("RS2"), cv), W=(T("TMPF", b),))
                ctx.op("dve", "tensor_tensor", dict(out=self.Y[:, hd, ts], in0=self.TMPF[:, b, :], in1=ZS[:, ts], op=ALU.mult),
                       R=(T("TMPF", b), T("ZS", t)), W=(T("Y", hd, t),))


    def final(self, sq):
        ctx = self.ctx
        T = ctx.T
        ctx.barrier()
        self.mark('final')
        OUT = self.av(KC * S, KC * S, F32, "p (k n) -> p k n", k=KC)
        self.rmsnorm(CV_FIN, OUT, lambda kc, t: T("OUT", kc, t))
        ctx.dma("sp", [(self.d_y[sq, :, kc, :], OUT[:, kc, :], {}) for kc in range(KC)],
                self.sem_o, R=[T("OUT", kc, t) for kc in range(KC) for t in range(NT)])


def _prep_core_inputs(inp, seqs, layers):
    x = inp["x"]
    xT = np.stack([np.ascontiguousarray(x[b].T.reshape(KC, 128, S).transpose(1, 0, 2)) for b in seqs])
    m = {"xT": xT, "cvec": _host_cvec(inp), "cmat": _host_consts()}
    return m


def _unlayout(yT):
    return np.ascontiguousarray(yT.transpose(1, 0, 2).reshape(D, S).T)


_CACHE = {}


def kernel(**inputs):
    inp = {k: np.asarray(v, dtype=np.float32) for k, v in inputs.items()}
    B = inp["x"].shape[0]
    prog = Prog(SEQ_PER_CORE, range(L))
    nc = prog.build()
    wts = _host_weights(inp, range(L))
    cv, cm = _host_cvec(inp), _host_consts()
    in_maps = []
    for c in range(NCORES):
        seqs = [c * SEQ_PER_CORE + i for i in range(SEQ_PER_CORE)]
        m = _prep_core_inputs(inp, seqs, range(L))
        m.update(wts)
        in_maps.append(m)
    res = run_bass_kernel_spmd(nc, in_maps, core_ids=list(range(NCORES)))
    out = np.empty((B, S, D), np.float32)
    for c in range(NCORES):
        yT = res.results[c]["yT"]
        for i in range(SEQ_PER_CORE):
            out[c * SEQ_PER_CORE + i] = _unlayout(yT[i])
    return out
```

```python
import numpy as np
from contextlib import ExitStack

import concourse.bass as bass
import concourse.mybir as mybir
from concourse.bass_utils import run_bass_kernel_spmd

F32 = mybir.dt.float32
F32R = mybir.dt.float32r
BF16 = mybir.dt.bfloat16
AF = mybir.ActivationFunctionType
ALU = mybir.AluOpType

D = 1024
KC = 8
S = 2048
TT = 512
NT = S // TT
FF = 2816
FC = 22
FH = 11
L = 4
NH = 4
EPS = 1e-6
NCORES = 8
SEQ_PER_CORE = 2
POOL_WINDOWS = (2, 4, 8, 16)
NEG = -30000.0
DNC = 128

CV_FFN = 0
CV_MIX = CV_FFN + L * 2 * KC
CV_FIN = CV_MIX + L * KC
CV_BG = CV_FIN + KC
CV_PS = CV_BG + L * 24
CV_CONV = CV_PS + L * 4
CV_ON = CV_CONV + L * 48
CV_AL = CV_ON + L
CV_DT = CV_AL + L * 4
CV_INVC = CV_DT + L * 4
CV_EPS = CV_INVC + 64
CV_ONE = CV_EPS + 1
NCV = CV_ONE + 1

CM_ID = 0
CM_ONES = 128
CM_NEGONES = 256
CM_NEGUI = 384
CM_UPI = 512
CM_LOS = 640
CM_INDA = 768
CM_INDB = 896
CM_MB = 1024
NCM = CM_MB + 4 * 512


def _host_consts():
    cm = np.zeros((128, NCM), np.float32)
    idx = np.arange(128)
    cm[:, CM_ID:CM_ID + 128] = np.eye(128, dtype=np.float32)
    cm[:, CM_ONES:CM_ONES + 128] = 1.0
    cm[:, CM_NEGONES:CM_NEGONES + 128] = -1.0
    cm[:, CM_NEGUI:CM_NEGUI + 128] = -(idx[:, None] >= idx[None, :]).astype(np.float32)
    same = (idx[:, None] // DNC) == (idx[None, :] // DNC)
    cm[:, CM_UPI:CM_UPI + 128] = (same & (idx[:, None] <= idx[None, :])).astype(np.float32)
    cm[:, CM_LOS:CM_LOS + 128] = (same & (idx[:, None] > idx[None, :])).astype(np.float32)
    cm[:, CM_INDA:CM_INDA + 128] = (idx[:, None] < DNC).astype(np.float32)
    cm[:, CM_INDB:CM_INDB + 128] = (idx[:, None] >= 64).astype(np.float32)
    t = np.arange(512)
    for m in range(4):
        ok = (m * 128 + idx[:, None]) < t[None, :]
        cm[:, CM_MB + m * 512:CM_MB + (m + 1) * 512] = np.where(ok, 0.0, NEG).astype(np.float32)
    return cm


def _fm(v):
    n = v.shape[-1] // 128
    return np.ascontiguousarray(v.reshape(n, 128).T)


def _host_cvec(inp):
    cv = np.zeros((128, NCV), np.float32)
    for l in range(L):
        for j in range(2):
            c = CV_FFN + (l * 2 + j) * KC
            cv[:, c:c + KC] = _fm(inp["ffn_norm"][l, j])
        cv[:, CV_MIX + l * KC:CV_MIX + (l + 1) * KC] = _fm(inp["mix_norm"][l])
        cv[:, CV_BG + l * 24:CV_BG + (l + 1) * 24] = _fm(inp["b_gate"][l])
        cv[:, CV_PS + l * 4:CV_PS + (l + 1) * 4] = _fm(inp["pool_scale"][l])
        for k in range(4):
            c = CV_CONV + l * 48 + k * 12
            cv[:, c:c + 12] = _fm(inp["dn_conv"][l, k])
        cv[:, CV_ON + l] = inp["dn_out_norm"][l]
        cv[:, CV_AL + l * 4:CV_AL + (l + 1) * 4] = np.broadcast_to(inp["dn_A_log"][l], (128, 4))
        cv[:, CV_DT + l * 4:CV_DT + (l + 1) * 4] = np.broadcast_to(inp["dn_dt_bias"][l], (128, 4))
    cv[:, CV_FIN:CV_FIN + KC] = _fm(inp["final_norm"])
    cv[:, CV_EPS] = EPS
    cv[:, CV_ONE] = 1.0
    for g, w in enumerate(POOL_WINDOWS):
        cnt = np.minimum(np.arange(1, 17), w).astype(np.float32)
        cv[:, CV_INVC + g * 16:CV_INVC + (g + 1) * 16] = (np.float32(1.0) / cnt)[None, :]
    return cv


def _unit_cols(w, c0, ncol=128):
    nk = w.shape[0] // 128
    blk = w[:, c0:c0 + ncol].reshape(nk, 128, ncol).transpose(1, 0, 2)
    return blk.reshape(128, nk * ncol)


WIN_POOL, WIN_DQ, WIN_DK, WIN_DV, WIN_DZ, WIN_SQ, WIN_SK, WIN_SV, WIN_G = 0, 4, 8, 12, 16, 20, 24, 28, 32
_WIN_COLS = ([i * 128 for i in range(20)] + [2568 + i * 128 for i in range(12)]
             + [4104 + i * 128 for i in range(24)])


def _host_weights(inp, layers):
    nl = len(layers)
    out = {}
    wg = np.empty((nl, 2, FC, 128, KC * 128), np.float32)
    wu = np.empty((nl, 2, FC, 128, KC * 128), np.float32)
    wd = np.empty((nl, 2, 2, KC, 128, FH * 128), np.float32)
    win = np.empty((nl, 56, 128, KC * 128), np.float32)
    wab = np.empty((nl, 128, KC * 8), np.float32)
    wb = np.empty((nl, 3, KC, 128, 4 * 128), np.float32)
    wo = np.empty((nl, KC, 128, KC * 128), np.float32)
    wpool = np.empty((nl, 128, 4 * 128), np.float32)
    for li, l in enumerate(layers):
        for j in range(2):
            G = inp["ffn_w_gate"][l, j]
            U = inp["ffn_w_up"][l, j]
            Dn = inp["ffn_w_down"][l, j]
            for fc in range(FC):
                wg[li, j, fc] = _unit_cols(G, fc * 128)
                wu[li, j, fc] = _unit_cols(U, fc * 128)
            for hf in range(2):
                sub = Dn[hf * FH * 128:(hf + 1) * FH * 128]
                for dc in range(KC):
                    wd[li, j, hf, dc] = _unit_cols(sub, dc * 128)
        W = inp["w_in"][l]
        for ci, c0 in enumerate(_WIN_COLS):
            win[li, ci] = _unit_cols(W, c0)
        wab[li] = _unit_cols(W, 2560, 8)
        for n in range(3):
            for dc in range(KC):
                wb[li, n, dc] = _unit_cols(inp["w_branch"][l, n], dc * 128)
        for dc in range(KC):
            wo[li, dc] = _unit_cols(inp["w_out"][l], dc * 128)
        wpool[li] = inp["pool_w"][l].transpose(1, 0, 2).reshape(128, 4 * 128)
    out.update(wg=wg, wu=wu, wd=wd, win=win, wab=wab, wb=wb, wo=wo, wpool=wpool)
    return out


class Sem:
    def __init__(self, h, name):
        self.h = h
        self.name = name
        self.count = 0


class Track:
    __slots__ = ("w", "rs", "excl")

    def __init__(self, excl=False):
        self.w = None
        self.rs = {}
        self.excl = excl


class Eng:
    def __init__(self, name, sem):
        self.name = name
        self.sem = sem
        self.known = {}
        self.q = []


class Ctx:
    def __init__(self, nc, es):
        self.nc = nc
        self.es = es
        self.eng = {}
        self.tracks = {}
        self.nops = 0

    def new_sem(self, name):
        return Sem(self.es.enter_context(self.nc.semaphore(name)), name)

    def add_engine(self, name):
        self.eng[name] = Eng(name, self.new_sem("s_" + name))

    def T(self, *key):
        t = self.tracks.get(key)
        if t is None:
            t = Track(excl=(key[0] == "PS"))
            self.tracks[key] = t
        return t

    @staticmethod
    def _split(R, W):
        ex = [t for t in R if t.excl]
        if ex:
            return [t for t in R if not t.excl], list(W) + ex
        return R, W

    def _waits(self, E, R, W):
        need = {}
        for t in R:
            if t.w is not None:
                s, v = t.w
                if need.get(s, 0) < v:
                    need[s] = v
        for t in W:
            if t.w is not None:
                s, v = t.w
                if need.get(s, 0) < v:
                    need[s] = v
            for s, v in t.rs.items():
                if need.get(s, 0) < v:
                    need[s] = v
        for s, v in need.items():
            if s is E.sem and E.name == "pe":
                continue
            if E.known.get(s, 0) >= v:
                continue
            E.q.append(("w", s, v))
            E.known[s] = v

    def _mark(self, tok, R, W):
        for t in R:
            t.rs[tok[0]] = tok[1]
        for t in W:
            t.w = tok
            t.rs = {}

    def op(self, en, meth, kw, R=(), W=()):
        E = self.eng[en]
        R, W = self._split(R, W)
        self._waits(E, R, W)
        if callable(meth):
            fn = meth
        else:
            fn = (lambda e, m=meth, k=dict(kw): getattr(e, m)(**k))
        E.q.append(("i", fn, E.sem, 1))
        E.sem.count += 1
        self._mark((E.sem, E.sem.count), R, W)
        self.nops += 1

    def dma(self, en, pairs, dsem, R=(), W=()):
        E = self.eng[en]
        self._waits(E, R, W)
        for o, i, kw in pairs:
            E.q.append(("i", (lambda e, o=o, i=i, kw=kw: e.dma_start(out=o, in_=i, **kw)), dsem, 16))
            dsem.count += 16
        self._mark((dsem, dsem.count), R, W)

    def barrier(self, extra_sems=(), include_pool=False):
        sems = [e.sem for e in self.eng.values()] + list(extra_sems)
        for E in self.eng.values():
            if E.name == "pool" and not include_pool:
                continue
            for s in sems:
                if s is E.sem or s.count == 0:
                    continue
                if E.known.get(s, 0) >= s.count:
                    continue
                E.q.append(("w", s, s.count))
                E.known[s] = s.count

    @staticmethod
    def replay(q, e):
        for a in q:
            if a[0] == "w":
                e.wait_ge(a[1].h, a[2])
            else:
                a[1](e).then_inc(a[2].h, a[3])


class Prog:
    def __init__(self, nseq, layers, phases=("ffn0", "mix", "ffn1"), final_norm=True, debug=None,
                 mix_order=(2, 0, 1)):
        self.mix_order = mix_order
        self.dn_stop = 3
        self.marks = []
        self.npe = 0
        self.dn_bsteps = 1000
        self.nseq = nseq
        self.layers = list(layers)
        self.nl = len(self.layers)
        self.phases = phases
        self.final_norm = final_norm
        self.debug = debug

    def build(self):
        nc = bass.Bass("TRN2", target_bir_lowering=False)
        self.nc = nc
        nl = self.nl
        dr = lambda name, shape, kind="ExternalInput": nc.dram_tensor(name, list(shape), F32, kind=kind)
        self.d_x = dr("xT", (self.nseq, 128, KC, S))
        self.d_y = dr("yT", (self.nseq, 128, KC, S), "ExternalOutput")
        self.d_cv = dr("cvec", (128, NCV))
        self.d_cm = dr("cmat", (128, NCM))
        self.d_wg = dr("wg", (nl, 2, FC, 128, KC * 128))
        self.d_wu = dr("wu", (nl, 2, FC, 128, KC * 128))
        self.d_wd = dr("wd", (nl, 2, 2, KC, 128, FH * 128))
        self.d_win = dr("win", (nl, 56, 128, KC * 128))
        self.d_wab = dr("wab", (nl, 128, KC * 8))
        self.d_wb = dr("wb", (nl, 3, KC, 128, 4 * 128))
        self.d_wo = dr("wo", (nl, KC, 128, KC * 128))
        self.d_wpool = dr("wpool", (nl, 128, 4 * 128))
        self.d_xsp = dr("xspill", (128, KC, S), "Internal")
        if self.debug:
            self.d_dbg = dr("dbg", self.debug["shape"], "ExternalOutput")

        with ExitStack() as es:
            ctx = Ctx(nc, es)
            self.ctx = ctx
            sb = lambda name, shape, dt: es.enter_context(nc.sbuf_tensor(name, list(shape), dt))
            self.CV = sb("CV", (128, NCV), F32)
            self.CMF = sb("CMF", (128, CM_MB), F32)
            self.CMB = sb("CMB", (128, NCM), BF16)
            self.CMR = self.CMF
            self.NRING = 5
            self.RSZ = 1408
            self.RING = sb("RING", (128, self.NRING, self.RSZ), BF16)
            self.H = sb("H", (128, KC, S), BF16)
            self.SQ = sb("SQ", (128, 2, TT), BF16)
            self.RSTD = sb("RSTD", (128, 2, TT), F32)
            self.TMPF = sb("TMPF", (128, 2, TT), F32)
            self.ACOLS = 35840
            self.ARENA = sb("ARENA", (128, self.ACOLS), F32)
            self.PS = [es.enter_context(nc.psum_tensor("PS%d" % i, [128, 512], F32)) for i in range(8)]
            self.ring_sems = [ctx.new_sem("ring%d" % i) for i in range(self.NRING)]
            self.sem_c = ctx.new_sem("consts")
            self.sem_c2 = ctx.new_sem("consts2")
            self.sem_x = ctx.new_sem("xio")
            self.sem_o = ctx.new_sem("out")
            self.ring_i = 0
            for en in ("pe", "act", "dve", "pool", "sp"):
                ctx.add_engine(en)
            self.emit()
            ctx.barrier(extra_sems=[self.sem_o, self.sem_x, self.sem_c, self.sem_c2] + self.ring_sems)
            block = es.enter_context(nc.Block())

            @block.tensor
            def _(e):
                Ctx.replay(ctx.eng["pe"].q, e)

            @block.scalar
            def _(e):
                Ctx.replay(ctx.eng["act"].q, e)

            @block.vector
            def _(e):
                Ctx.replay(ctx.eng["dve"].q, e)

            @block.gpsimd
            def _(e):
                Ctx.replay(ctx.eng["pool"].q, e)

            @block.sync
            def _(e):
                Ctx.replay(ctx.eng["sp"].q, e)
        return nc

    def av(self, off, n, dt=F32, pat=None, **kw):
        ap = self.ARENA[:, off:off + n]
        if dt is not F32:
            ap = ap.bitcast(dt)
        if pat:
            ap = ap.rearrange(pat, **kw)
        return ap

    def cvc(self, col, n=1):
        return self.CV[:, col:col + n]

    def wload(self, dram_ap, E):
        ctx = self.ctx
        i = self.ring_i % self.NRING
        self.ring_i += 1
        tr = ctx.T("ring", i)
        dst = self.RING[:, i, 0:E]
        pairs = []
        step = 2048
        for c0 in range(0, E, step):
            c1 = min(E, c0 + step)
            pairs.append((self.RING[:, i, c0:c1], dram_ap[:, c0:c1], {}))
        ctx.dma("pool", pairs, self.ring_sems[i], R=(), W=(tr,))
        return dst, tr

    def mark(self, name):
        self.marks.append((name, self.npe))

    def mm(self, out, pairs, R, W, start=True, stop=True):
        n = len(pairs)
        self.npe += n

        def f(e):
            last = None
            for i, (l, r) in enumerate(pairs):
                last = e.matmul(out, l, r, start=(start and i == 0), stop=(stop and i == n - 1))
            return last
        self.ctx.op("pe", f, None, R, W)

    def emit(self):
        ctx = self.ctx
        T = ctx.T
        ctx.dma("sp", [(self.CV[:, :], self.d_cv[:, :], {}), (self.CMF[:, :], self.d_cm[:, 0:CM_MB], {})],
                self.sem_c, W=(T("cv"), T("cmf")))
        pairs = [(self.CMB[:, c:c + 1024], self.d_cm[:, c:c + 1024], {}) for c in range(0, NCM, 1024)]
        ctx.dma("pool", pairs, self.sem_c2, W=(T("cmb"),))
        ctx.barrier(extra_sems=[self.sem_c, self.sem_c2])
        self.X = self.av(0, KC * S, F32, "p (k n) -> p k n", k=KC)
        for sq in range(self.nseq):
            ctx.dma("sp", [(self.X[:, kc, :], self.d_x[sq, :, kc, :], {}) for kc in range(KC)],
                    self.sem_x, W=[T("X", kc, t) for kc in range(KC) for t in range(NT)])
            for li in range(self.nl):
                if "ffn0" in self.phases:
                    self.ffn(li, 0)
                if "mix" in self.phases:
                    self.mixer(li)
                if "ffn1" in self.phases:
                    self.ffn(li, 1)
            if self.final_norm:
                self.final(sq)
            else:
                ctx.dma("sp", [(self.d_y[sq, :, kc, :], self.X[:, kc, :], {}) for kc in range(KC)],
                        self.sem_o, R=[T("X", kc, t) for kc in range(KC) for t in range(NT)])
            ctx.barrier(extra_sems=[self.sem_o, self.sem_x])

    def rmsnorm(self, gcol, dst, dst_tr):
        ctx = self.ctx
        T = ctx.T
        ones_b = self.CMB[:, CM_ONES:CM_ONES + 128]
        PN = self.PS[6]
        for t in range(NT):
            ts = slice(t * TT, (t + 1) * TT)
            for kc in range(KC):
                sl = (t * KC + kc) % 2
                ctx.op("act", "activation", dict(out=self.SQ[:, sl, :], in_=self.X[:, kc, ts], func=AF.Square),
                       R=(T("X", kc, t),), W=(T("SQ", sl),))
                self.mm(PN[:, :], [(ones_b, self.SQ[:, sl, :])], R=(T("SQ", sl), T("cmb")), W=(T("PS", 6),),
                        start=(kc == 0), stop=(kc == KC - 1))
            rs = t % 2
            ctx.op("act", "activation", dict(out=self.RSTD[:, rs, :], in_=PN[:, :], func=AF.Sqrt,
                                             bias=self.cvc(CV_EPS), scale=1.0 / D),
                   R=(T("PS", 6), T("cv")), W=(T("RSTD", rs),))
            ctx.op("dve", "reciprocal", dict(out=self.RSTD[:, rs, :], in_=self.RSTD[:, rs, :]),
                   R=(T("RSTD", rs),), W=(T("RSTD", rs),))
            for kc in range(KC):
                ctx.op("dve", "scalar_tensor_tensor", dict(
                    out=dst[:, kc, ts], in0=self.X[:, kc, ts], scalar=self.cvc(gcol + kc), in1=self.RSTD[:, rs, :],
                    op0=ALU.mult, op1=ALU.mult),
                    R=(T("X", kc, t), T("RSTD", rs), T("cv")), W=(dst_tr(kc, t),))

    def ffn(self, li, j):
        ctx = self.ctx
        T = ctx.T
        ctx.barrier()
        self.mark('ffn%d' % j)
        A = self.av(KC * S, FH * S // 2, BF16, "p (f n) -> p f n", f=FH)
        self.rmsnorm(CV_FFN + (li * 2 + j) * KC, self.H, lambda kc, t: T("H", t))
        k = 0
        for hf in range(2):
            for fl in range(FH):
                fc = hf * FH + fl
                wg, wgt = self.wload(self.d_wg[li, j, fc], KC * 128)
                wu, wut = self.wload(self.d_wu[li, j, fc], KC * 128)
                for t in range(NT):
                    ts = slice(t * TT, (t + 1) * TT)
                    b = k % 2
                    k += 1
                    PG, PU = self.PS[b], self.PS[2 + b]
                    self.mm(PG[:, :], [(wg[:, kc * 128:(kc + 1) * 128], self.H[:, kc, ts]) for kc in range(KC)],
                            R=(wgt, T("H", t)), W=(T("PS", b),))
                    self.mm(PU[:, :], [(wu[:, kc * 128:(kc + 1) * 128], self.H[:, kc, ts]) for kc in range(KC)],
                            R=(wut, T("H", t)), W=(T("PS", 2 + b),))
                    ctx.op("act", "activation", dict(out=self.TMPF[:, b, :], in_=PG[:, :], func=AF.Silu),
                           R=(T("PS", b),), W=(T("TMPF", b),))
                    ctx.op("dve", "tensor_tensor", dict(out=A[:, fl, ts], in0=self.TMPF[:, b, :], in1=PU[:, :],
                                                        op=ALU.mult),
                           R=(T("TMPF", b), T("PS", 2 + b)), W=(T("A", fl, t),))
            for dc in range(KC):
                wd, wdt = self.wload(self.d_wd[li, j, hf, dc], FH * 128)
                for t in range(NT):
                    ts = slice(t * TT, (t + 1) * TT)
                    b = k % 2
                    k += 1
                    PY = self.PS[4 + b]
                    self.mm(PY[:, :], [(wd[:, fl * 128:(fl + 1) * 128], A[:, fl, ts]) for fl in range(FH)],
                            R=[wdt] + [T("A", fl, t) for fl in range(FH)], W=(T("PS", 4 + b),))
                    ctx.op("dve", "scalar_tensor_tensor", dict(
                        out=self.X[:, dc, ts], in0=PY[:, :], scalar=0.5, in1=self.X[:, dc, ts],
                        op0=ALU.mult, op1=ALU.add),
                        R=(T("PS", 4 + b), T("X", dc, t)), W=(T("X", dc, t),))

    def mixer(self, li):
        ctx = self.ctx
        T = ctx.T
        ctx.barrier()
        self.mark('mixnorm')
        xtr = [T("X", kc, t) for kc in range(KC) for t in range(NT)]
        ctx.dma("sp", [(self.d_xsp[:, kc, :], self.X[:, kc, :], {}) for kc in range(KC)], self.sem_x, R=xtr)
        self.rmsnorm(CV_MIX + li * KC, self.H, lambda kc, t: T("H", t))
        ctx.barrier(extra_sems=[self.sem_x])
        self.MACC = self.av(self.ACOLS - 8192, 8192, BF16, "p (k n) -> p k n", k=KC)
        self.Y = self.av(self.ACOLS - 12288, 4096, BF16, "p (k n) -> p k n", k=4)
        order = self.mix_order
        first = True
        for n in order:
            self.mark('branch%d' % n)
            if n == 2:
                self.sb_branch(li)
            elif n == 0:
                self.pool_branch(li)
            else:
                self.dn_branch(li)
            if n == order[-1]:
                ctx.barrier()
                ctx.dma("sp", [(self.X[:, kc, :], self.d_xsp[:, kc, :], {}) for kc in range(KC)], self.sem_x, W=xtr)
            self.mark('merge%d' % n)
            self.merge(li, n, first)
            first = False
        self.mark('wout')
        k = 0
        for dc in range(KC):
            wo, wot = self.wload(self.d_wo[li, dc], KC * 128)
            for t in range(NT):
                ts = slice(t * TT, (t + 1) * TT)
                PY = self.PS[4 + k % 2]
                ptr = T("PS", 4 + k % 2)
                k += 1
                self.mm(PY[:, :], [(wo[:, kc * 128:(kc + 1) * 128], self.MACC[:, kc, ts]) for kc in range(KC)],
                        R=[wot] + [T("MACC", kc, t) for kc in range(KC)], W=(ptr,))
                ctx.op("dve", "tensor_tensor", dict(out=self.X[:, dc, ts], in0=PY[:, :], in1=self.X[:, dc, ts],
                                                    op=ALU.add),
                       R=(ptr, T("X", dc, t)), W=(T("X", dc, t),))

    def merge(self, li, n, first):
        ctx = self.ctx
        T = ctx.T
        k = 0
        for dc in range(KC):
            wgt, wgtt = self.wload(self.d_win[li, WIN_G + n * 8 + dc], KC * 128)
            wbr, wbrt = self.wload(self.d_wb[li, n, dc], 4 * 128)
            for t in range(NT):
                ts = slice(t * TT, (t + 1) * TT)
                b = k % 2
                k += 1
                PG, PB_ = self.PS[b], self.PS[2 + b]
                self.mm(PG[:, :], [(wgt[:, kc * 128:(kc + 1) * 128], self.H[:, kc, ts]) for kc in range(KC)],
                        R=(wgtt, T("H", t)), W=(T("PS", b),))
                ctx.op("act", "activation", dict(out=self.TMPF[:, b, :], in_=PG[:, :], func=AF.Sigmoid,
                                                 bias=self.cvc(CV_BG + li * 24 + n * 8 + dc), scale=1.0),
                       R=(T("PS", b), T("cv")), W=(T("TMPF", b),))
                self.mm(PB_[:, :], [(wbr[:, k4 * 128:(k4 + 1) * 128], self.Y[:, k4, ts]) for k4 in range(4)],
                        R=[wbrt] + [T("Y", k4, t) for k4 in range(4)], W=(T("PS", 2 + b),))
                if first:
                    ctx.op("dve", "tensor_tensor", dict(out=self.MACC[:, dc, ts], in0=self.TMPF[:, b, :],
                                                        in1=PB_[:, :], op=ALU.mult),
                           R=(T("TMPF", b), T("PS", 2 + b)), W=(T("MACC", dc, t),))
                else:
                    ctx.op("dve", "tensor_tensor", dict(out=self.RSTD[:, b, :], in0=self.TMPF[:, b, :],
                                                        in1=PB_[:, :], op=ALU.mult),
                           R=(T("TMPF", b), T("PS", 2 + b)), W=(T("RSTD", b),))
                    ctx.op("dve", "tensor_tensor", dict(out=self.MACC[:, dc, ts], in0=self.RSTD[:, b, :],
                                                        in1=self.MACC[:, dc, ts], op=ALU.add),
                           R=(T("RSTD", b), T("MACC", dc, t)), W=(T("MACC", dc, t),))

    def sb_branch(self, li):
        ctx = self.ctx
        T = ctx.T
        QT = self.av(0, 1024, BF16)
        KT = self.av(1024, 1024, BF16)
        VT = self.av(2048, 1024, BF16, "p (c d) -> p c d", c=16)
        SP = self.av(3072, 4096, BF16, "p (c t) -> p c t", c=16)
        ET = self.av(7168, 1024, F32, "p (b t) -> p b t", b=2)
        WT = self.av(8192, 512, BF16, "p (b t) -> p b t", b=2)
        SS = self.av(8704, 4096, BF16, "p (c t) -> p c t", c=16)
        ident_b = self.CMB[:, CM_ID:CM_ID + 128]
        negui = self.CMB[:, CM_NEGUI:CM_NEGUI + 128]
        negones = self.CMB[:, CM_NEGONES:CM_NEGONES + 128]
        cmb = T("cmb")
        ka = kb = ko = kp = 0
        for hd in range(NH):
            wq, wqt = self.wload(self.d_win[li, WIN_SQ + hd], KC * 128)
            wk, wkt = self.wload(self.d_win[li, WIN_SK + hd], KC * 128)
            wv, wvt = self.wload(self.d_win[li, WIN_SV + hd], KC * 128)
            for t in range(NT):
                ts = slice(t * TT, (t + 1) * TT)
                b = kp % 2
                kp += 1
                self.mm(self.PS[b][:, :], [(wq[:, kc * 128:(kc + 1) * 128], self.H[:, kc, ts]) for kc in range(KC)],
                        R=(wqt, T("H", t)), W=(T("PS", b),))
                ctx.op("dve", "tensor_scalar_mul", dict(out=QT[:, ts], in0=self.PS[b][:, :], scalar1=128.0 ** -0.5),
                       R=(T("PS", b),), W=(T("QT", t),))
                b = kp % 2
                kp += 1
                self.mm(self.PS[b][:, :], [(wk[:, kc * 128:(kc + 1) * 128], self.H[:, kc, ts]) for kc in range(KC)],
                        R=(wkt, T("H", t)), W=(T("PS", b),))
                ctx.op("dve", "tensor_copy", dict(out=KT[:, ts], in_=self.PS[b][:, :]),
                       R=(T("PS", b),), W=(T("KT", t),))
            for g4 in range(4):
                b = kp % 2
                kp += 1
                for cc in range(4):
                    c = g4 * 4 + cc
                    self.mm(self.PS[b][:, cc * 128:(cc + 1) * 128],
                            [(self.H[:, kc, c * 128:(c + 1) * 128], wv[:, kc * 128:(kc + 1) * 128]) for kc in range(KC)],
                            R=(wvt, T("H", g4)), W=(T("PS", b),))
                ctx.op("dve", "tensor_copy", dict(out=VT[:, g4 * 4:(g4 + 1) * 4, :],
                                                  in_=self.PS[b][:, :].rearrange("p (c d) -> p c d", c=4)),
                       R=(T("PS", b),), W=(T("VT", g4),))
            for I in range(4):
                n = 4 * (I + 1)
                qs = QT[:, I * TT:(I + 1) * TT]
                qtr = T("QT", I)
                for c in range(n - 1, -1, -1):
                    b = ka % 2
                    ka += 1
                    pairs = [(KT[:, c * 128:(c + 1) * 128], qs)]
                    if c >= 4 * I:
                        m = c - 4 * I
                        pairs.append((ident_b, self.CMB[:, CM_MB + m * 512:CM_MB + (m + 1) * 512]))
                    self.mm(self.PS[b][:, :], pairs, R=(T("KT", c // 4), qtr, cmb), W=(T("PS", b),))
                    ctx.op("act", "activation", dict(out=ET[:, b, :], in_=self.PS[b][:, :], func=AF.Exp),
                           R=(T("PS", b),), W=(T("ET", b),))
                    ctx.op("act", "activation", dict(out=SP[:, c, :], in_=ET[:, b, :], func=AF.Ln,
                                                     bias=self.cvc(CV_ONE), scale=1.0),
                           R=(T("ET", b), T("cv")), W=(T("SP", c),))
                    if c == n - 1:
                        pass
                    elif c >= 1:
                        prev = SP[:, c + 1, :] if c + 1 == n - 1 else SS[:, c + 1, :]
                        prevt = T("SP", c + 1) if c + 1 == n - 1 else T("SS", c + 1)
                        ctx.op("dve", "tensor_tensor", dict(out=SS[:, c, :], in0=prev, in1=SP[:, c, :], op=ALU.add),
                               R=(prevt, T("SP", c)), W=(T("SS", c),))
                po = 4 + ko % 2
                ko += 1

                def grp(c):
                    nonlocal kb
                    b = 2 + kb % 2
                    kb += 1
                    pairs = [(KT[:, c * 128:(c + 1) * 128], qs)]
                    if c >= 4 * I:
                        m = c - 4 * I
                        pairs.append((ident_b, self.CMB[:, CM_MB + m * 512:CM_MB + (m + 1) * 512]))
                    pairs.append((negui, SP[:, c, :]))
                    rd = [T("KT", c // 4), qtr, cmb, T("SP", c)]
                    if c + 1 <= n - 1:
                        if c + 1 == n - 1:
                            pairs.append((negones, SP[:, c + 1, :]))
                            rd.append(T("SP", c + 1))
                        else:
                            pairs.append((negones, SS[:, c + 1, :]))
                            rd.append(T("SS", c + 1))
                    self.mm(self.PS[b][:, :], pairs, R=rd, W=(T("PS", b),))
                    return b
                bnext = grp(0)
                for c in range(n):
                    b = bnext
                    wb_ = c % 2
                    ctx.op("act", "activation", dict(out=WT[:, wb_, :], in_=self.PS[b][:, :], func=AF.Exp),
                           R=(T("PS", b),), W=(T("WT", wb_),))
                    if c + 1 < n:
                        bnext = grp(c + 1)
                    self.mm(self.PS[po][:, :], [(VT[:, c, :], WT[:, wb_, :])],
                            R=(T("VT", c // 4), T("WT", wb_)), W=(T("PS", po),), start=(c == 0), stop=(c == n - 1))
                ctx.op("dve", "tensor_copy", dict(out=self.Y[:, hd, I * TT:(I + 1) * TT], in_=self.PS[po][:, :]),
                       R=(T("PS", po),), W=(T("Y", hd, I),))

    def pool_branch(self, li):
        ctx = self.ctx
        T = ctx.T
        PADW = 16
        UP = self.av(0, 2064, F32)
        PA = self.av(2064, 2064, F32)
        PBf = self.av(4128, 2064, F32)
        POOLED = self.av(6192, 1024, BF16)
        FIX = self.av(7216, 16, F32)
        for nm, buf in (("UP", UP), ("PA", PA), ("PBf", PBf)):
            ctx.op("dve", "memset", dict(ap=buf[:, 0:PADW], constant=0.0), W=(T(nm),))
        kp = 0
        for g, win in enumerate(POOL_WINDOWS):
            wu, wut = self.wload(self.d_win[li, WIN_POOL + g], KC * 128)
            wp, wpt = self.wload(self.d_wpool[li], 4 * 128)
            for t in range(NT):
                ts = slice(t * TT, (t + 1) * TT)
                b = kp % 2
                kp += 1
                self.mm(self.PS[b][:, :], [(wu[:, kc * 128:(kc + 1) * 128], self.H[:, kc, ts]) for kc in range(KC)],
                        R=(wut, T("H", t)), W=(T("PS", b),))
                ctx.op("dve", "tensor_copy", dict(out=UP[:, PADW + t * TT:PADW + (t + 1) * TT], in_=self.PS[b][:, :]),
                       R=(T("PS", b),), W=(T("UP"),))
            cur, curn = UP, "UP"
            bufs = [(PA, "PA"), (PBf, "PBf")]
            for lev in range(g + 1):
                sh = 2 ** lev
                nxt, nxtn = bufs[lev % 2]
                ctx.op("dve", "tensor_tensor", dict(out=nxt[:, PADW:PADW + S], in0=cur[:, PADW:PADW + S],
                                                    in1=cur[:, PADW - sh:PADW - sh + S], op=ALU.add),
                       R=(T(curn),), W=(T(nxtn),))
                cur, curn = nxt, nxtn
            ctx.op("dve", "scalar_tensor_tensor", dict(out=POOLED[:, :], in0=cur[:, PADW:PADW + S], scalar=1.0 / win,
                                                       in1=UP[:, PADW:PADW + S], op0=ALU.mult, op1=ALU.subtract),
                   R=(T(curn), T("UP")), W=(T("POOLED"),))
            ctx.op("dve", "tensor_tensor", dict(out=FIX[:, :], in0=cur[:, PADW:PADW + 16],
                                                in1=self.cvc(CV_INVC + g * 16, 16), op=ALU.mult),
                   R=(T(curn), T("cv")), W=(T("FIX"),))
            ctx.op("dve", "tensor_tensor", dict(out=POOLED[:, 0:16], in0=FIX[:, :], in1=UP[:, PADW:PADW + 16],
                                                op=ALU.subtract),
                   R=(T("FIX"), T("UP")), W=(T("POOLED"),))
            for t in range(NT):
                ts = slice(t * TT, (t + 1) * TT)
                b = kp % 2
                kp += 1
                self.mm(self.PS[b][:, :], [(wp[:, g * 128:(g + 1) * 128], POOLED[:, ts])],
                        R=(wpt, T("POOLED")), W=(T("PS", b),))
                ctx.op("dve", "tensor_scalar_mul", dict(out=self.Y[:, g, ts], in0=self.PS[b][:, :],
                                                        scalar1=self.cvc(CV_PS + li * 4 + g)),
                       R=(T("PS", b), T("cv")), W=(T("Y", g, t),))

    def dn_branch(self, li):
        ctx = self.ctx
        T = ctx.T
        o = 0

        def take(n):
            nonlocal o
            r = o
            o += n
            return r
        QINS = [self.av(take(2056), 2056, F32) for _ in range(2)]
        OT = QINS[0][:, 8:8 + S]
        QN = self.av(take(2048), 2048, F32)
        KN = self.av(take(2048), 2048, F32)
        VN = self.av(take(2048), 2048, F32)
        KDEC = KN.rearrange("p (c d) -> p c d", c=16)
        ATT = VN.rearrange("p (c d) -> p c d", c=16)
        NCH = 8
        KBG = self.av(take(2048), 2048, F32, "p (c d) -> p c d", c=16)
        WTf = KBG.rearrange("p c d -> p (c d)")
        BV = self.av(take(2048), 2048, F32, "p (c d) -> p c d", c=16)
        SQ2 = self.av(take(1024), 1024, BF16)
        IB = [[self.av(take(128), 128, F32) for _ in range(5)] for _ in range(NCH)]
        ABT = self.av(take(128), 128, F32, "p (c e) -> p c e", c=16)
        G = self.av(take(64), 64, F32, "p (h c) -> p h c", h=4)
        BETA = self.av(take(64), 64, F32, "p (h c) -> p h c", h=4)
        EXPS = self.av(take(256), 256, F32, "p (k h c) -> p k h c", k=4, h=4)
        EGC, EREM, CDA, CDB = (EXPS[:, i] for i in range(4))
        BEG = self.av(take(64), 64, F32, "p (h c) -> p h c", h=4)
        EA = self.av(take(4), 4, F32)
        E1 = self.av(take(32), 32, F32)
        ST = self.av(take(128), 128, F32)
        VNEW = self.av(take(128), 128, F32)
        RS2 = self.TMPF[:, 0, :]
        assert o <= self.ACOLS - 12288, o
        Rr = lambda ap: ap
        IDF = self.CMF[:, CM_ID:CM_ID + 128]
        UPI = self.CMF[:, CM_UPI:CM_UPI + 128]
        LOS = self.CMF[:, CM_LOS:CM_LOS + 128]
        IDR = self.CMR[:, CM_ID:CM_ID + 128]
        ONESR = self.CMR[:, CM_ONES:CM_ONES + 128]
        UPIR = self.CMR[:, CM_UPI:CM_UPI + 128]
        INDS = [self.CMR[:, c:c + 128] for c in (CM_UPI, CM_LOS, CM_INDA, CM_INDB)]
        ones_b = self.CMB[:, CM_ONES:CM_ONES + 128]
        cmf, cv = T("cmf"), T("cv")
        P7 = self.PS[7]

        wab, wabt = self.wload(self.d_wab[li], KC * 8)
        for c in range(16):
            self.mm(P7[:, c * 8:(c + 1) * 8],
                    [(self.H[:, kc, c * 128:(c + 1) * 128], wab[:, kc * 8:(kc + 1) * 8]) for kc in range(KC)],
                    R=(wabt, T("H", c // 4)), W=(T("PS", 7),))
        ctx.op("dve", "tensor_copy", dict(out=ABT[:, :, :], in_=P7[:, 0:128].rearrange("p (c e) -> p c e", c=16)),
               R=(T("PS", 7),), W=(T("ABT"),))
        ctx.op("act", "activation", dict(out=EA[:, :], in_=self.cvc(CV_AL + li * 4, 4), func=AF.Exp),
               R=(cv,), W=(T("EA"),))
        for hd in range(NH):
            ctx.op("act", "activation", dict(out=E1[:, 0:16], in_=ABT[:, :, hd], func=AF.Exp,
                                             bias=self.cvc(CV_DT + li * 4 + hd), scale=1.0),
                   R=(T("ABT"), cv), W=(T("E1"),))
            ctx.op("act", "activation", dict(out=E1[:, 0:16], in_=E1[:, 0:16], func=AF.Ln,
                                             bias=self.cvc(CV_ONE), scale=1.0),
                   R=(T("E1"), cv), W=(T("E1"),))
            ctx.op("dve", "tensor_scalar", dict(out=Rr(G[:, hd, :]), in0=E1[:, 0:16], scalar1=EA[:, hd:hd + 1], scalar2=-1.0,
                                                op0=ALU.mult, op1=ALU.mult),
                   R=(T("E1"), T("EA")), W=(T("G"),))
            ctx.op("act", "activation", dict(out=E1[:, 16:32], in_=ABT[:, :, 4 + hd], func=AF.Exp, scale=-1.0),
                   R=(T("ABT"),), W=(T("E1b"),))
            ctx.op("dve", "tensor_scalar_add", dict(out=E1[:, 16:32], in0=E1[:, 16:32], scalar1=1.0),
                   R=(T("E1b"),), W=(T("E1b"),))
            ctx.op("dve", "reciprocal", dict(out=BETA[:, hd, :], in_=E1[:, 16:32]),
                   R=(T("E1b"),), W=(T("BETA"),))
        Gf = G.rearrange("p h c -> p (h c)")
        for i, M in enumerate(INDS):
            self.mm(P7[:, 128 + i * 64:128 + (i + 1) * 64], [(M, Rr(Gf))], R=(T("G"), cmf), W=(T("PS", 7),))
        ctx.op("act", "activation", dict(out=EXPS.rearrange("p k h c -> p (k h c)"), in_=P7[:, 128:384], func=AF.Exp),
               R=(T("PS", 7),), W=(T("EXPS"),))
        ctx.op("dve", "tensor_tensor", dict(out=BEG[:, :, :], in0=BETA[:, :, :], in1=EGC, op=ALU.mult),
               R=(T("BETA"), T("EXPS")), W=(T("BEG"),))

        kp = 0
        if self.dn_stop == 0:
            return
        for hd in range(NH):
            self.mark('dnA%d' % hd)
            ZS = self.Y[:, hd, :]
            def chunk_gen(jq, nm, dst, widx, cidx):
                nonlocal kp
                qsel = (3 * hd + jq) % 2
                QIN = QINS[qsel]
                w, wt = self.wload(self.d_win[li, widx], KC * 128)
                ctx.op("dve", "memset", dict(ap=QIN[:, 0:8], constant=0.0), W=(T("QIN", qsel),))
                for t in range(NT):
                    ts = slice(t * TT, (t + 1) * TT)
                    b = kp % 2
                    kp += 1
                    self.mm(self.PS[b][:, :], [(w[:, kc * 128:(kc + 1) * 128], self.H[:, kc, ts]) for kc in range(KC)],
                            R=(wt, T("H", t)), W=(T("PS", b),))
                    ctx.op("act", "activation", dict(out=QIN[:, 8 + t * TT:8 + (t + 1) * TT], in_=self.PS[b][:, :], func=AF.Copy),
                           R=(T("PS", b),), W=(T("QIN", qsel),))
                yield
                dtr = [T(nm, p) for p in range(16)]
                cbase = CV_CONV + li * 48 + cidx
                ctx.op("act", "activation", dict(out=Rr(dst[:, :]), in_=QIN[:, 5:5 + S], func=AF.Copy, scale=self.cvc(cbase)),
                       R=(T("QIN", qsel), cv), W=dtr)
                for k in range(1, 4):
                    ctx.op("dve", "scalar_tensor_tensor", dict(out=Rr(dst[:, :]), in0=QIN[:, 5 + k:5 + k + S],
                                                               scalar=self.cvc(cbase + 12 * k), in1=dst[:, :],
                                                               op0=ALU.mult, op1=ALU.add),
                           R=[T("QIN", qsel), cv] + dtr, W=dtr)
                yield
                ctx.op("act", "activation", dict(out=Rr(dst[:, :]), in_=dst[:, :], func=AF.Silu), R=dtr, W=dtr)
                if nm != "VN":
                    ctx.op("act", "activation", dict(out=SQ2[:, :], in_=dst[:, :], func=AF.Square), R=dtr, W=(T("SQ2"),))
                    for t in range(NT):
                        ts = slice(t * TT, (t + 1) * TT)
                        b = kp % 2
                        kp += 1
                        ptr = [T(nm, p) for p in range(4 * t, 4 * t + 4)]
                        self.mm(self.PS[b][:, :], [(ones_b, SQ2[:, ts])], R=(T("SQ2"), T("cmb")), W=(T("PS", b),))
                        ctx.op("act", "activation", dict(out=RS2[:, :], in_=self.PS[b][:, :], func=AF.Sqrt,
                                                         bias=self.cvc(CV_EPS), scale=1.0),
                               R=(T("PS", b), cv), W=(T("TMPF", 0),))
                        ctx.op("dve", "reciprocal", dict(out=RS2[:, :], in_=RS2[:, :]), R=(T("TMPF", 0),), W=(T("TMPF", 0),))
                        if nm == "QN":
                            ctx.op("dve", "scalar_tensor_tensor", dict(out=Rr(dst[:, ts]), in0=dst[:, ts], scalar=128.0 ** -0.5,
                                                                       in1=RS2[:, :], op0=ALU.mult, op1=ALU.mult),
                                   R=[T("TMPF", 0)] + ptr, W=ptr)
                        else:
                            ctx.op("dve", "tensor_tensor", dict(out=Rr(dst[:, ts]), in0=dst[:, ts], in1=RS2[:, :], op=ALU.mult),
                                   R=[T("TMPF", 0)] + ptr, W=ptr)

            def _fin(g_):
                for _ in g_:
                    pass
            cg = [chunk_gen(jq, *args) for jq, args in enumerate((("QN", QN, WIN_DQ + hd, hd), ("KN", KN, WIN_DK + hd, 4 + hd),
                                                                  ("VN", VN, WIN_DV + hd, 8 + hd)))]
            next(cg[0]); next(cg[0]); next(cg[1]); _fin(cg[0]); next(cg[1]); next(cg[2]); _fin(cg[1]); next(cg[2]); _fin(cg[2])
            w, wt = self.wload(self.d_win[li, WIN_DZ + hd], KC * 128)
            for t in range(NT):
                ts = slice(t * TT, (t + 1) * TT)
                b = kp % 2
                kp += 1
                self.mm(self.PS[b][:, :], [(w[:, kc * 128:(kc + 1) * 128], self.H[:, kc, ts]) for kc in range(KC)],
                        R=(wt, T("H", t)), W=(T("PS", b),))
                ctx.op("act", "activation", dict(out=ZS[:, ts], in_=self.PS[b][:, :], func=AF.Silu),
                       R=(T("PS", b),), W=(T("Y", hd, t),))

            if self.dn_stop == 1:
                continue
            self.mark('dnB%d' % hd)
            def chain(ci, p):
                cs = slice(p * 128, (p + 1) * 128)
                B0, B1, B2, B3, B4 = IB[ci]
                bt = [T("IB", ci, i) for i in range(5)]
                Pc = self.PS[ci]
                pt = T("PS", ci)

                def q(i):
                    return Pc[:, i * 128:(i + 1) * 128]
                kn, qn, vn = T("KN", p), T("QN", p), T("VN", p)
                self.mm(q(0), [(Rr(KN[:, cs]), IDR)], R=(kn, cmf), W=(pt,))
                self.mm(q(1), [(Rr(VN[:, cs]), IDR)], R=(vn, cmf), W=(pt,))
                self.mm(q(2), [(Rr(KN[:, cs]), Rr(KN[:, cs]))], R=(kn,), W=(pt,))
                self.mm(q(3), [(Rr(KN[:, cs]), Rr(QN[:, cs]))], R=(kn, qn), W=(pt,))
                ctx.op("dve", "tensor_scalar_mul", dict(out=Rr(B0), in0=LOS, scalar1=G[:, hd, p:p + 1]),
                       R=(T("G"), cmf), W=(bt[0],))
                yield
                ctx.op("dve", "tensor_scalar_mul", dict(out=Rr(KBG[:, p, :]), in0=q(0), scalar1=BEG[:, hd, p:p + 1]),
                       R=(pt, T("BEG")), W=(T("KBG", p),))
                ctx.op("dve", "tensor_scalar_mul", dict(out=Rr(KDEC[:, p, :]), in0=q(0), scalar1=EREM[:, hd, p:p + 1]),
                       R=(pt, T("EXPS")), W=(kn,))
                ctx.op("dve", "tensor_scalar_mul", dict(out=Rr(BV[:, p, :]), in0=q(1), scalar1=BETA[:, hd, p:p + 1]),
                       R=(pt, T("BETA")), W=(T("BV", p),))
                self.mm(q(0), [(UPIR, Rr(B0))], R=(bt[0], cmf), W=(pt,))
                self.mm(q(1), [(Rr(B0), UPIR)], R=(bt[0], cmf), W=(pt,))
                yield
                ctx.op("act", "activation", dict(out=B1, in_=q(0), func=AF.Exp), R=(pt,), W=(bt[1],))
                ctx.op("act", "activation", dict(out=B2, in_=q(1), func=AF.Exp), R=(pt,), W=(bt[2],))
                yield
                ctx.op("dve", "scalar_tensor_tensor", dict(out=Rr(B3), in0=B1, scalar=BETA[:, hd, p:p + 1], in1=LOS,
                                                           op0=ALU.mult, op1=ALU.mult),
                       R=(bt[1], T("BETA"), cmf), W=(bt[3],))
                ctx.op("dve", "tensor_tensor", dict(out=Rr(B3), in0=B3, in1=q(2), op=ALU.mult),
                       R=(bt[3], pt), W=(bt[3],))
                ctx.op("dve", "tensor_tensor", dict(out=B2, in0=B2, in1=UPI, op=ALU.mult),
                       R=(bt[2], cmf), W=(bt[2],))
                ctx.op("dve", "tensor_tensor", dict(out=Rr(ATT[:, p, :]), in0=B2, in1=q(3), op=ALU.mult),
                       R=(bt[2], pt), W=(vn,))
                yield
                self.mm(q(0), [(Rr(B3), IDR)], R=(bt[3], cmf), W=(pt,))
                yield
                ctx.op("dve", "tensor_copy", dict(out=Rr(B4), in_=q(0)), R=(pt,), W=(bt[4],))
                ctx.op("dve", "tensor_tensor", dict(out=Rr(B2), in0=IDF, in1=q(0), op=ALU.subtract),
                       R=(pt, cmf, vn), W=(bt[2],))
                yield
                Lc, Uc, Ln, Un = 3, 4, 1, 0
                NLEV = 6 if DNC == 128 else 5
                for lev in range(NLEV):
                    last = lev == NLEV - 1
                    if not last:
                        self.mm(q(0), [(Rr(IB[ci][Lc]), Rr(IB[ci][Uc]))], R=(bt[Lc], bt[Uc]), W=(pt,))
                    self.mm(q(1), [(Rr(IB[ci][Uc]), Rr(IB[ci][Lc]))], R=(bt[Lc], bt[Uc]), W=(pt,))
                    yield
                    if not last:
                        ctx.op("act", "activation", dict(out=Rr(IB[ci][Un]), in_=q(0), func=AF.Copy),
                               R=(pt,), W=(bt[Un],))
                    ctx.op("dve", "tensor_copy", dict(out=Rr(IB[ci][Ln]), in_=q(1)), R=(pt,), W=(bt[Ln],))
                    yield
                    self.mm(q(2), [(Rr(IB[ci][Ln]), Rr(B2))], R=(bt[Ln], bt[2]), W=(pt,))
                    yield
                    ctx.op("dve", "tensor_tensor", dict(out=Rr(B2), in0=B2, in1=q(2), op=ALU.add),
                           R=(bt[2], pt), W=(bt[2],))
                    yield
                    Lc, Ln = Ln, Lc
                    Uc, Un = Un, Uc
                self.mm(q(0), [(Rr(B2), Rr(BV[:, p, :]))], R=(bt[2], T("BV", p)), W=(pt,))
                self.mm(q(1), [(Rr(KBG[:, p, :]), Rr(B2))], R=(bt[2], T("KBG", p)), W=(pt,))
                ctx.op("dve", "tensor_scalar_mul", dict(out=Rr(B0), in0=IDF, scalar1=EGC[:, hd, p:p + 1]),
                       R=(T("EXPS"), cmf, bt[0]), W=(bt[0],))
                self.mm(q(2), [(ONESR, Rr(B0))], R=(bt[0], cmf), W=(pt,))
                yield
                ctx.op("dve", "tensor_copy", dict(out=Rr(BV[:, p, :]), in_=q(0)), R=(pt,), W=(T("BV", p),))
                ctx.op("dve", "tensor_copy", dict(out=Rr(WTf[:, cs]), in_=q(1)),
                       R=(pt,), W=(T("KBG", p),))
                ctx.op("dve", "tensor_tensor", dict(out=Rr(QN[:, cs]), in0=QN[:, cs], in1=q(2), op=ALU.mult),
                       R=(qn, pt), W=(qn,))
                yield

            for g4 in range(16 // NCH):
                gens = [chain(ci, g4 * NCH + ci) for ci in range(NCH)]
                alive = True
                rounds = 0
                while alive and rounds < self.dn_bsteps:
                    rounds += 1
                    alive = False
                    for gen in gens:
                        try:
                            next(gen)
                            alive = True
                        except StopIteration:
                            pass

            if self.dn_stop == 2:
                continue
            self.mark('dnC%d' % hd)
            ctx.op("dve", "memset", dict(ap=Rr(ST[:, :]), constant=0.0), W=(T("ST"),))
            NSTEP = S // DNC
            PERB = 512 // DNC
            for n in range(NSTEP):
                if DNC == 128:
                    p, hf = n, 0
                    r = slice(0, 128)
                else:
                    p, hf = n // 2, n % 2
                    r = slice(hf * 64, hf * 64 + 64)
                a = n % 2
                PVb, PSb = self.PS[a], self.PS[2 + a]
                po = 4 + (n // PERB) % 2
                self.mm(PVb[:, 0:128], [(Rr(WTf[:, p * 128:(p + 1) * 128]), Rr(ST[:, :]))], R=(T("KBG", p), T("ST")),
                        W=(T("PS", a),))
                oc = slice((n % PERB) * DNC, (n % PERB) * DNC + DNC)
                self.mm(self.PS[po][:, oc], [(Rr(ST[:, :]), Rr(QN[:, n * DNC:(n + 1) * DNC]))],
                        R=(T("ST"), T("QN", p)), W=(T("PS", po),), start=True, stop=False)
                ctx.op("dve", "tensor_tensor", dict(out=Rr(VNEW[r, :]), in0=BV[r, p, :], in1=PVb[r, 0:128], op=ALU.subtract),
                       R=(T("BV", p), T("PS", a)), W=(T("VNEW"),))
                self.mm(PSb[:, 0:128], [(Rr(KDEC[r, p, :]), Rr(VNEW[r, :]))], R=(T("KN", p), T("VNEW")), W=(T("PS", 2 + a),))
                self.mm(self.PS[po][:, oc], [(Rr(VNEW[r, :]), Rr(ATT[r, p, r]))],
                        R=(T("VNEW"), T("VN", p)), W=(T("PS", po),), start=False, stop=True)
                cd = (CDA if hf == 0 else CDB)[:, hd, p:p + 1]
                ctx.op("dve", "scalar_tensor_tensor", dict(out=Rr(ST[:, :]), in0=ST[:, :], scalar=cd, in1=PSb[:, 0:128],
                                                           op0=ALU.mult, op1=ALU.add),
                       R=(T("ST"), T("EXPS"), T("PS", 2 + a)), W=(T("ST"),))
                if n % PERB == PERB - 1:
                    t = n // PERB
                    ctx.op("dve", "tensor_copy", dict(out=OT[:, t * TT:(t + 1) * TT], in_=self.PS[po][:, :]),
                           R=(T("PS", po),), W=(T("QIN", 0),))
            self.mark('dnN%d' % hd)
            ctx.op("act", "activation", dict(out=SQ2[:, :], in_=OT, func=AF.Square), R=(T("QIN", 0),), W=(T("SQ2"),))
            for t in range(NT):
                ts = slice(t * TT, (t + 1) * TT)
                b = kp % 2
                kp += 1
                self.mm(self.PS[b][:, :], [(ones_b, SQ2[:, ts])], R=(T("SQ2"), T("cmb")), W=(T("PS", b),))
                ctx.op("act", "activation", dict(out=self.RSTD[:, 0, :], in_=self.PS[b][:, :], func=AF.Sqrt,
                                                 bias=self.cvc(CV_EPS), scale=1.0 / 128),
                       R=(T("PS", b), cv), W=(T("RSTD", 0),))
                ctx.op("dve", "reciprocal", dict(out=self.RSTD[:, 0, :], in_=self.RSTD[:, 0, :]), R=(T("RSTD", 0),), W=(T("RSTD", 0),))
                ctx.op("dve", "scalar_tensor_tensor", dict(out=self.TMPF[:, b, :], in0=OT[:, ts], scalar=self.cvc(CV_ON + li),
                                                           in1=self.RSTD[:, 0, :], op0=ALU.mult, op1=ALU.mult),
                       R=(T("QIN", 0), T("RSTD", 0), cv), W=(T("TMPF", b),))
                ctx.op("dve", "tensor_tensor", dict(out=self.Y[:, hd, ts], in0=self.TMPF[:, b, :], in1=ZS[:, ts], op=ALU.mult),
                       R=(T("TMPF", b), T("Y", hd, t)), W=(T("Y", hd, t),))


    def final(self, sq):
        ctx = self.ctx
        T = ctx.T
        ctx.barrier()
        self.mark('final')
        OUT = self.av(KC * S, KC * S, F32, "p (k n) -> p k n", k=KC)
        self.rmsnorm(CV_FIN, OUT, lambda kc, t: T("OUT", kc, t))
        ctx.dma("sp", [(self.d_y[sq, :, kc, :], OUT[:, kc, :], {}) for kc in range(KC)],
                self.sem_o, R=[T("OUT", kc, t) for kc in range(KC) for t in range(NT)])


def _prep_core_inputs(inp, seqs, layers):
    x = inp["x"]
    xT = np.stack([np.ascontiguousarray(x[b].T.reshape(KC, 128, S).transpose(1, 0, 2)) for b in seqs])
    m = {"xT": xT, "cvec": _host_cvec(inp), "cmat": _host_consts()}
    return m


def _unlayout(yT):
    return np.ascontiguousarray(yT.transpose(1, 0, 2).reshape(D, S).T)


_CACHE = {}


def kernel(**inputs):
    inp = {k: np.asarray(v, dtype=np.float32) for k, v in inputs.items()}
    B = inp["x"].shape[0]
    prog = Prog(SEQ_PER_CORE, range(L))
    nc = prog.build()
    wts = _host_weights(inp, range(L))
    cv, cm = _host_cvec(inp), _host_consts()
    in_maps = []
    for c in range(NCORES):
        seqs = [c * SEQ_PER_CORE + i for i in range(SEQ_PER_CORE)]
        m = _prep_core_inputs(inp, seqs, range(L))
        m.update(wts)
        in_maps.append(m)
    res = run_bass_kernel_spmd(nc, in_maps, core_ids=list(range(NCORES)))
    out = np.empty((B, S, D), np.float32)
    for c in range(NCORES):
        yT = res.results[c]["yT"]
        for i in range(SEQ_PER_CORE):
            out[c * SEQ_PER_CORE + i] = _unlayout(yT[i])
    return out
```

```python
import numpy as np
from contextlib import ExitStack

import concourse.bass as bass
import concourse.mybir as mybir
from concourse.bass_utils import run_bass_kernel_spmd

F32 = mybir.dt.float32
F32R = mybir.dt.float32r
BF16 = mybir.dt.bfloat16
AF = mybir.ActivationFunctionType
ALU = mybir.AluOpType

D = 1024
KC = 8
S = 2048
TT = 512
NT = S // TT
FF = 2816
FC = 22
FH = 11
L = 4
NH = 4
EPS = 1e-6
NCORES = 8
SEQ_PER_CORE = 2
POOL_WINDOWS = (2, 4, 8, 16)
NEG = -30000.0
DNC = 128

CV_FFN = 0
CV_MIX = CV_FFN + L * 2 * KC
CV_FIN = CV_MIX + L * KC
CV_BG = CV_FIN + KC
CV_PS = CV_BG + L * 24
CV_CONV = CV_PS + L * 4
CV_ON = CV_CONV + L * 48
CV_AL = CV_ON + L
CV_DT = CV_AL + L * 4
CV_INVC = CV_DT + L * 4
CV_EPS = CV_INVC + 64
CV_ONE = CV_EPS + 1
NCV = CV_ONE + 1

CM_ID = 0
CM_ONES = 128
CM_NEGONES = 256
CM_NEGUI = 384
CM_UPI = 512
CM_LOS = 640
CM_INDA = 768
CM_INDB = 896
CM_MB = 1024
NCM = CM_MB + 4 * 512


def _host_consts():
    cm = np.zeros((128, NCM), np.float32)
    idx = np.arange(128)
    cm[:, CM_ID:CM_ID + 128] = np.eye(128, dtype=np.float32)
    cm[:, CM_ONES:CM_ONES + 128] = 1.0
    cm[:, CM_NEGONES:CM_NEGONES + 128] = -1.0
    cm[:, CM_NEGUI:CM_NEGUI + 128] = -(idx[:, None] >= idx[None, :]).astype(np.float32)
    same = (idx[:, None] // DNC) == (idx[None, :] // DNC)
    cm[:, CM_UPI:CM_UPI + 128] = (same & (idx[:, None] <= idx[None, :])).astype(np.float32)
    cm[:, CM_LOS:CM_LOS + 128] = (same & (idx[:, None] > idx[None, :])).astype(np.float32)
    cm[:, CM_INDA:CM_INDA + 128] = (idx[:, None] < DNC).astype(np.float32)
    cm[:, CM_INDB:CM_INDB + 128] = (idx[:, None] >= 64).astype(np.float32)
    t = np.arange(512)
    for m in range(4):
        ok = (m * 128 + idx[:, None]) < t[None, :]
        cm[:, CM_MB + m * 512:CM_MB + (m + 1) * 512] = np.where(ok, 0.0, NEG).astype(np.float32)
    return cm


def _fm(v):
    n = v.shape[-1] // 128
    return np.ascontiguousarray(v.reshape(n, 128).T)


def _host_cvec(inp):
    cv = np.zeros((128, NCV), np.float32)
    for l in range(L):
        for j in range(2):
            c = CV_FFN + (l * 2 + j) * KC
            cv[:, c:c + KC] = _fm(inp["ffn_norm"][l, j])
        cv[:, CV_MIX + l * KC:CV_MIX + (l + 1) * KC] = _fm(inp["mix_norm"][l])
        cv[:, CV_BG + l * 24:CV_BG + (l + 1) * 24] = _fm(inp["b_gate"][l])
        cv[:, CV_PS + l * 4:CV_PS + (l + 1) * 4] = _fm(inp["pool_scale"][l])
        for k in range(4):
            c = CV_CONV + l * 48 + k * 12
            cv[:, c:c + 12] = _fm(inp["dn_conv"][l, k])
        cv[:, CV_ON + l] = inp["dn_out_norm"][l]
        cv[:, CV_AL + l * 4:CV_AL + (l + 1) * 4] = np.broadcast_to(inp["dn_A_log"][l], (128, 4))
        cv[:, CV_DT + l * 4:CV_DT + (l + 1) * 4] = np.broadcast_to(inp["dn_dt_bias"][l], (128, 4))
    cv[:, CV_FIN:CV_FIN + KC] = _fm(inp["final_norm"])
    cv[:, CV_EPS] = EPS
    cv[:, CV_ONE] = 1.0
    for g, w in enumerate(POOL_WINDOWS):
        cnt = np.minimum(np.arange(1, 17), w).astype(np.float32)
        cv[:, CV_INVC + g * 16:CV_INVC + (g + 1) * 16] = (np.float32(1.0) / cnt)[None, :]
    return cv


def _unit_cols(w, c0, ncol=128):
    nk = w.shape[0] // 128
    blk = w[:, c0:c0 + ncol].reshape(nk, 128, ncol).transpose(1, 0, 2)
    return blk.reshape(128, nk * ncol)


WIN_POOL, WIN_DQ, WIN_DK, WIN_DV, WIN_DZ, WIN_SQ, WIN_SK, WIN_SV, WIN_G = 0, 4, 8, 12, 16, 20, 24, 28, 32
_WIN_COLS = ([i * 128 for i in range(20)] + [2568 + i * 128 for i in range(12)]
             + [4104 + i * 128 for i in range(24)])


def _host_weights(inp, layers):
    nl = len(layers)
    out = {}
    wg = np.empty((nl, 2, FC, 128, KC * 128), np.float32)
    wu = np.empty((nl, 2, FC, 128, KC * 128), np.float32)
    wd = np.empty((nl, 2, 2, KC, 128, FH * 128), np.float32)
    win = np.empty((nl, 56, 128, KC * 128), np.float32)
    wab = np.empty((nl, 128, KC * 8), np.float32)
    wb = np.empty((nl, 3, KC, 128, 4 * 128), np.float32)
    wo = np.empty((nl, KC, 128, KC * 128), np.float32)
    wpool = np.empty((nl, 128, 4 * 128), np.float32)
    for li, l in enumerate(layers):
        for j in range(2):
            G = inp["ffn_w_gate"][l, j]
            U = inp["ffn_w_up"][l, j]
            Dn = inp["ffn_w_down"][l, j]
            for fc in range(FC):
                wg[li, j, fc] = _unit_cols(G, fc * 128)
                wu[li, j, fc] = _unit_cols(U, fc * 128)
            for hf in range(2):
                sub = Dn[hf * FH * 128:(hf + 1) * FH * 128]
                for dc in range(KC):
                    wd[li, j, hf, dc] = _unit_cols(sub, dc * 128)
        W = inp["w_in"][l]
        for ci, c0 in enumerate(_WIN_COLS):
            win[li, ci] = _unit_cols(W, c0)
        wab[li] = _unit_cols(W, 2560, 8)
        for n in range(3):
            for dc in range(KC):
                wb[li, n, dc] = _unit_cols(inp["w_branch"][l, n], dc * 128)
        for dc in range(KC):
            wo[li, dc] = _unit_cols(inp["w_out"][l], dc * 128)
        wpool[li] = inp["pool_w"][l].transpose(1, 0, 2).reshape(128, 4 * 128)
    out.update(wg=wg, wu=wu, wd=wd, win=win, wab=wab, wb=wb, wo=wo, wpool=wpool)
    return out


class Sem:
    def __init__(self, h, name):
        self.h = h
        self.name = name
        self.count = 0


class Track:
    __slots__ = ("w", "rs", "excl")

    def __init__(self, excl=False):
        self.w = None
        self.rs = {}
        self.excl = excl


class Eng:
    def __init__(self, name, sem):
        self.name = name
        self.sem = sem
        self.known = {}
        self.q = []


class Ctx:
    def __init__(self, nc, es):
        self.nc = nc
        self.es = es
        self.eng = {}
        self.tracks = {}
        self.nops = 0

    def new_sem(self, name):
        return Sem(self.es.enter_context(self.nc.semaphore(name)), name)

    def add_engine(self, name):
        self.eng[name] = Eng(name, self.new_sem("s_" + name))

    def T(self, *key):
        t = self.tracks.get(key)
        if t is None:
            t = Track(excl=(key[0] == "PS"))
            self.tracks[key] = t
        return t

    @staticmethod
    def _split(R, W):
        ex = [t for t in R if t.excl]
        if ex:
            return [t for t in R if not t.excl], list(W) + ex
        return R, W

    def _waits(self, E, R, W):
        need = {}
        for t in R:
            if t.w is not None:
                s, v = t.w
                if need.get(s, 0) < v:
                    need[s] = v
        for t in W:
            if t.w is not None:
                s, v = t.w
                if need.get(s, 0) < v:
                    need[s] = v
            for s, v in t.rs.items():
                if need.get(s, 0) < v:
                    need[s] = v
        for s, v in need.items():
            if s is E.sem and E.name == "pe":
                continue
            if E.known.get(s, 0) >= v:
                continue
            E.q.append(("w", s, v))
            E.known[s] = v

    def _mark(self, tok, R, W):
        for t in R:
            t.rs[tok[0]] = tok[1]
        for t in W:
            t.w = tok
            t.rs = {}

    def op(self, en, meth, kw, R=(), W=()):
        E = self.eng[en]
        R, W = self._split(R, W)
        self._waits(E, R, W)
        if callable(meth):
            fn = meth
        else:
            fn = (lambda e, m=meth, k=dict(kw): getattr(e, m)(**k))
        E.q.append(("i", fn, E.sem, 1))
        E.sem.count += 1
        self._mark((E.sem, E.sem.count), R, W)
        self.nops += 1

    def dma(self, en, pairs, dsem, R=(), W=()):
        E = self.eng[en]
        self._waits(E, R, W)
        for o, i, kw in pairs:
            E.q.append(("i", (lambda e, o=o, i=i, kw=kw: e.dma_start(out=o, in_=i, **kw)), dsem, 16))
            dsem.count += 16
        self._mark((dsem, dsem.count), R, W)

    def barrier(self, extra_sems=()):
        sems = [e.sem for e in self.eng.values()] + list(extra_sems)
        for E in self.eng.values():
            for s in sems:
                if s is E.sem or s.count == 0:
                    continue
                if E.known.get(s, 0) >= s.count:
                    continue
                E.q.append(("w", s, s.count))
                E.known[s] = s.count

    @staticmethod
    def replay(q, e):
        for a in q:
            if a[0] == "w":
                e.wait_ge(a[1].h, a[2])
            else:
                a[1](e).then_inc(a[2].h, a[3])


class Prog:
    def __init__(self, nseq, layers, phases=("ffn0", "mix", "ffn1"), final_norm=True, debug=None,
                 mix_order=(2, 0, 1)):
        self.mix_order = mix_order
        self.dn_stop = 3
        self.marks = []
        self.npe = 0
        self.dn_bsteps = 1000
        self.nseq = nseq
        self.layers = list(layers)
        self.nl = len(self.layers)
        self.phases = phases
        self.final_norm = final_norm
        self.debug = debug

    def build(self):
        nc = bass.Bass("TRN2", target_bir_lowering=False)
        self.nc = nc
        nl = self.nl
        dr = lambda name, shape, kind="ExternalInput": nc.dram_tensor(name, list(shape), F32, kind=kind)
        self.d_x = dr("xT", (self.nseq, 128, KC, S))
        self.d_y = dr("yT", (self.nseq, 128, KC, S), "ExternalOutput")
        self.d_cv = dr("cvec", (128, NCV))
        self.d_cm = dr("cmat", (128, NCM))
        self.d_wg = dr("wg", (nl, 2, FC, 128, KC * 128))
        self.d_wu = dr("wu", (nl, 2, FC, 128, KC * 128))
        self.d_wd = dr("wd", (nl, 2, 2, KC, 128, FH * 128))
        self.d_win = dr("win", (nl, 56, 128, KC * 128))
        self.d_wab = dr("wab", (nl, 128, KC * 8))
        self.d_wb = dr("wb", (nl, 3, KC, 128, 4 * 128))
        self.d_wo = dr("wo", (nl, KC, 128, KC * 128))
        self.d_wpool = dr("wpool", (nl, 128, 4 * 128))
        self.d_xsp = dr("xspill", (128, KC, S), "Internal")
        if self.debug:
            self.d_dbg = dr("dbg", self.debug["shape"], "ExternalOutput")

        with ExitStack() as es:
            ctx = Ctx(nc, es)
            self.ctx = ctx
            sb = lambda name, shape, dt: es.enter_context(nc.sbuf_tensor(name, list(shape), dt))
            self.CV = sb("CV", (128, NCV), F32)
            self.CMF = sb("CMF", (128, CM_MB), F32)
            self.CMB = sb("CMB", (128, NCM), BF16)
            self.CMR = self.CMF
            self.NRING = 4
            self.RSZ = 1408
            self.RING = sb("RING", (128, self.NRING, self.RSZ), BF16)
            self.H = sb("H", (128, KC, S), BF16)
            self.SQ = sb("SQ", (128, 2, TT), BF16)
            self.RSTD = sb("RSTD", (128, 2, TT), F32)
            self.TMPF = sb("TMPF", (128, 2, TT), F32)
            self.ACOLS = 35840
            self.ARENA = sb("ARENA", (128, self.ACOLS), F32)
            self.PS = [es.enter_context(nc.psum_tensor("PS%d" % i, [128, 512], F32)) for i in range(8)]
            self.ring_sems = [ctx.new_sem("ring%d" % i) for i in range(self.NRING)]
            self.sem_c = ctx.new_sem("consts")
            self.sem_c2 = ctx.new_sem("consts2")
            self.sem_x = ctx.new_sem("xio")
            self.sem_o = ctx.new_sem("out")
            self.ring_i = 0
            for en in ("pe", "act", "dve", "pool", "sp"):
                ctx.add_engine(en)
            self.emit()
            ctx.barrier(extra_sems=[self.sem_o, self.sem_x, self.sem_c, self.sem_c2] + self.ring_sems)
            block = es.enter_context(nc.Block())

            @block.tensor
            def _(e):
                Ctx.replay(ctx.eng["pe"].q, e)

            @block.scalar
            def _(e):
                Ctx.replay(ctx.eng["act"].q, e)

            @block.vector
            def _(e):
                Ctx.replay(ctx.eng["dve"].q, e)

            @block.gpsimd
            def _(e):
                Ctx.replay(ctx.eng["pool"].q, e)

            @block.sync
            def _(e):
                Ctx.replay(ctx.eng["sp"].q, e)
        return nc

    def av(self, off, n, dt=F32, pat=None, **kw):
        ap = self.ARENA[:, off:off + n]
        if dt is not F32:
            ap = ap.bitcast(dt)
        if pat:
            ap = ap.rearrange(pat, **kw)
        return ap

    def cvc(self, col, n=1):
        return self.CV[:, col:col + n]

    def wload(self, dram_ap, E):
        ctx = self.ctx
        i = self.ring_i % self.NRING
        self.ring_i += 1
        tr = ctx.T("ring", i)
        dst = self.RING[:, i, 0:E]
        pairs = []
        step = 2048
        for c0 in range(0, E, step):
            c1 = min(E, c0 + step)
            pairs.append((self.RING[:, i, c0:c1], dram_ap[:, c0:c1], {}))
        ctx.dma("pool", pairs, self.ring_sems[i], R=(), W=(tr,))
        return dst, tr

    def mark(self, name):
        self.marks.append((name, self.npe))

    def mm(self, out, pairs, R, W, start=True, stop=True):
        n = len(pairs)
        self.npe += n

        def f(e):
            last = None
            for i, (l, r) in enumerate(pairs):
                last = e.matmul(out, l, r, start=(start and i == 0), stop=(stop and i == n - 1))
            return last
        self.ctx.op("pe", f, None, R, W)

    def emit(self):
        ctx = self.ctx
        T = ctx.T
        ctx.dma("sp", [(self.CV[:, :], self.d_cv[:, :], {}), (self.CMF[:, :], self.d_cm[:, 0:CM_MB], {})],
                self.sem_c, W=(T("cv"), T("cmf")))
        pairs = [(self.CMB[:, c:c + 1024], self.d_cm[:, c:c + 1024], {}) for c in range(0, NCM, 1024)]
        ctx.dma("pool", pairs, self.sem_c2, W=(T("cmb"),))
        ctx.barrier(extra_sems=[self.sem_c, self.sem_c2])
        self.X = self.av(0, KC * S, F32, "p (k n) -> p k n", k=KC)
        for sq in range(self.nseq):
            ctx.dma("sp", [(self.X[:, kc, :], self.d_x[sq, :, kc, :], {}) for kc in range(KC)],
                    self.sem_x, W=[T("X", kc, t) for kc in range(KC) for t in range(NT)])
            for li in range(self.nl):
                if "ffn0" in self.phases:
                    self.ffn(li, 0)
                if "mix" in self.phases:
                    self.mixer(li)
                if "ffn1" in self.phases:
                    self.ffn(li, 1)
            if self.final_norm:
                self.final(sq)
            else:
                ctx.dma("sp", [(self.d_y[sq, :, kc, :], self.X[:, kc, :], {}) for kc in range(KC)],
                        self.sem_o, R=[T("X", kc, t) for kc in range(KC) for t in range(NT)])
            ctx.barrier(extra_sems=[self.sem_o, self.sem_x])

    def rmsnorm(self, gcol, dst, dst_tr):
        ctx = self.ctx
        T = ctx.T
        ones_b = self.CMB[:, CM_ONES:CM_ONES + 128]
        PN = self.PS[6]
        for t in range(NT):
            ts = slice(t * TT, (t + 1) * TT)
            for kc in range(KC):
                sl = (t * KC + kc) % 2
                ctx.op("act", "activation", dict(out=self.SQ[:, sl, :], in_=self.X[:, kc, ts], func=AF.Square),
                       R=(T("X", kc, t),), W=(T("SQ", sl),))
                self.mm(PN[:, :], [(ones_b, self.SQ[:, sl, :])], R=(T("SQ", sl), T("cmb")), W=(T("PS", 6),),
                        start=(kc == 0), stop=(kc == KC - 1))
            rs = t % 2
            ctx.op("act", "activation", dict(out=self.RSTD[:, rs, :], in_=PN[:, :], func=AF.Sqrt,
                                             bias=self.cvc(CV_EPS), scale=1.0 / D),
                   R=(T("PS", 6), T("cv")), W=(T("RSTD", rs),))
            ctx.op("dve", "reciprocal", dict(out=self.RSTD[:, rs, :], in_=self.RSTD[:, rs, :]),
                   R=(T("RSTD", rs),), W=(T("RSTD", rs),))
            for kc in range(KC):
                ctx.op("dve", "scalar_tensor_tensor", dict(
                    out=dst[:, kc, ts], in0=self.X[:, kc, ts], scalar=self.cvc(gcol + kc), in1=self.RSTD[:, rs, :],
                    op0=ALU.mult, op1=ALU.mult),
                    R=(T("X", kc, t), T("RSTD", rs), T("cv")), W=(dst_tr(kc, t),))

    def ffn(self, li, j):
        ctx = self.ctx
        T = ctx.T
        ctx.barrier()
        self.mark('ffn%d' % j)
        A = self.av(KC * S, FH * S // 2, BF16, "p (f n) -> p f n", f=FH)
        self.rmsnorm(CV_FFN + (li * 2 + j) * KC, self.H, lambda kc, t: T("H", t))
        k = 0
        for hf in range(2):
            for fl in range(FH):
                fc = hf * FH + fl
                wg, wgt = self.wload(self.d_wg[li, j, fc], KC * 128)
                wu, wut = self.wload(self.d_wu[li, j, fc], KC * 128)
                for t in range(NT):
                    ts = slice(t * TT, (t + 1) * TT)
                    b = k % 2
                    k += 1
                    PG, PU = self.PS[b], self.PS[2 + b]
                    self.mm(PG[:, :], [(wg[:, kc * 128:(kc + 1) * 128], self.H[:, kc, ts]) for kc in range(KC)],
                            R=(wgt, T("H", t)), W=(T("PS", b),))
                    self.mm(PU[:, :], [(wu[:, kc * 128:(kc + 1) * 128], self.H[:, kc, ts]) for kc in range(KC)],
                            R=(wut, T("H", t)), W=(T("PS", 2 + b),))
                    ctx.op("act", "activation", dict(out=self.TMPF[:, b, :], in_=PG[:, :], func=AF.Silu),
                           R=(T("PS", b),), W=(T("TMPF", b),))
                    ctx.op("dve", "tensor_tensor", dict(out=A[:, fl, ts], in0=self.TMPF[:, b, :], in1=PU[:, :],
                                                        op=ALU.mult),
                           R=(T("TMPF", b), T("PS", 2 + b)), W=(T("A", fl, t),))
            for dc in range(KC):
                wd, wdt = self.wload(self.d_wd[li, j, hf, dc], FH * 128)
                for t in range(NT):
                    ts = slice(t * TT, (t + 1) * TT)
                    b = k % 2
                    k += 1
                    PY = self.PS[4 + b]
                    self.mm(PY[:, :], [(wd[:, fl * 128:(fl + 1) * 128], A[:, fl, ts]) for fl in range(FH)],
                            R=[wdt] + [T("A", fl, t) for fl in range(FH)], W=(T("PS", 4 + b),))
                    ctx.op("dve", "scalar_tensor_tensor", dict(
                        out=self.X[:, dc, ts], in0=PY[:, :], scalar=0.5, in1=self.X[:, dc, ts],
                        op0=ALU.mult, op1=ALU.add),
                        R=(T("PS", 4 + b), T("X", dc, t)), W=(T("X", dc, t),))

    def mixer(self, li):
        ctx = self.ctx
        T = ctx.T
        ctx.barrier()
        self.mark('mixnorm')
        self.rmsnorm(CV_MIX + li * KC, self.H, lambda kc, t: T("H", t))
        xtr = [T("X", kc, t) for kc in range(KC) for t in range(NT)]
        ctx.dma("sp", [(self.d_xsp[:, kc, :], self.X[:, kc, :], {}) for kc in range(KC)], self.sem_x, R=xtr)
        ctx.barrier(extra_sems=[self.sem_x])
        self.MACC = self.av(self.ACOLS - 8192, 8192, BF16, "p (k n) -> p k n", k=KC)
        self.Y = self.av(self.ACOLS - 12288, 4096, BF16, "p (k n) -> p k n", k=4)
        order = self.mix_order
        first = True
        for n in order:
            self.mark('branch%d' % n)
            if n == 2:
                self.sb_branch(li)
            elif n == 0:
                self.pool_branch(li)
            else:
                self.dn_branch(li)
            self.mark('merge%d' % n)
            self.merge(li, n, first)
            first = False
        ctx.barrier()
        self.mark('wout')
        ctx.dma("sp", [(self.X[:, kc, :], self.d_xsp[:, kc, :], {}) for kc in range(KC)], self.sem_x, W=xtr)
        k = 0
        for dc in range(KC):
            wo, wot = self.wload(self.d_wo[li, dc], KC * 128)
            for t in range(NT):
                ts = slice(t * TT, (t + 1) * TT)
                PY = self.PS[4 + k % 2]
                ptr = T("PS", 4 + k % 2)
                k += 1
                self.mm(PY[:, :], [(wo[:, kc * 128:(kc + 1) * 128], self.MACC[:, kc, ts]) for kc in range(KC)],
                        R=[wot] + [T("MACC", kc, t) for kc in range(KC)], W=(ptr,))
                ctx.op("dve", "tensor_tensor", dict(out=self.X[:, dc, ts], in0=PY[:, :], in1=self.X[:, dc, ts],
                                                    op=ALU.add),
                       R=(ptr, T("X", dc, t)), W=(T("X", dc, t),))

    def merge(self, li, n, first):
        ctx = self.ctx
        T = ctx.T
        k = 0
        for dc in range(KC):
            wgt, wgtt = self.wload(self.d_win[li, WIN_G + n * 8 + dc], KC * 128)
            wbr, wbrt = self.wload(self.d_wb[li, n, dc], 4 * 128)
            for t in range(NT):
                ts = slice(t * TT, (t + 1) * TT)
                b = k % 2
                k += 1
                PG, PB_ = self.PS[b], self.PS[2 + b]
                self.mm(PG[:, :], [(wgt[:, kc * 128:(kc + 1) * 128], self.H[:, kc, ts]) for kc in range(KC)],
                        R=(wgtt, T("H", t)), W=(T("PS", b),))
                ctx.op("act", "activation", dict(out=self.TMPF[:, b, :], in_=PG[:, :], func=AF.Sigmoid,
                                                 bias=self.cvc(CV_BG + li * 24 + n * 8 + dc), scale=1.0),
                       R=(T("PS", b), T("cv")), W=(T("TMPF", b),))
                self.mm(PB_[:, :], [(wbr[:, k4 * 128:(k4 + 1) * 128], self.Y[:, k4, ts]) for k4 in range(4)],
                        R=[wbrt] + [T("Y", k4, t) for k4 in range(4)], W=(T("PS", 2 + b),))
                if first:
                    ctx.op("dve", "tensor_tensor", dict(out=self.MACC[:, dc, ts], in0=self.TMPF[:, b, :],
                                                        in1=PB_[:, :], op=ALU.mult),
                           R=(T("TMPF", b), T("PS", 2 + b)), W=(T("MACC", dc, t),))
                else:
                    ctx.op("dve", "tensor_tensor", dict(out=self.RSTD[:, b, :], in0=self.TMPF[:, b, :],
                                                        in1=PB_[:, :], op=ALU.mult),
                           R=(T("TMPF", b), T("PS", 2 + b)), W=(T("RSTD", b),))
                    ctx.op("dve", "tensor_tensor", dict(out=self.MACC[:, dc, ts], in0=self.RSTD[:, b, :],
                                                        in1=self.MACC[:, dc, ts], op=ALU.add),
                           R=(T("RSTD", b), T("MACC", dc, t)), W=(T("MACC", dc, t),))

    def sb_branch(self, li):
        ctx = self.ctx
        T = ctx.T
        QT = self.av(0, 1024, BF16)
        KT = self.av(1024, 1024, BF16)
        VT = self.av(2048, 1024, BF16, "p (c d) -> p c d", c=16)
        SP = self.av(3072, 4096, BF16, "p (c t) -> p c t", c=16)
        ET = self.av(7168, 1024, F32, "p (b t) -> p b t", b=2)
        WT = self.av(8192, 512, BF16, "p (b t) -> p b t", b=2)
        SS = self.av(8704, 4096, BF16, "p (c t) -> p c t", c=16)
        ident_b = self.CMB[:, CM_ID:CM_ID + 128]
        negui = self.CMB[:, CM_NEGUI:CM_NEGUI + 128]
        negones = self.CMB[:, CM_NEGONES:CM_NEGONES + 128]
        cmb = T("cmb")
        ka = kb = ko = kp = 0
        for hd in range(NH):
            wq, wqt = self.wload(self.d_win[li, WIN_SQ + hd], KC * 128)
            wk, wkt = self.wload(self.d_win[li, WIN_SK + hd], KC * 128)
            wv, wvt = self.wload(self.d_win[li, WIN_SV + hd], KC * 128)
            for t in range(NT):
                ts = slice(t * TT, (t + 1) * TT)
                b = kp % 2
                kp += 1
                self.mm(self.PS[b][:, :], [(wq[:, kc * 128:(kc + 1) * 128], self.H[:, kc, ts]) for kc in range(KC)],
                        R=(wqt, T("H", t)), W=(T("PS", b),))
                ctx.op("dve", "tensor_scalar_mul", dict(out=QT[:, ts], in0=self.PS[b][:, :], scalar1=128.0 ** -0.5),
                       R=(T("PS", b),), W=(T("QT", t),))
                b = kp % 2
                kp += 1
                self.mm(self.PS[b][:, :], [(wk[:, kc * 128:(kc + 1) * 128], self.H[:, kc, ts]) for kc in range(KC)],
                        R=(wkt, T("H", t)), W=(T("PS", b),))
                ctx.op("dve", "tensor_copy", dict(out=KT[:, ts], in_=self.PS[b][:, :]),
                       R=(T("PS", b),), W=(T("KT", t),))
            for g4 in range(4):
                b = kp % 2
                kp += 1
                for cc in range(4):
                    c = g4 * 4 + cc
                    self.mm(self.PS[b][:, cc * 128:(cc + 1) * 128],
                            [(self.H[:, kc, c * 128:(c + 1) * 128], wv[:, kc * 128:(kc + 1) * 128]) for kc in range(KC)],
                            R=(wvt, T("H", g4)), W=(T("PS", b),))
                ctx.op("dve", "tensor_copy", dict(out=VT[:, g4 * 4:(g4 + 1) * 4, :],
                                                  in_=self.PS[b][:, :].rearrange("p (c d) -> p c d", c=4)),
                       R=(T("PS", b),), W=(T("VT", g4),))
            for I in range(4):
                n = 4 * (I + 1)
                qs = QT[:, I * TT:(I + 1) * TT]
                qtr = T("QT", I)
                for c in range(n - 1, -1, -1):
                    b = ka % 2
                    ka += 1
                    pairs = [(KT[:, c * 128:(c + 1) * 128], qs)]
                    if c >= 4 * I:
                        m = c - 4 * I
                        pairs.append((ident_b, self.CMB[:, CM_MB + m * 512:CM_MB + (m + 1) * 512]))
                    self.mm(self.PS[b][:, :], pairs, R=(T("KT", c // 4), qtr, cmb), W=(T("PS", b),))
                    ctx.op("act", "activation", dict(out=ET[:, b, :], in_=self.PS[b][:, :], func=AF.Exp),
                           R=(T("PS", b),), W=(T("ET", b),))
                    ctx.op("act", "activation", dict(out=SP[:, c, :], in_=ET[:, b, :], func=AF.Ln,
                                                     bias=self.cvc(CV_ONE), scale=1.0),
                           R=(T("ET", b), T("cv")), W=(T("SP", c),))
                    if c == n - 1:
                        pass
                    elif c >= 1:
                        prev = SP[:, c + 1, :] if c + 1 == n - 1 else SS[:, c + 1, :]
                        prevt = T("SP", c + 1) if c + 1 == n - 1 else T("SS", c + 1)
                        ctx.op("dve", "tensor_tensor", dict(out=SS[:, c, :], in0=prev, in1=SP[:, c, :], op=ALU.add),
                               R=(prevt, T("SP", c)), W=(T("SS", c),))
                po = 4 + ko % 2
                ko += 1

                def grp(c):
                    nonlocal kb
                    b = 2 + kb % 2
                    kb += 1
                    pairs = [(KT[:, c * 128:(c + 1) * 128], qs)]
                    if c >= 4 * I:
                        m = c - 4 * I
                        pairs.append((ident_b, self.CMB[:, CM_MB + m * 512:CM_MB + (m + 1) * 512]))
                    pairs.append((negui, SP[:, c, :]))
                    rd = [T("KT", c // 4), qtr, cmb, T("SP", c)]
                    if c + 1 <= n - 1:
                        if c + 1 == n - 1:
                            pairs.append((negones, SP[:, c + 1, :]))
                            rd.append(T("SP", c + 1))
                        else:
                            pairs.append((negones, SS[:, c + 1, :]))
                            rd.append(T("SS", c + 1))
                    self.mm(self.PS[b][:, :], pairs, R=rd, W=(T("PS", b),))
                    return b
                bnext = grp(0)
                for c in range(n):
                    b = bnext
                    wb_ = c % 2
                    ctx.op("act", "activation", dict(out=WT[:, wb_, :], in_=self.PS[b][:, :], func=AF.Exp),
                           R=(T("PS", b),), W=(T("WT", wb_),))
                    if c + 1 < n:
                        bnext = grp(c + 1)
                    self.mm(self.PS[po][:, :], [(VT[:, c, :], WT[:, wb_, :])],
                            R=(T("VT", c // 4), T("WT", wb_)), W=(T("PS", po),), start=(c == 0), stop=(c == n - 1))
                ctx.op("dve", "tensor_copy", dict(out=self.Y[:, hd, I * TT:(I + 1) * TT], in_=self.PS[po][:, :]),
                       R=(T("PS", po),), W=(T("Y", hd, I),))

    def pool_branch(self, li):
        ctx = self.ctx
        T = ctx.T
        PADW = 16
        UP = self.av(0, 2064, F32)
        PA = self.av(2064, 2064, F32)
        PBf = self.av(4128, 2064, F32)
        POOLED = self.av(6192, 1024, BF16)
        FIX = self.av(7216, 16, F32)
        for nm, buf in (("UP", UP), ("PA", PA), ("PBf", PBf)):
            ctx.op("dve", "memset", dict(ap=buf[:, 0:PADW], constant=0.0), W=(T(nm),))
        kp = 0
        for g, win in enumerate(POOL_WINDOWS):
            wu, wut = self.wload(self.d_win[li, WIN_POOL + g], KC * 128)
            wp, wpt = self.wload(self.d_wpool[li], 4 * 128)
            for t in range(NT):
                ts = slice(t * TT, (t + 1) * TT)
                b = kp % 2
                kp += 1
                self.mm(self.PS[b][:, :], [(wu[:, kc * 128:(kc + 1) * 128], self.H[:, kc, ts]) for kc in range(KC)],
                        R=(wut, T("H", t)), W=(T("PS", b),))
                ctx.op("dve", "tensor_copy", dict(out=UP[:, PADW + t * TT:PADW + (t + 1) * TT], in_=self.PS[b][:, :]),
                       R=(T("PS", b),), W=(T("UP"),))
            cur, curn = UP, "UP"
            bufs = [(PA, "PA"), (PBf, "PBf")]
            for lev in range(g + 1):
                sh = 2 ** lev
                nxt, nxtn = bufs[lev % 2]
                ctx.op("dve", "tensor_tensor", dict(out=nxt[:, PADW:PADW + S], in0=cur[:, PADW:PADW + S],
                                                    in1=cur[:, PADW - sh:PADW - sh + S], op=ALU.add),
                       R=(T(curn),), W=(T(nxtn),))
                cur, curn = nxt, nxtn
            ctx.op("dve", "scalar_tensor_tensor", dict(out=POOLED[:, :], in0=cur[:, PADW:PADW + S], scalar=1.0 / win,
                                                       in1=UP[:, PADW:PADW + S], op0=ALU.mult, op1=ALU.subtract),
                   R=(T(curn), T("UP")), W=(T("POOLED"),))
            ctx.op("dve", "tensor_tensor", dict(out=FIX[:, :], in0=cur[:, PADW:PADW + 16],
                                                in1=self.cvc(CV_INVC + g * 16, 16), op=ALU.mult),
                   R=(T(curn), T("cv")), W=(T("FIX"),))
            ctx.op("dve", "tensor_tensor", dict(out=POOLED[:, 0:16], in0=FIX[:, :], in1=UP[:, PADW:PADW + 16],
                                                op=ALU.subtract),
                   R=(T("FIX"), T("UP")), W=(T("POOLED"),))
            for t in range(NT):
                ts = slice(t * TT, (t + 1) * TT)
                b = kp % 2
                kp += 1
                self.mm(self.PS[b][:, :], [(wp[:, g * 128:(g + 1) * 128], POOLED[:, ts])],
                        R=(wpt, T("POOLED")), W=(T("PS", b),))
                ctx.op("dve", "tensor_scalar_mul", dict(out=self.Y[:, g, ts], in0=self.PS[b][:, :],
                                                        scalar1=self.cvc(CV_PS + li * 4 + g)),
                       R=(T("PS", b), T("cv")), W=(T("Y", g, t),))

    def dn_branch(self, li):
        ctx = self.ctx
        T = ctx.T
        o = 0

        def take(n):
            nonlocal o
            r = o
            o += n
            return r
        QIN = self.av(take(2056), 2056, F32)
        OT = QIN[:, 8:8 + S]
        QN = self.av(take(2048), 2048, F32)
        KN = self.av(take(2048), 2048, F32)
        VN = self.av(take(2048), 2048, F32)
        KDEC = KN.rearrange("p (c d) -> p c d", c=16)
        ATT = VN.rearrange("p (c d) -> p c d", c=16)
        NCH = 8
        ZS = self.av(take(1024), 1024, BF16)
        KBG = self.av(take(2048), 2048, F32, "p (c d) -> p c d", c=16)
        WTf = KBG.rearrange("p c d -> p (c d)")
        BV = self.av(take(2048), 2048, F32, "p (c d) -> p c d", c=16)
        SQ2 = self.av(take(1024), 1024, BF16)
        IB = [[self.av(take(128), 128, F32) for _ in range(5)] for _ in range(NCH)]
        ABT = self.av(take(128), 128, F32, "p (c e) -> p c e", c=16)
        G = self.av(take(64), 64, F32, "p (h c) -> p h c", h=4)
        BETA = self.av(take(64), 64, F32, "p (h c) -> p h c", h=4)
        EXPS = self.av(take(256), 256, F32, "p (k h c) -> p k h c", k=4, h=4)
        EGC, EREM, CDA, CDB = (EXPS[:, i] for i in range(4))
        BEG = self.av(take(64), 64, F32, "p (h c) -> p h c", h=4)
        EA = self.av(take(4), 4, F32)
        E1 = self.av(take(32), 32, F32)
        ST = self.av(take(128), 128, F32)
        VNEW = self.av(take(128), 128, F32)
        RS2 = self.av(take(512), 512, F32)
        assert o <= self.ACOLS - 12288, o
        Rr = lambda ap: ap
        IDF = self.CMF[:, CM_ID:CM_ID + 128]
        UPI = self.CMF[:, CM_UPI:CM_UPI + 128]
        LOS = self.CMF[:, CM_LOS:CM_LOS + 128]
        IDR = self.CMR[:, CM_ID:CM_ID + 128]
        ONESR = self.CMR[:, CM_ONES:CM_ONES + 128]
        UPIR = self.CMR[:, CM_UPI:CM_UPI + 128]
        INDS = [self.CMR[:, c:c + 128] for c in (CM_UPI, CM_LOS, CM_INDA, CM_INDB)]
        ones_b = self.CMB[:, CM_ONES:CM_ONES + 128]
        cmf, cv = T("cmf"), T("cv")
        P7 = self.PS[7]

        wab, wabt = self.wload(self.d_wab[li], KC * 8)
        for c in range(16):
            self.mm(P7[:, c * 8:(c + 1) * 8],
                    [(self.H[:, kc, c * 128:(c + 1) * 128], wab[:, kc * 8:(kc + 1) * 8]) for kc in range(KC)],
                    R=(wabt, T("H", c // 4)), W=(T("PS", 7),))
        ctx.op("dve", "tensor_copy", dict(out=ABT[:, :, :], in_=P7[:, 0:128].rearrange("p (c e) -> p c e", c=16)),
               R=(T("PS", 7),), W=(T("ABT"),))
        ctx.op("act", "activation", dict(out=EA[:, :], in_=self.cvc(CV_AL + li * 4, 4), func=AF.Exp),
               R=(cv,), W=(T("EA"),))
        for hd in range(NH):
            ctx.op("act", "activation", dict(out=E1[:, 0:16], in_=ABT[:, :, hd], func=AF.Exp,
                                             bias=self.cvc(CV_DT + li * 4 + hd), scale=1.0),
                   R=(T("ABT"), cv), W=(T("E1"),))
            ctx.op("act", "activation", dict(out=E1[:, 0:16], in_=E1[:, 0:16], func=AF.Ln,
                                             bias=self.cvc(CV_ONE), scale=1.0),
                   R=(T("E1"), cv), W=(T("E1"),))
            ctx.op("dve", "tensor_scalar", dict(out=Rr(G[:, hd, :]), in0=E1[:, 0:16], scalar1=EA[:, hd:hd + 1], scalar2=-1.0,
                                                op0=ALU.mult, op1=ALU.mult),
                   R=(T("E1"), T("EA")), W=(T("G"),))
            ctx.op("act", "activation", dict(out=E1[:, 16:32], in_=ABT[:, :, 4 + hd], func=AF.Exp, scale=-1.0),
                   R=(T("ABT"),), W=(T("E1b"),))
            ctx.op("dve", "tensor_scalar_add", dict(out=E1[:, 16:32], in0=E1[:, 16:32], scalar1=1.0),
                   R=(T("E1b"),), W=(T("E1b"),))
            ctx.op("dve", "reciprocal", dict(out=BETA[:, hd, :], in_=E1[:, 16:32]),
                   R=(T("E1b"),), W=(T("BETA"),))
        Gf = G.rearrange("p h c -> p (h c)")
        for i, M in enumerate(INDS):
            self.mm(P7[:, 128 + i * 64:128 + (i + 1) * 64], [(M, Rr(Gf))], R=(T("G"), cmf), W=(T("PS", 7),))
        ctx.op("act", "activation", dict(out=EXPS.rearrange("p k h c -> p (k h c)"), in_=P7[:, 128:384], func=AF.Exp),
               R=(T("PS", 7),), W=(T("EXPS"),))
        ctx.op("dve", "tensor_tensor", dict(out=BEG[:, :, :], in0=BETA[:, :, :], in1=EGC, op=ALU.mult),
               R=(T("BETA"), T("EXPS")), W=(T("BEG"),))

        kp = 0
        if self.dn_stop == 0:
            return
        for hd in range(NH):
            self.mark('dnA%d' % hd)
            chunks = (("QN", QN, WIN_DQ + hd, hd), ("KN", KN, WIN_DK + hd, 4 + hd), ("VN", VN, WIN_DV + hd, 8 + hd))
            for nm, dst, widx, cidx in chunks:
                w, wt = self.wload(self.d_win[li, widx], KC * 128)
                ctx.op("dve", "memset", dict(ap=QIN[:, 0:8], constant=0.0), W=(T("QIN"),))
                for t in range(NT):
                    ts = slice(t * TT, (t + 1) * TT)
                    b = kp % 2
                    kp += 1
                    self.mm(self.PS[b][:, :], [(w[:, kc * 128:(kc + 1) * 128], self.H[:, kc, ts]) for kc in range(KC)],
                            R=(wt, T("H", t)), W=(T("PS", b),))
                    ctx.op("act", "activation", dict(out=QIN[:, 8 + t * TT:8 + (t + 1) * TT], in_=self.PS[b][:, :], func=AF.Copy),
                           R=(T("PS", b),), W=(T("QIN"),))
                dtr = [T(nm, p) for p in range(16)]
                cbase = CV_CONV + li * 48 + cidx
                ctx.op("act", "activation", dict(out=Rr(dst[:, :]), in_=QIN[:, 5:5 + S], func=AF.Copy, scale=self.cvc(cbase)),
                       R=(T("QIN"), cv), W=dtr)
                for k in range(1, 4):
                    ctx.op("dve", "scalar_tensor_tensor", dict(out=Rr(dst[:, :]), in0=QIN[:, 5 + k:5 + k + S],
                                                               scalar=self.cvc(cbase + 12 * k), in1=dst[:, :],
                                                               op0=ALU.mult, op1=ALU.add),
                           R=[T("QIN"), cv] + dtr, W=dtr)
            w, wt = self.wload(self.d_win[li, WIN_DZ + hd], KC * 128)
            for t in range(NT):
                ts = slice(t * TT, (t + 1) * TT)
                b = kp % 2
                kp += 1
                self.mm(self.PS[b][:, :], [(w[:, kc * 128:(kc + 1) * 128], self.H[:, kc, ts]) for kc in range(KC)],
                        R=(wt, T("H", t)), W=(T("PS", b),))
                ctx.op("act", "activation", dict(out=ZS[:, ts], in_=self.PS[b][:, :], func=AF.Silu),
                       R=(T("PS", b),), W=(T("ZS", t),))
            for nm, dst, widx, cidx in chunks:
                dtr = [T(nm, p) for p in range(16)]
                ctx.op("act", "activation", dict(out=Rr(dst[:, :]), in_=dst[:, :], func=AF.Silu), R=dtr, W=dtr)
            for nm, dst, widx, cidx in chunks[:2]:
                dtr = [T(nm, p) for p in range(16)]
                ctx.op("act", "activation", dict(out=SQ2[:, :], in_=dst[:, :], func=AF.Square), R=dtr, W=(T("SQ2"),))
                for t in range(NT):
                    ts = slice(t * TT, (t + 1) * TT)
                    b = kp % 2
                    kp += 1
                    ptr = [T(nm, p) for p in range(4 * t, 4 * t + 4)]
                    self.mm(self.PS[b][:, :], [(ones_b, SQ2[:, ts])], R=(T("SQ2"), T("cmb")), W=(T("PS", b),))
                    ctx.op("act", "activation", dict(out=RS2[:, :], in_=self.PS[b][:, :], func=AF.Ln,
                                                     bias=self.cvc(CV_EPS), scale=1.0),
                           R=(T("PS", b), cv), W=(T("RS2"),))
                    ctx.op("act", "activation", dict(out=RS2[:, :], in_=RS2[:, :], func=AF.Exp, scale=-0.5),
                           R=(T("RS2"),), W=(T("RS2"),))
                    if nm == "QN":
                        ctx.op("dve", "scalar_tensor_tensor", dict(out=Rr(dst[:, ts]), in0=dst[:, ts], scalar=128.0 ** -0.5,
                                                                   in1=RS2[:, :], op0=ALU.mult, op1=ALU.mult),
                               R=[T("RS2")] + ptr, W=ptr)
                    else:
                        ctx.op("dve", "tensor_tensor", dict(out=Rr(dst[:, ts]), in0=dst[:, ts], in1=RS2[:, :], op=ALU.mult),
                               R=[T("RS2")] + ptr, W=ptr)

            if self.dn_stop == 1:
                continue
            self.mark('dnB%d' % hd)
            def chain(ci, p):
                cs = slice(p * 128, (p + 1) * 128)
                B0, B1, B2, B3, B4 = IB[ci]
                bt = [T("IB", ci, i) for i in range(5)]
                Pc = self.PS[ci]
                pt = T("PS", ci)

                def q(i):
                    return Pc[:, i * 128:(i + 1) * 128]
                kn, qn, vn = T("KN", p), T("QN", p), T("VN", p)
                self.mm(q(0), [(Rr(KN[:, cs]), IDR)], R=(kn, cmf), W=(pt,))
                self.mm(q(1), [(Rr(VN[:, cs]), IDR)], R=(vn, cmf), W=(pt,))
                self.mm(q(2), [(Rr(KN[:, cs]), Rr(KN[:, cs]))], R=(kn,), W=(pt,))
                self.mm(q(3), [(Rr(KN[:, cs]), Rr(QN[:, cs]))], R=(kn, qn), W=(pt,))
                ctx.op("dve", "tensor_scalar_mul", dict(out=Rr(B0), in0=LOS, scalar1=G[:, hd, p:p + 1]),
                       R=(T("G"), cmf), W=(bt[0],))
                yield
                ctx.op("dve", "tensor_scalar_mul", dict(out=Rr(KBG[:, p, :]), in0=q(0), scalar1=BEG[:, hd, p:p + 1]),
                       R=(pt, T("BEG")), W=(T("KBG", p),))
                ctx.op("dve", "tensor_scalar_mul", dict(out=Rr(KDEC[:, p, :]), in0=q(0), scalar1=EREM[:, hd, p:p + 1]),
                       R=(pt, T("EXPS")), W=(kn,))
                ctx.op("dve", "tensor_scalar_mul", dict(out=Rr(BV[:, p, :]), in0=q(1), scalar1=BETA[:, hd, p:p + 1]),
                       R=(pt, T("BETA")), W=(T("BV", p),))
                self.mm(q(0), [(UPIR, Rr(B0))], R=(bt[0], cmf), W=(pt,))
                self.mm(q(1), [(Rr(B0), UPIR)], R=(bt[0], cmf), W=(pt,))
                yield
                ctx.op("act", "activation", dict(out=B1, in_=q(0), func=AF.Exp), R=(pt,), W=(bt[1],))
                ctx.op("act", "activation", dict(out=B2, in_=q(1), func=AF.Exp), R=(pt,), W=(bt[2],))
                yield
                ctx.op("dve", "scalar_tensor_tensor", dict(out=Rr(B3), in0=B1, scalar=BETA[:, hd, p:p + 1], in1=LOS,
                                                           op0=ALU.mult, op1=ALU.mult),
                       R=(bt[1], T("BETA"), cmf), W=(bt[3],))
                ctx.op("dve", "tensor_tensor", dict(out=Rr(B3), in0=B3, in1=q(2), op=ALU.mult),
                       R=(bt[3], pt), W=(bt[3],))
                ctx.op("dve", "tensor_tensor", dict(out=B2, in0=B2, in1=UPI, op=ALU.mult),
                       R=(bt[2], cmf), W=(bt[2],))
                ctx.op("dve", "tensor_tensor", dict(out=Rr(ATT[:, p, :]), in0=B2, in1=q(3), op=ALU.mult),
                       R=(bt[2], pt), W=(vn,))
                yield
                self.mm(q(0), [(Rr(B3), IDR)], R=(bt[3], cmf), W=(pt,))
                yield
                ctx.op("act", "activation", dict(out=Rr(B4), in_=q(0), func=AF.Copy), R=(pt,), W=(bt[4],))
                ctx.op("dve", "tensor_tensor", dict(out=Rr(B2), in0=IDF, in1=q(0), op=ALU.subtract),
                       R=(pt, cmf, vn), W=(bt[2],))
                yield
                Lc, Uc, Ln, Un = 3, 4, 1, 0
                NLEV = 6 if DNC == 128 else 5
                for lev in range(NLEV):
                    last = lev == NLEV - 1
                    if not last:
                        self.mm(q(0), [(Rr(IB[ci][Lc]), Rr(IB[ci][Uc]))], R=(bt[Lc], bt[Uc]), W=(pt,))
                    self.mm(q(1), [(Rr(IB[ci][Uc]), Rr(IB[ci][Lc]))], R=(bt[Lc], bt[Uc]), W=(pt,))
                    yield
                    if not last:
                        ctx.op("act", "activation", dict(out=Rr(IB[ci][Un]), in_=q(0), func=AF.Copy),
                               R=(pt,), W=(bt[Un],))
                    ctx.op("dve", "tensor_copy", dict(out=Rr(IB[ci][Ln]), in_=q(1)), R=(pt,), W=(bt[Ln],))
                    yield
                    self.mm(q(2), [(Rr(IB[ci][Ln]), Rr(B2))], R=(bt[Ln], bt[2]), W=(pt,))
                    yield
                    ctx.op("dve", "tensor_tensor", dict(out=Rr(B2), in0=B2, in1=q(2), op=ALU.add),
                           R=(bt[2], pt), W=(bt[2],))
                    yield
                    Lc, Ln = Ln, Lc
                    Uc, Un = Un, Uc
                self.mm(q(0), [(Rr(B2), Rr(BV[:, p, :]))], R=(bt[2], T("BV", p)), W=(pt,))
                self.mm(q(1), [(Rr(KBG[:, p, :]), Rr(B2))], R=(bt[2], T("KBG", p)), W=(pt,))
                ctx.op("dve", "tensor_scalar_mul", dict(out=Rr(B0), in0=IDF, scalar1=EGC[:, hd, p:p + 1]),
                       R=(T("EXPS"), cmf, bt[0]), W=(bt[0],))
                self.mm(q(2), [(ONESR, Rr(B0))], R=(bt[0], cmf), W=(pt,))
                yield
                ctx.op("dve", "tensor_copy", dict(out=Rr(BV[:, p, :]), in_=q(0)), R=(pt,), W=(T("BV", p),))
                ctx.op("act", "activation", dict(out=Rr(WTf[:, cs]), in_=q(1), func=AF.Copy),
                       R=(pt,), W=(T("KBG", p),))
                ctx.op("dve", "tensor_tensor", dict(out=Rr(QN[:, cs]), in0=QN[:, cs], in1=q(2), op=ALU.mult),
                       R=(qn, pt), W=(qn,))
                yield

            for g4 in range(16 // NCH):
                gens = [chain(ci, g4 * NCH + ci) for ci in range(NCH)]
                alive = True
                rounds = 0
                while alive and rounds < self.dn_bsteps:
                    rounds += 1
                    alive = False
                    for gen in gens:
                        try:
                            next(gen)
                            alive = True
                        except StopIteration:
                            pass

            if self.dn_stop == 2:
                continue
            self.mark('dnC%d' % hd)
            ctx.op("dve", "memset", dict(ap=Rr(ST[:, :]), constant=0.0), W=(T("ST"),))
            NSTEP = S // DNC
            PERB = 512 // DNC
            for n in range(NSTEP):
                if DNC == 128:
                    p, hf = n, 0
                    r = slice(0, 128)
                else:
                    p, hf = n // 2, n % 2
                    r = slice(hf * 64, hf * 64 + 64)
                a = n % 2
                PVb, PSb = self.PS[a], self.PS[2 + a]
                po = 4 + (n // PERB) % 2
                self.mm(PVb[:, 0:128], [(Rr(WTf[:, p * 128:(p + 1) * 128]), Rr(ST[:, :]))], R=(T("KBG", p), T("ST")),
                        W=(T("PS", a),))
                oc = slice((n % PERB) * DNC, (n % PERB) * DNC + DNC)
                self.mm(self.PS[po][:, oc], [(Rr(ST[:, :]), Rr(QN[:, n * DNC:(n + 1) * DNC]))],
                        R=(T("ST"), T("QN", p)), W=(T("PS", po),), start=True, stop=False)
                ctx.op("dve", "tensor_tensor", dict(out=Rr(VNEW[r, :]), in0=BV[r, p, :], in1=PVb[r, 0:128], op=ALU.subtract),
                       R=(T("BV", p), T("PS", a)), W=(T("VNEW"),))
                self.mm(PSb[:, 0:128], [(Rr(KDEC[r, p, :]), Rr(VNEW[r, :]))], R=(T("KN", p), T("VNEW")), W=(T("PS", 2 + a),))
                self.mm(self.PS[po][:, oc], [(Rr(VNEW[r, :]), Rr(ATT[r, p, r]))],
                        R=(T("VNEW"), T("VN", p)), W=(T("PS", po),), start=False, stop=True)
                cd = (CDA if hf == 0 else CDB)[:, hd, p:p + 1]
                ctx.op("dve", "scalar_tensor_tensor", dict(out=Rr(ST[:, :]), in0=ST[:, :], scalar=cd, in1=PSb[:, 0:128],
                                                           op0=ALU.mult, op1=ALU.add),
                       R=(T("ST"), T("EXPS"), T("PS", 2 + a)), W=(T("ST"),))
                if n % PERB == PERB - 1:
                    t = n // PERB
                    ctx.op("dve", "tensor_copy", dict(out=OT[:, t * TT:(t + 1) * TT], in_=self.PS[po][:, :]),
                           R=(T("PS", po),), W=(T("QIN"),))
            self.mark('dnN%d' % hd)
            ctx.op("act", "activation", dict(out=SQ2[:, :], in_=OT, func=AF.Square), R=(T("QIN"),), W=(T("SQ2"),))
            for t in range(NT):
                ts = slice(t * TT, (t + 1) * TT)
                b = kp % 2
                kp += 1
                self.mm(self.PS[b][:, :], [(ones_b, SQ2[:, ts])], R=(T("SQ2"), T("cmb")), W=(T("PS", b),))
                ctx.op("act", "activation", dict(out=RS2[:, :], in_=self.PS[b][:, :], func=AF.Ln,
                                                 bias=self.cvc(CV_EPS), scale=1.0 / 128),
                       R=(T("PS", b), cv), W=(T("RS2"),))
                ctx.op("act", "activation", dict(out=RS2[:, :], in_=RS2[:, :], func=AF.Exp, scale=-0.5),
                       R=(T("RS2"),), W=(T("RS2"),))
                ctx.op("dve", "scalar_tensor_tensor", dict(out=self.TMPF[:, b, :], in0=OT[:, ts], scalar=self.cvc(CV_ON + li),
                                                           in1=RS2[:, :], op0=ALU.mult, op1=ALU.mult),
                       R=(T("QIN"), T("RS2"), cv), W=(T("TMPF", b),))
                ctx.op("dve", "tensor_tensor", dict(out=self.Y[:, hd, ts], in0=self.TMPF[:, b, :], in1=ZS[:, ts], op=ALU.mult),
                       R=(T("TMPF", b), T("ZS", t)), W=(T("Y", hd, t),))


    def final(self, sq):
        ctx = self.ctx
        T = ctx.T
        ctx.barrier()
        self.mark('final')
        OUT = self.av(KC * S, KC * S, F32, "p (k n) -> p k n", k=KC)
        self.rmsnorm(CV_FIN, OUT, lambda kc, t: T("OUT", kc, t))
        ctx.dma("sp", [(self.d_y[sq, :, kc, :], OUT[:, kc, :], {}) for kc in range(KC)],
                self.sem_o, R=[T("OUT", kc, t) for kc in range(KC) for t in range(NT)])


def _prep_core_inputs(inp, seqs, layers):
    x = inp["x"]
    xT = np.stack([np.ascontiguousarray(x[b].T.reshape(KC, 128, S).transpose(1, 0, 2)) for b in seqs])
    m = {"xT": xT, "cvec": _host_cvec(inp), "cmat": _host_consts()}
    return m


def _unlayout(yT):
    return np.ascontiguousarray(yT.transpose(1, 0, 2).reshape(D, S).T)


_CACHE = {}


def kernel(**inputs):
    inp = {k: np.asarray(v, dtype=np.float32) for k, v in inputs.items()}
    B = inp["x"].shape[0]
    prog = Prog(SEQ_PER_CORE, range(L))
    nc = prog.build()
    wts = _host_weights(inp, range(L))
    cv, cm = _host_cvec(inp), _host_consts()
    in_maps = []
    for c in range(NCORES):
        seqs = [c * SEQ_PER_CORE + i for i in range(SEQ_PER_CORE)]
        m = _prep_core_inputs(inp, seqs, range(L))
        m.update(wts)
        in_maps.append(m)
    res = run_bass_kernel_spmd(nc, in_maps, core_ids=list(range(NCORES)))
    out = np.empty((B, S, D), np.float32)
    for c in range(NCORES):
        yT = res.results[c]["yT"]
        for i in range(SEQ_PER_CORE):
            out[c * SEQ_PER_CORE + i] = _unlayout(yT[i])
    return out
```

```python
import numpy as np
from contextlib import ExitStack

import concourse.bass as bass
import concourse.mybir as mybir
from concourse.bass_utils import run_bass_kernel_spmd

F32 = mybir.dt.float32
F32R = mybir.dt.float32r
BF16 = mybir.dt.bfloat16
AF = mybir.ActivationFunctionType
ALU = mybir.AluOpType

D = 1024
KC = 8
S = 2048
TT = 512
NT = S // TT
FF = 2816
FC = 22
FH = 11
L = 4
NH = 4
EPS = 1e-6
NCORES = 8
SEQ_PER_CORE = 2
POOL_WINDOWS = (2, 4, 8, 16)
NEG = -30000.0
DNC = 128

CV_FFN = 0
CV_MIX = CV_FFN + L * 2 * KC
CV_FIN = CV_MIX + L * KC
CV_BG = CV_FIN + KC
CV_PS = CV_BG + L * 24
CV_CONV = CV_PS + L * 4
CV_ON = CV_CONV + L * 48
CV_AL = CV_ON + L
CV_DT = CV_AL + L * 4
CV_INVC = CV_DT + L * 4
CV_EPS = CV_INVC + 64
CV_ONE = CV_EPS + 1
NCV = CV_ONE + 1

CM_ID = 0
CM_ONES = 128
CM_NEGONES = 256
CM_NEGUI = 384
CM_UPI = 512
CM_LOS = 640
CM_INDA = 768
CM_INDB = 896
CM_MB = 1024
NCM = CM_MB + 4 * 512


def _host_consts():
    cm = np.zeros((128, NCM), np.float32)
    idx = np.arange(128)
    cm[:, CM_ID:CM_ID + 128] = np.eye(128, dtype=np.float32)
    cm[:, CM_ONES:CM_ONES + 128] = 1.0
    cm[:, CM_NEGONES:CM_NEGONES + 128] = -1.0
    cm[:, CM_NEGUI:CM_NEGUI + 128] = -(idx[:, None] >= idx[None, :]).astype(np.float32)
    same = (idx[:, None] // DNC) == (idx[None, :] // DNC)
    cm[:, CM_UPI:CM_UPI + 128] = (same & (idx[:, None] <= idx[None, :])).astype(np.float32)
    cm[:, CM_LOS:CM_LOS + 128] = (same & (idx[:, None] > idx[None, :])).astype(np.float32)
    cm[:, CM_INDA:CM_INDA + 128] = (idx[:, None] < DNC).astype(np.float32)
    cm[:, CM_INDB:CM_INDB + 128] = (idx[:, None] >= 64).astype(np.float32)
    t = np.arange(512)
    for m in range(4):
        ok = (m * 128 + idx[:, None]) < t[None, :]
        cm[:, CM_MB + m * 512:CM_MB + (m + 1) * 512] = np.where(ok, 0.0, NEG).astype(np.float32)
    return cm


def _fm(v):
    n = v.shape[-1] // 128
    return np.ascontiguousarray(v.reshape(n, 128).T)


def _host_cvec(inp):
    cv = np.zeros((128, NCV), np.float32)
    for l in range(L):
        for j in range(2):
            c = CV_FFN + (l * 2 + j) * KC
            cv[:, c:c + KC] = _fm(inp["ffn_norm"][l, j])
        cv[:, CV_MIX + l * KC:CV_MIX + (l + 1) * KC] = _fm(inp["mix_norm"][l])
        cv[:, CV_BG + l * 24:CV_BG + (l + 1) * 24] = _fm(inp["b_gate"][l])
        cv[:, CV_PS + l * 4:CV_PS + (l + 1) * 4] = _fm(inp["pool_scale"][l])
        for k in range(4):
            c = CV_CONV + l * 48 + k * 12
            cv[:, c:c + 12] = _fm(inp["dn_conv"][l, k])
        cv[:, CV_ON + l] = inp["dn_out_norm"][l]
        cv[:, CV_AL + l * 4:CV_AL + (l + 1) * 4] = np.broadcast_to(inp["dn_A_log"][l], (128, 4))
        cv[:, CV_DT + l * 4:CV_DT + (l + 1) * 4] = np.broadcast_to(inp["dn_dt_bias"][l], (128, 4))
    cv[:, CV_FIN:CV_FIN + KC] = _fm(inp["final_norm"])
    cv[:, CV_EPS] = EPS
    cv[:, CV_ONE] = 1.0
    for g, w in enumerate(POOL_WINDOWS):
        cnt = np.minimum(np.arange(1, 17), w).astype(np.float32)
        cv[:, CV_INVC + g * 16:CV_INVC + (g + 1) * 16] = (np.float32(1.0) / cnt)[None, :]
    return cv


def _unit_cols(w, c0, ncol=128):
    nk = w.shape[0] // 128
    blk = w[:, c0:c0 + ncol].reshape(nk, 128, ncol).transpose(1, 0, 2)
    return blk.reshape(128, nk * ncol)


WIN_POOL, WIN_DQ, WIN_DK, WIN_DV, WIN_DZ, WIN_SQ, WIN_SK, WIN_SV, WIN_G = 0, 4, 8, 12, 16, 20, 24, 28, 32
_WIN_COLS = ([i * 128 for i in range(20)] + [2568 + i * 128 for i in range(12)]
             + [4104 + i * 128 for i in range(24)])


def _host_weights(inp, layers):
    nl = len(layers)
    out = {}
    wg = np.empty((nl, 2, FC, 128, KC * 128), np.float32)
    wu = np.empty((nl, 2, FC, 128, KC * 128), np.float32)
    wd = np.empty((nl, 2, 2, KC, 128, FH * 128), np.float32)
    win = np.empty((nl, 56, 128, KC * 128), np.float32)
    wab = np.empty((nl, 128, KC * 8), np.float32)
    wb = np.empty((nl, 3, KC, 128, 4 * 128), np.float32)
    wo = np.empty((nl, KC, 128, KC * 128), np.float32)
    wpool = np.empty((nl, 128, 4 * 128), np.float32)
    for li, l in enumerate(layers):
        for j in range(2):
            G = inp["ffn_w_gate"][l, j]
            U = inp["ffn_w_up"][l, j]
            Dn = inp["ffn_w_down"][l, j]
            for fc in range(FC):
                wg[li, j, fc] = _unit_cols(G, fc * 128)
                wu[li, j, fc] = _unit_cols(U, fc * 128)
            for hf in range(2):
                sub = Dn[hf * FH * 128:(hf + 1) * FH * 128]
                for dc in range(KC):
                    wd[li, j, hf, dc] = _unit_cols(sub, dc * 128)
        W = inp["w_in"][l]
        for ci, c0 in enumerate(_WIN_COLS):
            win[li, ci] = _unit_cols(W, c0)
        wab[li] = _unit_cols(W, 2560, 8)
        for n in range(3):
            for dc in range(KC):
                wb[li, n, dc] = _unit_cols(inp["w_branch"][l, n], dc * 128)
        for dc in range(KC):
            wo[li, dc] = _unit_cols(inp["w_out"][l], dc * 128)
        wpool[li] = inp["pool_w"][l].transpose(1, 0, 2).reshape(128, 4 * 128)
    out.update(wg=wg, wu=wu, wd=wd, win=win, wab=wab, wb=wb, wo=wo, wpool=wpool)
    return out


class Sem:
    def __init__(self, h, name):
        self.h = h
        self.name = name
        self.count = 0


class Track:
    __slots__ = ("w", "rs", "excl")

    def __init__(self, excl=False):
        self.w = None
        self.rs = {}
        self.excl = excl


class Eng:
    def __init__(self, name, sem):
        self.name = name
        self.sem = sem
        self.known = {}
        self.q = []


class Ctx:
    def __init__(self, nc, es):
        self.nc = nc
        self.es = es
        self.eng = {}
        self.tracks = {}
        self.nops = 0

    def new_sem(self, name):
        return Sem(self.es.enter_context(self.nc.semaphore(name)), name)

    def add_engine(self, name):
        self.eng[name] = Eng(name, self.new_sem("s_" + name))

    def T(self, *key):
        t = self.tracks.get(key)
        if t is None:
            t = Track(excl=(key[0] == "PS"))
            self.tracks[key] = t
        return t

    @staticmethod
    def _split(R, W):
        ex = [t for t in R if t.excl]
        if ex:
            return [t for t in R if not t.excl], list(W) + ex
        return R, W

    def _waits(self, E, R, W):
        need = {}
        for t in R:
            if t.w is not None:
                s, v = t.w
                if need.get(s, 0) < v:
                    need[s] = v
        for t in W:
            if t.w is not None:
                s, v = t.w
                if need.get(s, 0) < v:
                    need[s] = v
            for s, v in t.rs.items():
                if need.get(s, 0) < v:
                    need[s] = v
        for s, v in need.items():
            if s is E.sem and E.name == "pe":
                continue
            if E.known.get(s, 0) >= v:
                continue
            E.q.append(("w", s, v))
            E.known[s] = v

    def _mark(self, tok, R, W):
        for t in R:
            t.rs[tok[0]] = tok[1]
        for t in W:
            t.w = tok
            t.rs = {}

    def op(self, en, meth, kw, R=(), W=()):
        E = self.eng[en]
        R, W = self._split(R, W)
        self._waits(E, R, W)
        if callable(meth):
            fn = meth
        else:
            fn = (lambda e, m=meth, k=dict(kw): getattr(e, m)(**k))
        E.q.append(("i", fn, E.sem, 1))
        E.sem.count += 1
        self._mark((E.sem, E.sem.count), R, W)
        self.nops += 1

    def dma(self, en, pairs, dsem, R=(), W=()):
        E = self.eng[en]
        self._waits(E, R, W)
        for o, i, kw in pairs:
            E.q.append(("i", (lambda e, o=o, i=i, kw=kw: e.dma_start(out=o, in_=i, **kw)), dsem, 16))
            dsem.count += 16
        self._mark((dsem, dsem.count), R, W)

    def barrier(self, extra_sems=()):
        sems = [e.sem for e in self.eng.values()] + list(extra_sems)
        for E in self.eng.values():
            for s in sems:
                if s is E.sem or s.count == 0:
                    continue
                if E.known.get(s, 0) >= s.count:
                    continue
                E.q.append(("w", s, s.count))
                E.known[s] = s.count

    @staticmethod
    def replay(q, e):
        for a in q:
            if a[0] == "w":
                e.wait_ge(a[1].h, a[2])
            else:
                a[1](e).then_inc(a[2].h, a[3])


class Prog:
    def __init__(self, nseq, layers, phases=("ffn0", "mix", "ffn1"), final_norm=True, debug=None,
                 mix_order=(2, 0, 1)):
        self.mix_order = mix_order
        self.dn_stop = 3
        self.marks = []
        self.npe = 0
        self.dn_bsteps = 1000
        self.nseq = nseq
        self.layers = list(layers)
        self.nl = len(self.layers)
        self.phases = phases
        self.final_norm = final_norm
        self.debug = debug

    def build(self):
        nc = bass.Bass("TRN2", target_bir_lowering=False)
        self.nc = nc
        nl = self.nl
        dr = lambda name, shape, kind="ExternalInput": nc.dram_tensor(name, list(shape), F32, kind=kind)
        self.d_x = dr("xT", (self.nseq, 128, KC, S))
        self.d_y = dr("yT", (self.nseq, 128, KC, S), "ExternalOutput")
        self.d_cv = dr("cvec", (128, NCV))
        self.d_cm = dr("cmat", (128, NCM))
        self.d_wg = dr("wg", (nl, 2, FC, 128, KC * 128))
        self.d_wu = dr("wu", (nl, 2, FC, 128, KC * 128))
        self.d_wd = dr("wd", (nl, 2, 2, KC, 128, FH * 128))
        self.d_win = dr("win", (nl, 56, 128, KC * 128))
        self.d_wab = dr("wab", (nl, 128, KC * 8))
        self.d_wb = dr("wb", (nl, 3, KC, 128, 4 * 128))
        self.d_wo = dr("wo", (nl, KC, 128, KC * 128))
        self.d_wpool = dr("wpool", (nl, 128, 4 * 128))
        self.d_xsp = dr("xspill", (128, KC, S), "Internal")
        if self.debug:
            self.d_dbg = dr("dbg", self.debug["shape"], "ExternalOutput")

        with ExitStack() as es:
            ctx = Ctx(nc, es)
            self.ctx = ctx
            sb = lambda name, shape, dt: es.enter_context(nc.sbuf_tensor(name, list(shape), dt))
            self.CV = sb("CV", (128, NCV), F32)
            self.CMF = sb("CMF", (128, CM_MB), F32)
            self.CMB = sb("CMB", (128, NCM), BF16)
            self.CMR = self.CMF
            self.NRING = 4
            self.RSZ = 1408
            self.RING = sb("RING", (128, self.NRING, self.RSZ), BF16)
            self.H = sb("H", (128, KC, S), BF16)
            self.SQ = sb("SQ", (128, 2, TT), BF16)
            self.RSTD = sb("RSTD", (128, 2, TT), F32)
            self.TMPF = sb("TMPF", (128, 2, TT), F32)
            self.ACOLS = 35840
            self.ARENA = sb("ARENA", (128, self.ACOLS), F32)
            self.PS = [es.enter_context(nc.psum_tensor("PS%d" % i, [128, 512], F32)) for i in range(8)]
            self.ring_sems = [ctx.new_sem("ring%d" % i) for i in range(self.NRING)]
            self.sem_c = ctx.new_sem("consts")
            self.sem_c2 = ctx.new_sem("consts2")
            self.sem_x = ctx.new_sem("xio")
            self.sem_o = ctx.new_sem("out")
            self.ring_i = 0
            for en in ("pe", "act", "dve", "pool", "sp"):
                ctx.add_engine(en)
            self.emit()
            ctx.barrier(extra_sems=[self.sem_o, self.sem_x, self.sem_c, self.sem_c2] + self.ring_sems)
            block = es.enter_context(nc.Block())

            @block.tensor
            def _(e):
                Ctx.replay(ctx.eng["pe"].q, e)

            @block.scalar
            def _(e):
                Ctx.replay(ctx.eng["act"].q, e)

            @block.vector
            def _(e):
                Ctx.replay(ctx.eng["dve"].q, e)

            @block.gpsimd
            def _(e):
                Ctx.replay(ctx.eng["pool"].q, e)

            @block.sync
            def _(e):
                Ctx.replay(ctx.eng["sp"].q, e)
        return nc

    def av(self, off, n, dt=F32, pat=None, **kw):
        ap = self.ARENA[:, off:off + n]
        if dt is not F32:
            ap = ap.bitcast(dt)
        if pat:
            ap = ap.rearrange(pat, **kw)
        return ap

    def cvc(self, col, n=1):
        return self.CV[:, col:col + n]

    def wload(self, dram_ap, E):
        ctx = self.ctx
        i = self.ring_i % self.NRING
        self.ring_i += 1
        tr = ctx.T("ring", i)
        dst = self.RING[:, i, 0:E]
        pairs = []
        step = 2048
        for c0 in range(0, E, step):
            c1 = min(E, c0 + step)
            pairs.append((self.RING[:, i, c0:c1], dram_ap[:, c0:c1], {}))
        ctx.dma("pool", pairs, self.ring_sems[i], R=(), W=(tr,))
        return dst, tr

    def mark(self, name):
        self.marks.append((name, self.npe))

    def mm(self, out, pairs, R, W, start=True, stop=True):
        n = len(pairs)
        self.npe += n

        def f(e):
            last = None
            for i, (l, r) in enumerate(pairs):
                last = e.matmul(out, l, r, start=(start and i == 0), stop=(stop and i == n - 1))
            return last
        self.ctx.op("pe", f, None, R, W)

    def emit(self):
        ctx = self.ctx
        T = ctx.T
        ctx.dma("sp", [(self.CV[:, :], self.d_cv[:, :], {}), (self.CMF[:, :], self.d_cm[:, 0:CM_MB], {})],
                self.sem_c, W=(T("cv"), T("cmf")))
        pairs = [(self.CMB[:, c:c + 1024], self.d_cm[:, c:c + 1024], {}) for c in range(0, NCM, 1024)]
        ctx.dma("pool", pairs, self.sem_c2, W=(T("cmb"),))
        ctx.barrier(extra_sems=[self.sem_c, self.sem_c2])
        self.X = self.av(0, KC * S, F32, "p (k n) -> p k n", k=KC)
        for sq in range(self.nseq):
            ctx.dma("sp", [(self.X[:, kc, :], self.d_x[sq, :, kc, :], {}) for kc in range(KC)],
                    self.sem_x, W=[T("X", kc, t) for kc in range(KC) for t in range(NT)])
            for li in range(self.nl):
                if "ffn0" in self.phases:
                    self.ffn(li, 0)
                if "mix" in self.phases:
                    self.mixer(li)
                if "ffn1" in self.phases:
                    self.ffn(li, 1)
            if self.final_norm:
                self.final(sq)
            else:
                ctx.dma("sp", [(self.d_y[sq, :, kc, :], self.X[:, kc, :], {}) for kc in range(KC)],
                        self.sem_o, R=[T("X", kc, t) for kc in range(KC) for t in range(NT)])
            ctx.barrier(extra_sems=[self.sem_o, self.sem_x])

    def rmsnorm(self, gcol, dst, dst_tr):
        ctx = self.ctx
        T = ctx.T
        ones_b = self.CMB[:, CM_ONES:CM_ONES + 128]
        PN = self.PS[6]
        for t in range(NT):
            ts = slice(t * TT, (t + 1) * TT)
            for kc in range(KC):
                sl = (t * KC + kc) % 2
                ctx.op("act", "activation", dict(out=self.SQ[:, sl, :], in_=self.X[:, kc, ts], func=AF.Square),
                       R=(T("X", kc, t),), W=(T("SQ", sl),))
                self.mm(PN[:, :], [(ones_b, self.SQ[:, sl, :])], R=(T("SQ", sl), T("cmb")), W=(T("PS", 6),),
                        start=(kc == 0), stop=(kc == KC - 1))
            rs = t % 2
            ctx.op("act", "activation", dict(out=self.RSTD[:, rs, :], in_=PN[:, :], func=AF.Ln,
                                             bias=self.cvc(CV_EPS), scale=1.0 / D),
                   R=(T("PS", 6), T("cv")), W=(T("RSTD", rs),))
            ctx.op("act", "activation", dict(out=self.RSTD[:, rs, :], in_=self.RSTD[:, rs, :], func=AF.Exp, scale=-0.5),
                   R=(T("RSTD", rs),), W=(T("RSTD", rs),))
            for kc in range(KC):
                ctx.op("dve", "scalar_tensor_tensor", dict(
                    out=dst[:, kc, ts], in0=self.X[:, kc, ts], scalar=self.cvc(gcol + kc), in1=self.RSTD[:, rs, :],
                    op0=ALU.mult, op1=ALU.mult),
                    R=(T("X", kc, t), T("RSTD", rs), T("cv")), W=(dst_tr(kc, t),))

    def ffn(self, li, j):
        ctx = self.ctx
        T = ctx.T
        ctx.barrier()
        self.mark('ffn%d' % j)
        A = self.av(KC * S, FH * S // 2, BF16, "p (f n) -> p f n", f=FH)
        self.rmsnorm(CV_FFN + (li * 2 + j) * KC, self.H, lambda kc, t: T("H", t))
        k = 0
        for hf in range(2):
            for fl in range(FH):
                fc = hf * FH + fl
                wg, wgt = self.wload(self.d_wg[li, j, fc], KC * 128)
                wu, wut = self.wload(self.d_wu[li, j, fc], KC * 128)
                for t in range(NT):
                    ts = slice(t * TT, (t + 1) * TT)
                    b = k % 2
                    k += 1
                    PG, PU = self.PS[b], self.PS[2 + b]
                    self.mm(PG[:, :], [(wg[:, kc * 128:(kc + 1) * 128], self.H[:, kc, ts]) for kc in range(KC)],
                            R=(wgt, T("H", t)), W=(T("PS", b),))
                    self.mm(PU[:, :], [(wu[:, kc * 128:(kc + 1) * 128], self.H[:, kc, ts]) for kc in range(KC)],
                            R=(wut, T("H", t)), W=(T("PS", 2 + b),))
                    ctx.op("act", "activation", dict(out=self.TMPF[:, b, :], in_=PG[:, :], func=AF.Silu),
                           R=(T("PS", b),), W=(T("TMPF", b),))
                    ctx.op("dve", "tensor_tensor", dict(out=A[:, fl, ts], in0=self.TMPF[:, b, :], in1=PU[:, :],
                                                        op=ALU.mult),
                           R=(T("TMPF", b), T("PS", 2 + b)), W=(T("A", fl, t),))
            for dc in range(KC):
                wd, wdt = self.wload(self.d_wd[li, j, hf, dc], FH * 128)
                for t in range(NT):
                    ts = slice(t * TT, (t + 1) * TT)
                    b = k % 2
                    k += 1
                    PY = self.PS[4 + b]
                    self.mm(PY[:, :], [(wd[:, fl * 128:(fl + 1) * 128], A[:, fl, ts]) for fl in range(FH)],
                            R=[wdt] + [T("A", fl, t) for fl in range(FH)], W=(T("PS", 4 + b),))
                    ctx.op("dve", "scalar_tensor_tensor", dict(
                        out=self.X[:, dc, ts], in0=PY[:, :], scalar=0.5, in1=self.X[:, dc, ts],
                        op0=ALU.mult, op1=ALU.add),
                        R=(T("PS", 4 + b), T("X", dc, t)), W=(T("X", dc, t),))

    def mixer(self, li):
        ctx = self.ctx
        T = ctx.T
        ctx.barrier()
        self.mark('mixnorm')
        self.rmsnorm(CV_MIX + li * KC, self.H, lambda kc, t: T("H", t))
        xtr = [T("X", kc, t) for kc in range(KC) for t in range(NT)]
        ctx.dma("sp", [(self.d_xsp[:, kc, :], self.X[:, kc, :], {}) for kc in range(KC)], self.sem_x, R=xtr)
        ctx.barrier(extra_sems=[self.sem_x])
        self.MACC = self.av(self.ACOLS - 8192, 8192, BF16, "p (k n) -> p k n", k=KC)
        self.Y = self.av(self.ACOLS - 12288, 4096, BF16, "p (k n) -> p k n", k=4)
        order = self.mix_order
        first = True
        for n in order:
            self.mark('branch%d' % n)
            if n == 2:
                self.sb_branch(li)
            elif n == 0:
                self.pool_branch(li)
            else:
                self.dn_branch(li)
            self.mark('merge%d' % n)
            self.merge(li, n, first)
            first = False
        ctx.barrier()
        self.mark('wout')
        ctx.dma("sp", [(self.X[:, kc, :], self.d_xsp[:, kc, :], {}) for kc in range(KC)], self.sem_x, W=xtr)
        k = 0
        for dc in range(KC):
            wo, wot = self.wload(self.d_wo[li, dc], KC * 128)
            for t in range(NT):
                ts = slice(t * TT, (t + 1) * TT)
                PY = self.PS[4 + k % 2]
                ptr = T("PS", 4 + k % 2)
                k += 1
                self.mm(PY[:, :], [(wo[:, kc * 128:(kc + 1) * 128], self.MACC[:, kc, ts]) for kc in range(KC)],
                        R=[wot] + [T("MACC", kc, t) for kc in range(KC)], W=(ptr,))
                ctx.op("dve", "tensor_tensor", dict(out=self.X[:, dc, ts], in0=PY[:, :], in1=self.X[:, dc, ts],
                                                    op=ALU.add),
                       R=(ptr, T("X", dc, t)), W=(T("X", dc, t),))

    def merge(self, li, n, first):
        ctx = self.ctx
        T = ctx.T
        k = 0
        for dc in range(KC):
            wgt, wgtt = self.wload(self.d_win[li, WIN_G + n * 8 + dc], KC * 128)
            wbr, wbrt = self.wload(self.d_wb[li, n, dc], 4 * 128)
            for t in range(NT):
                ts = slice(t * TT, (t + 1) * TT)
                b = k % 2
                k += 1
                PG, PB_ = self.PS[b], self.PS[2 + b]
                self.mm(PG[:, :], [(wgt[:, kc * 128:(kc + 1) * 128], self.H[:, kc, ts]) for kc in range(KC)],
                        R=(wgtt, T("H", t)), W=(T("PS", b),))
                ctx.op("act", "activation", dict(out=self.TMPF[:, b, :], in_=PG[:, :], func=AF.Sigmoid,
                                                 bias=self.cvc(CV_BG + li * 24 + n * 8 + dc), scale=1.0),
                       R=(T("PS", b), T("cv")), W=(T("TMPF", b),))
                self.mm(PB_[:, :], [(wbr[:, k4 * 128:(k4 + 1) * 128], self.Y[:, k4, ts]) for k4 in range(4)],
                        R=[wbrt] + [T("Y", k4, t) for k4 in range(4)], W=(T("PS", 2 + b),))
                if first:
                    ctx.op("dve", "tensor_tensor", dict(out=self.MACC[:, dc, ts], in0=self.TMPF[:, b, :],
                                                        in1=PB_[:, :], op=ALU.mult),
                           R=(T("TMPF", b), T("PS", 2 + b)), W=(T("MACC", dc, t),))
                else:
                    ctx.op("dve", "tensor_tensor", dict(out=self.RSTD[:, b, :], in0=self.TMPF[:, b, :],
                                                        in1=PB_[:, :], op=ALU.mult),
                           R=(T("TMPF", b), T("PS", 2 + b)), W=(T("RSTD", b),))
                    ctx.op("dve", "tensor_tensor", dict(out=self.MACC[:, dc, ts], in0=self.RSTD[:, b, :],
                                                        in1=self.MACC[:, dc, ts], op=ALU.add),
                           R=(T("RSTD", b), T("MACC", dc, t)), W=(T("MACC", dc, t),))

    def sb_branch(self, li):
        ctx = self.ctx
        T = ctx.T
        QT = self.av(0, 1024, BF16)
        KT = self.av(1024, 1024, BF16)
        VT = self.av(2048, 1024, BF16, "p (c d) -> p c d", c=16)
        SP = self.av(3072, 4096, BF16, "p (c t) -> p c t", c=16)
        ET = self.av(7168, 1024, F32, "p (b t) -> p b t", b=2)
        WT = self.av(8192, 512, BF16, "p (b t) -> p b t", b=2)
        SS = self.av(8704, 4096, BF16, "p (c t) -> p c t", c=16)
        ident_b = self.CMB[:, CM_ID:CM_ID + 128]
        negui = self.CMB[:, CM_NEGUI:CM_NEGUI + 128]
        negones = self.CMB[:, CM_NEGONES:CM_NEGONES + 128]
        cmb = T("cmb")
        ka = kb = ko = kp = 0
        for hd in range(NH):
            wq, wqt = self.wload(self.d_win[li, WIN_SQ + hd], KC * 128)
            wk, wkt = self.wload(self.d_win[li, WIN_SK + hd], KC * 128)
            wv, wvt = self.wload(self.d_win[li, WIN_SV + hd], KC * 128)
            for t in range(NT):
                ts = slice(t * TT, (t + 1) * TT)
                b = kp % 2
                kp += 1
                self.mm(self.PS[b][:, :], [(wq[:, kc * 128:(kc + 1) * 128], self.H[:, kc, ts]) for kc in range(KC)],
                        R=(wqt, T("H", t)), W=(T("PS", b),))
                ctx.op("dve", "tensor_scalar_mul", dict(out=QT[:, ts], in0=self.PS[b][:, :], scalar1=128.0 ** -0.5),
                       R=(T("PS", b),), W=(T("QT", t),))
                b = kp % 2
                kp += 1
                self.mm(self.PS[b][:, :], [(wk[:, kc * 128:(kc + 1) * 128], self.H[:, kc, ts]) for kc in range(KC)],
                        R=(wkt, T("H", t)), W=(T("PS", b),))
                ctx.op("dve", "tensor_copy", dict(out=KT[:, ts], in_=self.PS[b][:, :]),
                       R=(T("PS", b),), W=(T("KT", t),))
            for g4 in range(4):
                b = kp % 2
                kp += 1
                for cc in range(4):
                    c = g4 * 4 + cc
                    self.mm(self.PS[b][:, cc * 128:(cc + 1) * 128],
                            [(self.H[:, kc, c * 128:(c + 1) * 128], wv[:, kc * 128:(kc + 1) * 128]) for kc in range(KC)],
                            R=(wvt, T("H", g4)), W=(T("PS", b),))
                ctx.op("dve", "tensor_copy", dict(out=VT[:, g4 * 4:(g4 + 1) * 4, :],
                                                  in_=self.PS[b][:, :].rearrange("p (c d) -> p c d", c=4)),
                       R=(T("PS", b),), W=(T("VT", g4),))
            for I in range(4):
                n = 4 * (I + 1)
                qs = QT[:, I * TT:(I + 1) * TT]
                qtr = T("QT", I)
                for c in range(n - 1, -1, -1):
                    b = ka % 2
                    ka += 1
                    pairs = [(KT[:, c * 128:(c + 1) * 128], qs)]
                    if c >= 4 * I:
                        m = c - 4 * I
                        pairs.append((ident_b, self.CMB[:, CM_MB + m * 512:CM_MB + (m + 1) * 512]))
                    self.mm(self.PS[b][:, :], pairs, R=(T("KT", c // 4), qtr, cmb), W=(T("PS", b),))
                    ctx.op("act", "activation", dict(out=ET[:, b, :], in_=self.PS[b][:, :], func=AF.Exp),
                           R=(T("PS", b),), W=(T("ET", b),))
                    ctx.op("act", "activation", dict(out=SP[:, c, :], in_=ET[:, b, :], func=AF.Ln,
                                                     bias=self.cvc(CV_ONE), scale=1.0),
                           R=(T("ET", b), T("cv")), W=(T("SP", c),))
                    if c == n - 1:
                        pass
                    elif c >= 1:
                        prev = SP[:, c + 1, :] if c + 1 == n - 1 else SS[:, c + 1, :]
                        prevt = T("SP", c + 1) if c + 1 == n - 1 else T("SS", c + 1)
                        ctx.op("dve", "tensor_tensor", dict(out=SS[:, c, :], in0=prev, in1=SP[:, c, :], op=ALU.add),
                               R=(prevt, T("SP", c)), W=(T("SS", c),))
                po = 4 + ko % 2
                ko += 1

                def grp(c):
                    nonlocal kb
                    b = 2 + kb % 2
                    kb += 1
                    pairs = [(KT[:, c * 128:(c + 1) * 128], qs)]
                    if c >= 4 * I:
                        m = c - 4 * I
                        pairs.append((ident_b, self.CMB[:, CM_MB + m * 512:CM_MB + (m + 1) * 512]))
                    pairs.append((negui, SP[:, c, :]))
                    rd = [T("KT", c // 4), qtr, cmb, T("SP", c)]
                    if c + 1 <= n - 1:
                        if c + 1 == n - 1:
                            pairs.append((negones, SP[:, c + 1, :]))
                            rd.append(T("SP", c + 1))
                        else:
                            pairs.append((negones, SS[:, c + 1, :]))
                            rd.append(T("SS", c + 1))
                    self.mm(self.PS[b][:, :], pairs, R=rd, W=(T("PS", b),))
                    return b
                bnext = grp(0)
                for c in range(n):
                    b = bnext
                    wb_ = c % 2
                    ctx.op("act", "activation", dict(out=WT[:, wb_, :], in_=self.PS[b][:, :], func=AF.Exp),
                           R=(T("PS", b),), W=(T("WT", wb_),))
                    if c + 1 < n:
                        bnext = grp(c + 1)
                    self.mm(self.PS[po][:, :], [(VT[:, c, :], WT[:, wb_, :])],
                            R=(T("VT", c // 4), T("WT", wb_)), W=(T("PS", po),), start=(c == 0), stop=(c == n - 1))
                ctx.op("dve", "tensor_copy", dict(out=self.Y[:, hd, I * TT:(I + 1) * TT], in_=self.PS[po][:, :]),
                       R=(T("PS", po),), W=(T("Y", hd, I),))

    def pool_branch(self, li):
        ctx = self.ctx
        T = ctx.T
        PADW = 16
        UP = self.av(0, 2064, F32)
        PA = self.av(2064, 2064, F32)
        PBf = self.av(4128, 2064, F32)
        POOLED = self.av(6192, 1024, BF16)
        FIX = self.av(7216, 16, F32)
        for nm, buf in (("UP", UP), ("PA", PA), ("PBf", PBf)):
            ctx.op("dve", "memset", dict(ap=buf[:, 0:PADW], constant=0.0), W=(T(nm),))
        kp = 0
        for g, win in enumerate(POOL_WINDOWS):
            wu, wut = self.wload(self.d_win[li, WIN_POOL + g], KC * 128)
            wp, wpt = self.wload(self.d_wpool[li], 4 * 128)
            for t in range(NT):
                ts = slice(t * TT, (t + 1) * TT)
                b = kp % 2
                kp += 1
                self.mm(self.PS[b][:, :], [(wu[:, kc * 128:(kc + 1) * 128], self.H[:, kc, ts]) for kc in range(KC)],
                        R=(wut, T("H", t)), W=(T("PS", b),))
                ctx.op("dve", "tensor_copy", dict(out=UP[:, PADW + t * TT:PADW + (t + 1) * TT], in_=self.PS[b][:, :]),
                       R=(T("PS", b),), W=(T("UP"),))
            cur, curn = UP, "UP"
            bufs = [(PA, "PA"), (PBf, "PBf")]
            for lev in range(g + 1):
                sh = 2 ** lev
                nxt, nxtn = bufs[lev % 2]
                ctx.op("dve", "tensor_tensor", dict(out=nxt[:, PADW:PADW + S], in0=cur[:, PADW:PADW + S],
                                                    in1=cur[:, PADW - sh:PADW - sh + S], op=ALU.add),
                       R=(T(curn),), W=(T(nxtn),))
                cur, curn = nxt, nxtn
            ctx.op("dve", "scalar_tensor_tensor", dict(out=POOLED[:, :], in0=cur[:, PADW:PADW + S], scalar=1.0 / win,
                                                       in1=UP[:, PADW:PADW + S], op0=ALU.mult, op1=ALU.subtract),
                   R=(T(curn), T("UP")), W=(T("POOLED"),))
            ctx.op("dve", "tensor_tensor", dict(out=FIX[:, :], in0=cur[:, PADW:PADW + 16],
                                                in1=self.cvc(CV_INVC + g * 16, 16), op=ALU.mult),
                   R=(T(curn), T("cv")), W=(T("FIX"),))
            ctx.op("dve", "tensor_tensor", dict(out=POOLED[:, 0:16], in0=FIX[:, :], in1=UP[:, PADW:PADW + 16],
                                                op=ALU.subtract),
                   R=(T("FIX"), T("UP")), W=(T("POOLED"),))
            for t in range(NT):
                ts = slice(t * TT, (t + 1) * TT)
                b = kp % 2
                kp += 1
                self.mm(self.PS[b][:, :], [(wp[:, g * 128:(g + 1) * 128], POOLED[:, ts])],
                        R=(wpt, T("POOLED")), W=(T("PS", b),))
                ctx.op("dve", "tensor_scalar_mul", dict(out=self.Y[:, g, ts], in0=self.PS[b][:, :],
                                                        scalar1=self.cvc(CV_PS + li * 4 + g)),
                       R=(T("PS", b), T("cv")), W=(T("Y", g, t),))

    def dn_branch(self, li):
        ctx = self.ctx
        T = ctx.T
        o = 0

        def take(n):
            nonlocal o
            r = o
            o += n
            return r
        QIN = self.av(take(2056), 2056, F32)
        OT = QIN[:, 8:8 + S]
        QN = self.av(take(2048), 2048, F32)
        KN = self.av(take(2048), 2048, F32)
        VN = self.av(take(2048), 2048, F32)
        KDEC = KN.rearrange("p (c d) -> p c d", c=16)
        ATT = VN.rearrange("p (c d) -> p c d", c=16)
        NCH = 8
        ZS = self.av(take(1024), 1024, BF16)
        KBG = self.av(take(2048), 2048, F32, "p (c d) -> p c d", c=16)
        WTf = KBG.rearrange("p c d -> p (c d)")
        BV = self.av(take(2048), 2048, F32, "p (c d) -> p c d", c=16)
        SQ2 = self.av(take(1024), 1024, BF16)
        IB = [[self.av(take(128), 128, F32) for _ in range(5)] for _ in range(NCH)]
        ABT = self.av(take(128), 128, F32, "p (c e) -> p c e", c=16)
        G = self.av(take(64), 64, F32, "p (h c) -> p h c", h=4)
        BETA = self.av(take(64), 64, F32, "p (h c) -> p h c", h=4)
        EXPS = self.av(take(256), 256, F32, "p (k h c) -> p k h c", k=4, h=4)
        EGC, EREM, CDA, CDB = (EXPS[:, i] for i in range(4))
        BEG = self.av(take(64), 64, F32, "p (h c) -> p h c", h=4)
        EA = self.av(take(4), 4, F32)
        E1 = self.av(take(32), 32, F32)
        ST = self.av(take(128), 128, F32)
        VNEW = self.av(take(128), 128, F32)
        RS2 = self.av(take(512), 512, F32)
        assert o <= self.ACOLS - 12288, o
        Rr = lambda ap: ap
        IDF = self.CMF[:, CM_ID:CM_ID + 128]
        UPI = self.CMF[:, CM_UPI:CM_UPI + 128]
        LOS = self.CMF[:, CM_LOS:CM_LOS + 128]
        IDR = self.CMR[:, CM_ID:CM_ID + 128]
        ONESR = self.CMR[:, CM_ONES:CM_ONES + 128]
        UPIR = self.CMR[:, CM_UPI:CM_UPI + 128]
        INDS = [self.CMR[:, c:c + 128] for c in (CM_UPI, CM_LOS, CM_INDA, CM_INDB)]
        ones_b = self.CMB[:, CM_ONES:CM_ONES + 128]
        cmf, cv = T("cmf"), T("cv")
        P7 = self.PS[7]

        wab, wabt = self.wload(self.d_wab[li], KC * 8)
        for c in range(16):
            self.mm(P7[:, c * 8:(c + 1) * 8],
                    [(self.H[:, kc, c * 128:(c + 1) * 128], wab[:, kc * 8:(kc + 1) * 8]) for kc in range(KC)],
                    R=(wabt, T("H", c // 4)), W=(T("PS", 7),))
        ctx.op("dve", "tensor_copy", dict(out=ABT[:, :, :], in_=P7[:, 0:128].rearrange("p (c e) -> p c e", c=16)),
               R=(T("PS", 7),), W=(T("ABT"),))
        ctx.op("act", "activation", dict(out=EA[:, :], in_=self.cvc(CV_AL + li * 4, 4), func=AF.Exp),
               R=(cv,), W=(T("EA"),))
        for hd in range(NH):
            ctx.op("act", "activation", dict(out=E1[:, 0:16], in_=ABT[:, :, hd], func=AF.Exp,
                                             bias=self.cvc(CV_DT + li * 4 + hd), scale=1.0),
                   R=(T("ABT"), cv), W=(T("E1"),))
            ctx.op("act", "activation", dict(out=E1[:, 0:16], in_=E1[:, 0:16], func=AF.Ln,
                                             bias=self.cvc(CV_ONE), scale=1.0),
                   R=(T("E1"), cv), W=(T("E1"),))
            ctx.op("dve", "tensor_scalar", dict(out=Rr(G[:, hd, :]), in0=E1[:, 0:16], scalar1=EA[:, hd:hd + 1], scalar2=-1.0,
                                                op0=ALU.mult, op1=ALU.mult),
                   R=(T("E1"), T("EA")), W=(T("G"),))
            ctx.op("act", "activation", dict(out=E1[:, 16:32], in_=ABT[:, :, 4 + hd], func=AF.Exp, scale=-1.0),
                   R=(T("ABT"),), W=(T("E1b"),))
            ctx.op("dve", "tensor_scalar_add", dict(out=E1[:, 16:32], in0=E1[:, 16:32], scalar1=1.0),
                   R=(T("E1b"),), W=(T("E1b"),))
            ctx.op("dve", "reciprocal", dict(out=BETA[:, hd, :], in_=E1[:, 16:32]),
                   R=(T("E1b"),), W=(T("BETA"),))
        Gf = G.rearrange("p h c -> p (h c)")
        for i, M in enumerate(INDS):
            self.mm(P7[:, 128 + i * 64:128 + (i + 1) * 64], [(M, Rr(Gf))], R=(T("G"), cmf), W=(T("PS", 7),))
        ctx.op("act", "activation", dict(out=EXPS.rearrange("p k h c -> p (k h c)"), in_=P7[:, 128:384], func=AF.Exp),
               R=(T("PS", 7),), W=(T("EXPS"),))
        ctx.op("dve", "tensor_tensor", dict(out=BEG[:, :, :], in0=BETA[:, :, :], in1=EGC, op=ALU.mult),
               R=(T("BETA"), T("EXPS")), W=(T("BEG"),))

        kp = 0
        if self.dn_stop == 0:
            return
        for hd in range(NH):
            self.mark('dnA%d' % hd)
            chunks = (("QN", QN, WIN_DQ + hd, hd), ("KN", KN, WIN_DK + hd, 4 + hd), ("VN", VN, WIN_DV + hd, 8 + hd))
            for nm, dst, widx, cidx in chunks:
                w, wt = self.wload(self.d_win[li, widx], KC * 128)
                ctx.op("dve", "memset", dict(ap=QIN[:, 0:8], constant=0.0), W=(T("QIN"),))
                for t in range(NT):
                    ts = slice(t * TT, (t + 1) * TT)
                    b = kp % 2
                    kp += 1
                    self.mm(self.PS[b][:, :], [(w[:, kc * 128:(kc + 1) * 128], self.H[:, kc, ts]) for kc in range(KC)],
                            R=(wt, T("H", t)), W=(T("PS", b),))
                    ctx.op("act", "activation", dict(out=QIN[:, 8 + t * TT:8 + (t + 1) * TT], in_=self.PS[b][:, :], func=AF.Copy),
                           R=(T("PS", b),), W=(T("QIN"),))
                dtr = [T(nm, p) for p in range(16)]
                cbase = CV_CONV + li * 48 + cidx
                ctx.op("act", "activation", dict(out=Rr(dst[:, :]), in_=QIN[:, 5:5 + S], func=AF.Copy, scale=self.cvc(cbase)),
                       R=(T("QIN"), cv), W=dtr)
                for k in range(1, 4):
                    ctx.op("dve", "scalar_tensor_tensor", dict(out=Rr(dst[:, :]), in0=QIN[:, 5 + k:5 + k + S],
                                                               scalar=self.cvc(cbase + 12 * k), in1=dst[:, :],
                                                               op0=ALU.mult, op1=ALU.add),
                           R=[T("QIN"), cv] + dtr, W=dtr)
            w, wt = self.wload(self.d_win[li, WIN_DZ + hd], KC * 128)
            for t in range(NT):
                ts = slice(t * TT, (t + 1) * TT)
                b = kp % 2
                kp += 1
                self.mm(self.PS[b][:, :], [(w[:, kc * 128:(kc + 1) * 128], self.H[:, kc, ts]) for kc in range(KC)],
                        R=(wt, T("H", t)), W=(T("PS", b),))
                ctx.op("act", "activation", dict(out=ZS[:, ts], in_=self.PS[b][:, :], func=AF.Silu),
                       R=(T("PS", b),), W=(T("ZS", t),))
            for nm, dst, widx, cidx in chunks:
                dtr = [T(nm, p) for p in range(16)]
                ctx.op("act", "activation", dict(out=Rr(dst[:, :]), in_=dst[:, :], func=AF.Silu), R=dtr, W=dtr)
            for nm, dst, widx, cidx in chunks[:2]:
                dtr = [T(nm, p) for p in range(16)]
                ctx.op("act", "activation", dict(out=SQ2[:, :], in_=dst[:, :], func=AF.Square), R=dtr, W=(T("SQ2"),))
                for t in range(NT):
                    ts = slice(t * TT, (t + 1) * TT)
                    b = kp % 2
                    kp += 1
                    ptr = [T(nm, p) for p in range(4 * t, 4 * t + 4)]
                    self.mm(self.PS[b][:, :], [(ones_b, SQ2[:, ts])], R=(T("SQ2"), T("cmb")), W=(T("PS", b),))
                    ctx.op("act", "activation", dict(out=RS2[:, :], in_=self.PS[b][:, :], func=AF.Ln,
                                                     bias=self.cvc(CV_EPS), scale=1.0),
                           R=(T("PS", b), cv), W=(T("RS2"),))
                    ctx.op("act", "activation", dict(out=RS2[:, :], in_=RS2[:, :], func=AF.Exp, scale=-0.5),
                           R=(T("RS2"),), W=(T("RS2"),))
                    if nm == "QN":
                        ctx.op("dve", "scalar_tensor_tensor", dict(out=Rr(dst[:, ts]), in0=dst[:, ts], scalar=128.0 ** -0.5,
                                                                   in1=RS2[:, :], op0=ALU.mult, op1=ALU.mult),
                               R=[T("RS2")] + ptr, W=ptr)
                    else:
                        ctx.op("dve", "tensor_tensor", dict(out=Rr(dst[:, ts]), in0=dst[:, ts], in1=RS2[:, :], op=ALU.mult),
                               R=[T("RS2")] + ptr, W=ptr)

            if self.dn_stop == 1:
                continue
            self.mark('dnB%d' % hd)
            def chain(ci, p):
                cs = slice(p * 128, (p + 1) * 128)
                B0, B1, B2, B3, B4 = IB[ci]
                bt = [T("IB", ci, i) for i in range(5)]
                Pc = self.PS[ci]
                pt = T("PS", ci)

                def q(i):
                    return Pc[:, i * 128:(i + 1) * 128]
                kn, qn, vn = T("KN", p), T("QN", p), T("VN", p)
                self.mm(q(0), [(Rr(KN[:, cs]), IDR)], R=(kn, cmf), W=(pt,))
                self.mm(q(1), [(Rr(VN[:, cs]), IDR)], R=(vn, cmf), W=(pt,))
                self.mm(q(2), [(Rr(KN[:, cs]), Rr(KN[:, cs]))], R=(kn,), W=(pt,))
                self.mm(q(3), [(Rr(KN[:, cs]), Rr(QN[:, cs]))], R=(kn, qn), W=(pt,))
                ctx.op("dve", "tensor_scalar_mul", dict(out=Rr(B0), in0=LOS, scalar1=G[:, hd, p:p + 1]),
                       R=(T("G"), cmf), W=(bt[0],))
                yield
                ctx.op("dve", "tensor_scalar_mul", dict(out=Rr(KBG[:, p, :]), in0=q(0), scalar1=BEG[:, hd, p:p + 1]),
                       R=(pt, T("BEG")), W=(T("KBG", p),))
                ctx.op("dve", "tensor_scalar_mul", dict(out=Rr(KDEC[:, p, :]), in0=q(0), scalar1=EREM[:, hd, p:p + 1]),
                       R=(pt, T("EXPS")), W=(kn,))
                ctx.op("dve", "tensor_scalar_mul", dict(out=Rr(BV[:, p, :]), in0=q(1), scalar1=BETA[:, hd, p:p + 1]),
                       R=(pt, T("BETA")), W=(T("BV", p),))
                self.mm(q(0), [(UPIR, Rr(B0))], R=(bt[0], cmf), W=(pt,))
                self.mm(q(1), [(Rr(B0), UPIR)], R=(bt[0], cmf), W=(pt,))
                yield
                ctx.op("act", "activation", dict(out=B1, in_=q(0), func=AF.Exp), R=(pt,), W=(bt[1],))
                ctx.op("act", "activation", dict(out=B2, in_=q(1), func=AF.Exp), R=(pt,), W=(bt[2],))
                yield
                ctx.op("dve", "scalar_tensor_tensor", dict(out=Rr(B3), in0=B1, scalar=BETA[:, hd, p:p + 1], in1=LOS,
                                                           op0=ALU.mult, op1=ALU.mult),
                       R=(bt[1], T("BETA"), cmf), W=(bt[3],))
                ctx.op("dve", "tensor_tensor", dict(out=Rr(B3), in0=B3, in1=q(2), op=ALU.mult),
                       R=(bt[3], pt), W=(bt[3],))
                ctx.op("dve", "tensor_tensor", dict(out=B2, in0=B2, in1=UPI, op=ALU.mult),
                       R=(bt[2], cmf), W=(bt[2],))
                ctx.op("dve", "tensor_tensor", dict(out=Rr(ATT[:, p, :]), in0=B2, in1=q(3), op=ALU.mult),
                       R=(bt[2], pt), W=(vn,))
                yield
                self.mm(q(0), [(Rr(B3), IDR)], R=(bt[3], cmf), W=(pt,))
                yield
                ctx.op("act", "activation", dict(out=Rr(B4), in_=q(0), func=AF.Copy), R=(pt,), W=(bt[4],))
                ctx.op("dve", "tensor_tensor", dict(out=Rr(B2), in0=IDF, in1=q(0), op=ALU.subtract),
                       R=(pt, cmf, vn), W=(bt[2],))
                yield
                Lc, Uc, Ln, Un = 3, 4, 1, 0
                NLEV = 6 if DNC == 128 else 5
                for lev in range(NLEV):
                    last = lev == NLEV - 1
                    if not last:
                        self.mm(q(0), [(Rr(IB[ci][Lc]), Rr(IB[ci][Uc]))], R=(bt[Lc], bt[Uc]), W=(pt,))
                    self.mm(q(1), [(Rr(IB[ci][Uc]), Rr(IB[ci][Lc]))], R=(bt[Lc], bt[Uc]), W=(pt,))
                    yield
                    if not last:
                        ctx.op("act", "activation", dict(out=Rr(IB[ci][Un]), in_=q(0), func=AF.Copy),
                               R=(pt,), W=(bt[Un],))
                    ctx.op("dve", "tensor_copy", dict(out=Rr(IB[ci][Ln]), in_=q(1)), R=(pt,), W=(bt[Ln],))
                    yield
                    self.mm(q(2), [(Rr(IB[ci][Ln]), Rr(B2))], R=(bt[Ln], bt[2]), W=(pt,))
                    yield
                    ctx.op("dve", "tensor_tensor", dict(out=Rr(B2), in0=B2, in1=q(2), op=ALU.add),
                           R=(bt[2], pt), W=(bt[2],))
                    yield
                    Lc, Ln = Ln, Lc
                    Uc, Un = Un, Uc
                self.mm(q(0), [(Rr(B2), Rr(BV[:, p, :]))], R=(bt[2], T("BV", p)), W=(pt,))
                self.mm(q(1), [(Rr(KBG[:, p, :]), Rr(B2))], R=(bt[2], T("KBG", p)), W=(pt,))
                ctx.op("dve", "tensor_scalar_mul", dict(out=Rr(B0), in0=IDF, scalar1=EGC[:, hd, p:p + 1]),
                       R=(T("EXPS"), cmf, bt[0]), W=(bt[0],))
                self.mm(q(2), [(ONESR, Rr(B0))], R=(bt[0], cmf), W=(pt,))
                yield
                ctx.op("dve", "tensor_copy", dict(out=Rr(BV[:, p, :]), in_=q(0)), R=(pt,), W=(T("BV", p),))
                ctx.op("act", "activation", dict(out=Rr(WTf[:, cs]), in_=q(1), func=AF.Copy),
                       R=(pt,), W=(T("KBG", p),))
                ctx.op("dve", "tensor_tensor", dict(out=Rr(QN[:, cs]), in0=QN[:, cs], in1=q(2), op=ALU.mult),
                       R=(qn, pt), W=(qn,))
                yield

            for g4 in range(16 // NCH):
                gens = [chain(ci, g4 * NCH + ci) for ci in range(NCH)]
                alive = True
                rounds = 0
                while alive and rounds < self.dn_bsteps:
                    rounds += 1
                    alive = False
                    for gen in gens:
                        try:
                            next(gen)
                            alive = True
                        except StopIteration:
                            pass

            if self.dn_stop == 2:
                continue
            self.mark('dnC%d' % hd)
            ctx.op("dve", "memset", dict(ap=Rr(ST[:, :]), constant=0.0), W=(T("ST"),))
            NSTEP = S // DNC
            PERB = 512 // DNC
            for n in range(NSTEP):
                if DNC == 128:
                    p, hf = n, 0
                    r = slice(0, 128)
                else:
                    p, hf = n // 2, n % 2
                    r = slice(hf * 64, hf * 64 + 64)
                a = n % 2
                PVb, PSb = self.PS[a], self.PS[2 + a]
                po = 4 + (n // PERB) % 2
                self.mm(PVb[:, 0:128], [(Rr(WTf[:, p * 128:(p + 1) * 128]), Rr(ST[:, :]))], R=(T("KBG", p), T("ST")),
                        W=(T("PS", a),))
                oc = slice((n % PERB) * DNC, (n % PERB) * DNC + DNC)
                self.mm(self.PS[po][:, oc], [(Rr(ST[:, :]), Rr(QN[:, n * DNC:(n + 1) * DNC]))],
                        R=(T("ST"), T("QN", p)), W=(T("PS", po),), start=True, stop=False)
                ctx.op("dve", "tensor_tensor", dict(out=Rr(VNEW[r, :]), in0=BV[r, p, :], in1=PVb[r, 0:128], op=ALU.subtract),
                       R=(T("BV", p), T("PS", a)), W=(T("VNEW"),))
                self.mm(PSb[:, 0:128], [(Rr(KDEC[r, p, :]), Rr(VNEW[r, :]))], R=(T("KN", p), T("VNEW")), W=(T("PS", 2 + a),))
                self.mm(self.PS[po][:, oc], [(Rr(VNEW[r, :]), Rr(ATT[r, p, r]))],
                        R=(T("VNEW"), T("VN", p)), W=(T("PS", po),), start=False, stop=True)
                cd = (CDA if hf == 0 else CDB)[:, hd, p:p + 1]
                ctx.op("dve", "scalar_tensor_tensor", dict(out=Rr(ST[:, :]), in0=ST[:, :], scalar=cd, in1=PSb[:, 0:128],
                                                           op0=ALU.mult, op1=ALU.add),
                       R=(T("ST"), T("EXPS"), T("PS", 2 + a)), W=(T("ST"),))
                if n % PERB == PERB - 1:
                    t = n // PERB
                    ctx.op("dve", "tensor_copy", dict(out=OT[:, t * TT:(t + 1) * TT], in_=self.PS[po][:, :]),
                           R=(T("PS", po),), W=(T("QIN"),))
            self.mark('dnN%d' % hd)
            ctx.op("act", "activation", dict(out=SQ2[:, :], in_=OT, func=AF.Square), R=(T("QIN"),), W=(T("SQ2"),))
            for t in range(NT):
                ts = slice(t * TT, (t + 1) * TT)
                b = kp % 2
                kp += 1
                self.mm(self.PS[b][:, :], [(ones_b, SQ2[:, ts])], R=(T("SQ2"), T("cmb")), W=(T("PS", b),))
                ctx.op("act", "activation", dict(out=RS2[:, :], in_=self.PS[b][:, :], func=AF.Ln,
                                                 bias=self.cvc(CV_EPS), scale=1.0 / 128),
                       R=(T("PS", b), cv), W=(T("RS2"),))
                ctx.op("act", "activation", dict(out=RS2[:, :], in_=RS2[:, :], func=AF.Exp, scale=-0.5),
                       R=(T("RS2"),), W=(T("RS2"),))
                ctx.op("dve", "scalar_tensor_tensor", dict(out=self.TMPF[:, b, :], in0=OT[:, ts], scalar=self.cvc(CV_ON + li),
                                                           in1=RS2[:, :], op0=ALU.mult, op1=ALU.mult),
                       R=(T("QIN"), T("RS2"), cv), W=(T("TMPF", b),))
                ctx.op("dve", "tensor_tensor", dict(out=self.Y[:, hd, ts], in0=self.TMPF[:, b, :], in1=ZS[:, ts], op=ALU.mult),
                       R=(T("TMPF", b), T("ZS", t)), W=(T("Y", hd, t),))


    def final(self, sq):
        ctx = self.ctx
        T = ctx.T
        ctx.barrier()
        self.mark('final')
        OUT = self.av(KC * S, KC * S, F32, "p (k n) -> p k n", k=KC)
        self.rmsnorm(CV_FIN, OUT, lambda kc, t: T("OUT", kc, t))
        ctx.dma("sp", [(self.d_y[sq, :, kc, :], OUT[:, kc, :], {}) for kc in range(KC)],
                self.sem_o, R=[T("OUT", kc, t) for kc in range(KC) for t in range(NT)])


def _prep_core_inputs(inp, seqs, layers):
    x = inp["x"]
    xT = np.stack([np.ascontiguousarray(x[b].T.reshape(KC, 128, S).transpose(1, 0, 2)) for b in seqs])
    m = {"xT": xT, "cvec": _host_cvec(inp), "cmat": _host_consts()}
    return m


def _unlayout(yT):
    return np.ascontiguousarray(yT.transpose(1, 0, 2).reshape(D, S).T)


_CACHE = {}


def kernel(**inputs):
    inp = {k: np.asarray(v, dtype=np.float32) for k, v in inputs.items()}
    B = inp["x"].shape[0]
    prog = Prog(SEQ_PER_CORE, range(L))
    nc = prog.build()
    wts = _host_weights(inp, range(L))
    cv, cm = _host_cvec(inp), _host_consts()
    in_maps = []
    for c in range(NCORES):
        seqs = [c * SEQ_PER_CORE + i for i in range(SEQ_PER_CORE)]
        m = _prep_core_inputs(inp, seqs, range(L))
        m.update(wts)
        in_maps.append(m)
    res = run_bass_kernel_spmd(nc, in_maps, core_ids=list(range(NCORES)))
    out = np.empty((B, S, D), np.float32)
    for c in range(NCORES):
        yT = res.results[c]["yT"]
        for i in range(SEQ_PER_CORE):
            out[c * SEQ_PER_CORE + i] = _unlayout(yT[i])
    return out
```
